# Optimizing a Trainium2 kernel written in Bass

```python
import math
import jax, jax.numpy as jnp
from jax import lax
import numpy as np

D_MODEL = 1024
BATCH = 8
SEQ = 2048
DEPTH = 4
DEC_BATCH = 128
DEC_SEQ = 1
PAST_LEN = 16384
PAGE_SIZE = 128

N_EVEN = (DEPTH + 1) // 2
N_ODD = DEPTH // 2
EPS = 1e-6

GLA_HEADS = 4
GLA_DK = D_MODEL // 8
GLA_DV = D_MODEL // 4
GLA_K_WIDTH = GLA_HEADS * GLA_DK
GLA_V_WIDTH = GLA_HEADS * GLA_DV
GLA_RANK = 16
GLA_GATE_NORM = 16.0
GLA_CHUNK = 64

S5_GROUP_CH = 16
S5_P = 64
S5_WIDTH = D_MODEL // 2
S5_GROUPS = S5_WIDTH // S5_GROUP_CH
S5_DT_MIN = 0.001
S5_DT_MAX = 0.1

DN_HEADS = 8
DN_DK = D_MODEL // 8
DN_DV = D_MODEL // 8
DN_K_WIDTH = DN_HEADS * DN_DK
DN_V_WIDTH = DN_HEADS * DN_DV
DN_CONV_DIM = 2 * DN_K_WIDTH + DN_V_WIDTH
CONV_W = 4
DN_CHUNK = 64

EVEN_SIZES = (GLA_K_WIDTH, GLA_K_WIDTH, GLA_V_WIDTH, GLA_V_WIDTH, GLA_RANK, S5_WIDTH, S5_WIDTH)
ODD_SIZES = (DN_CONV_DIM, DN_V_WIDTH, DN_HEADS, DN_HEADS)
EVEN_IN = 2 * GLA_K_WIDTH + 2 * GLA_V_WIDTH + GLA_RANK + 2 * S5_WIDTH
ODD_IN = DN_CONV_DIM + DN_V_WIDTH + 2 * DN_HEADS

kernel_name = 'hybrid_gla_s5_gdn_step'


def _offsets(sizes):
    out, acc = [], 0
    for s in sizes[:-1]:
        acc += s
        out.append(acc)
    return out


def rmsnorm(x, w):
    xf = x.astype(jnp.float32)
    y = xf * lax.rsqrt(jnp.mean(xf * xf, axis=-1, keepdims=True) + EPS)
    return (y * w.astype(jnp.float32)).astype(x.dtype)


def l2norm(x):
    return x * lax.rsqrt(jnp.sum(x * x, axis=-1, keepdims=True) + EPS)


def _to_chunks(x, c, n):
    pad = n * c - x.shape[1]
    x = jnp.pad(x, [(0, 0), (0, pad)] + [(0, 0)] * (x.ndim - 2))
    x = x.reshape(x.shape[0], n, c, *x.shape[2:])
    return jnp.swapaxes(jnp.moveaxis(x, 1, 0), 2, 3)


def _from_chunks(o, L):
    n, b, h, c, v = o.shape
    return o.transpose(1, 0, 3, 2, 4).reshape(b, n * c, h, v)[:, :L]


def gla_chunked(q, k, v, log_a, s0):
    L = q.shape[1]
    c = min(GLA_CHUNK, L)
    n = -(-L // c)
    xs = tuple(_to_chunks(t, c, n) for t in (q, k, v, log_a))
    tril = jnp.tril(jnp.ones((c, c), dtype=bool))

    def step(S, inp):
        qi, ki, vi, gi = inp
        b = jnp.cumsum(gi, axis=2)
        diff = b[:, :, :, None, :] - b[:, :, None, :, :]
        decay = jnp.exp(jnp.where(tril[:, :, None], diff, -jnp.inf))
        scores = jnp.einsum('bhik,bhjk,bhijk->bhij', qi, ki, decay)
        o = (jnp.einsum('bhik,bhkv->bhiv', qi * jnp.exp(b), S)
             + jnp.einsum('bhij,bhjv->bhiv', scores, vi))
        b_last = b[:, :, -1:, :]
        S = (S * jnp.exp(b_last)[:, :, 0, :, None]
             + jnp.einsum('bhjk,bhjv->bhkv', ki * jnp.exp(b_last - b), vi))
        return S, o

    S, o = lax.scan(step, s0, xs)
    return _from_chunks(o, L), S


def gated_delta_chunked(q, k, v, g, beta, s0):
    L = q.shape[1]
    c = min(DN_CHUNK, L)
    n = -(-L // c)
    xs = tuple(_to_chunks(t, c, n) for t in (q, k, v, g, beta))
    idx = jnp.arange(c)
    tril = idx[:, None] >= idx[None, :]
    strict = idx[:, None] > idx[None, :]
    eye = jnp.eye(c, dtype=jnp.float32)

    def step(S, inp):
        qi, ki, vi, gi, bi = inp
        gcum = jnp.cumsum(gi, axis=-1)
        decay = jnp.exp(jnp.where(tril, gcum[..., :, None] - gcum[..., None, :], -jnp.inf))
        kk = jnp.einsum('bhik,bhjk->bhij', ki, ki)
        lower = jnp.where(strict, bi[..., :, None] * kk * decay, 0.0)
        rhs = jnp.concatenate([vi * bi[..., None], ki * (bi * jnp.exp(gcum))[..., None]], axis=-1)
        sol = lax.linalg.triangular_solve(lower + eye, rhs, left_side=True, lower=True,
                                          unit_diagonal=True)
        u_v, w_k = sol[..., :DN_DV], sol[..., DN_DV:]
        v_new = u_v - jnp.einsum('bhik,bhkv->bhiv', w_k, S)
        scores = jnp.einsum('bhik,bhjk->bhij', qi, ki) * decay
        o = (jnp.einsum('bhik,bhkv->bhiv', qi * jnp.exp(gcum)[..., None], S)
             + jnp.einsum('bhij,bhjv->bhiv', scores, v_new))
        g_last = gcum[..., -1:]
        S = (S * jnp.exp(g_last)[..., None]
             + jnp.einsum('bhjk,bhjv->bhkv', ki * jnp.exp(g_last - gcum)[..., None], v_new))
        return S, o

    S, o = lax.scan(step, s0, xs)
    return _from_chunks(o, L), S


def s5_scan(u, h0_re, h0_im, lam_re, lam_im, log_dt, b_re, b_im, c_re, c_im, d):
    bsz, L, _ = u.shape
    ug = u.reshape(bsz, L, S5_GROUPS, S5_GROUP_CH)
    dt = jnp.exp(log_dt)[:, None]
    mag = jnp.exp(lam_re * dt)
    ar, ai = mag * jnp.cos(lam_im * dt), mag * jnp.sin(lam_im * dt)
    den = lam_re * lam_re + lam_im * lam_im
    wr = ((ar - 1.0) * lam_re + ai * lam_im) / den
    wi = (ai * lam_re - (ar - 1.0) * lam_im) / den
    bb_re = wr[..., None] * b_re - wi[..., None] * b_im
    bb_im = wr[..., None] * b_im + wi[..., None] * b_re
    bu_re = jnp.einsum('gph,blgh->blgp', bb_re, ug)
    bu_im = jnp.einsum('gph,blgh->blgp', bb_im, ug)
    bu_re = bu_re.at[:, 0].add(ar * h0_re - ai * h0_im)
    bu_im = bu_im.at[:, 0].add(ar * h0_im + ai * h0_re)
    a_re = jnp.broadcast_to(ar, bu_re.shape)
    a_im = jnp.broadcast_to(ai, bu_im.shape)

    def combine(e1, e2):
        ar1, ai1, br1, bi1 = e1
        ar2, ai2, br2, bi2 = e2
        return (ar1 * ar2 - ai1 * ai2, ar1 * ai2 + ai1 * ar2,
                ar2 * br1 - ai2 * bi1 + br2, ar2 * bi1 + ai2 * br1 + bi2)

    _, _, h_re, h_im = lax.associative_scan(combine, (a_re, a_im, bu_re, bu_im), axis=1)
    y = jnp.einsum('ghp,blgp->blgh', c_re, h_re) - jnp.einsum('ghp,blgp->blgh', c_im, h_im)
    y = y.reshape(bsz, L, S5_WIDTH) + d * u
    return y, h_re[:, -1], h_im[:, -1]


def causal_conv(x, buf, w):
    L = x.shape[1]
    xp = jnp.concatenate([buf, x], axis=1)
    y = xp[:, 0:L] * w[0]
    for i in range(1, CONV_W):
        y = y + xp[:, i:i + L] * w[i]
    return y, xp[:, L:]


def even_mixer(h, s_gla, s_re, s_im, w_in, w_gate_up, b_gate, gla_norm_w, lam_re, lam_im,
               log_dt, b_re, b_im, c_re, c_im, d, w_glu, b_glu, w_out):
    f32 = jnp.float32
    bsz, L, _ = h.shape
    q, k, v, r, lr, u, sg = jnp.split((h @ w_in).astype(f32), _offsets(EVEN_SIZES), axis=-1)
    q = q.reshape(bsz, L, GLA_HEADS, GLA_DK) * (GLA_DK ** -0.5)
    k = k.reshape(bsz, L, GLA_HEADS, GLA_DK)
    v = v.reshape(bsz, L, GLA_HEADS, GLA_DV)
    log_a = jax.nn.log_sigmoid(lr @ w_gate_up.astype(f32) + b_gate.astype(f32))
    log_a = log_a.reshape(bsz, L, GLA_HEADS, GLA_DK) / GLA_GATE_NORM
    o, s_gla_new = gla_chunked(q, k, v, log_a, s_gla.astype(f32))
    o = rmsnorm(o, gla_norm_w).reshape(bsz, L, GLA_V_WIDTH) * jax.nn.silu(r)
    y5, h_re, h_im = s5_scan(u, s_re.astype(f32), s_im.astype(f32), lam_re.astype(f32),
                             lam_im.astype(f32), log_dt.astype(f32), b_re.astype(f32),
                             b_im.astype(f32), c_re.astype(f32), c_im.astype(f32), d.astype(f32))
    y5 = jax.nn.gelu(y5)
    y5 = y5 * jax.nn.sigmoid(y5 @ w_glu.astype(f32) + b_glu.astype(f32)) * jax.nn.silu(sg)
    out = jnp.concatenate([o, y5], axis=-1).astype(h.dtype) @ w_out
    return out, s_gla_new, h_re, h_im


def odd_mixer(h, s_delta, s_conv, w_in, conv_w, a_log, dt_bias, norm_w, w_out):
    f32 = jnp.float32
    bsz, L, _ = h.shape
    qkv, z, b, a = jnp.split((h @ w_in).astype(f32), _offsets(ODD_SIZES), axis=-1)
    qkv, conv_new = causal_conv(qkv, s_conv.astype(f32), conv_w.astype(f32))
    qkv = jax.nn.silu(qkv)
    q, k, v = jnp.split(qkv, [DN_K_WIDTH, 2 * DN_K_WIDTH], axis=-1)
    q = l2norm(q.reshape(bsz, L, DN_HEADS, DN_DK)) * (DN_DK ** -0.5)
    k = l2norm(k.reshape(bsz, L, DN_HEADS, DN_DK))
    v = v.reshape(bsz, L, DN_HEADS, DN_DV)
    beta = jax.nn.sigmoid(b)
    g = -jnp.exp(a_log.astype(f32)) * jax.nn.softplus(a + dt_bias.astype(f32))
    o, s_new = gated_delta_chunked(q, k, v, g, beta, s_delta.astype(f32))
    o = rmsnorm(o, norm_w).reshape(bsz, L, DN_V_WIDTH) * jax.nn.silu(z)
    return o.astype(h.dtype) @ w_out, s_new, conv_new


def setup_inputs(seed: int = 0) -> dict:
    key = jax.random.key(seed)
    ks = iter(jax.random.split(key, 48))
    f32 = jnp.float32

    def nrm(shape, scale):
        return jax.random.normal(next(ks), shape, f32) * scale

    def unif(shape, lo, hi):
        return jax.random.uniform(next(ks), shape, f32, lo, hi)

    x_prompt = nrm((BATCH, SEQ, D_MODEL), 1.0)
    x_sample = nrm((DEC_BATCH, DEC_SEQ, D_MODEL), 1.0)
    state_gla = nrm((N_EVEN, DEC_BATCH, GLA_HEADS, GLA_DK, GLA_DV), 0.5)
    state_s5_re = nrm((N_EVEN, DEC_BATCH, S5_GROUPS, S5_P), 0.1)
    state_s5_im = nrm((N_EVEN, DEC_BATCH, S5_GROUPS, S5_P), 0.1)
    state_delta = nrm((N_ODD, DEC_BATCH, DN_HEADS, DN_DK, DN_DV), 0.5)
    state_conv = nrm((N_ODD, DEC_BATCH, CONV_W - 1, DN_CONV_DIM), 1.0)
    norm_w = 1.0 + nrm((DEPTH, D_MODEL), 0.02)
    final_norm_w = 1.0 + nrm((D_MODEL,), 0.02)
    w_in_even = nrm((N_EVEN, D_MODEL, EVEN_IN), D_MODEL ** -0.5)
    gla_w_gate_up = nrm((N_EVEN, GLA_RANK, GLA_K_WIDTH), GLA_RANK ** -0.5)
    gla_b_gate = nrm((N_EVEN, GLA_K_WIDTH), 0.1)
    gla_norm_w = 1.0 + nrm((N_EVEN, GLA_DV), 0.02)
    n_idx = jnp.arange(S5_P, dtype=f32)
    s5_lambda_re = -0.5 + nrm((N_EVEN, S5_GROUPS, S5_P), 0.01)
    s5_lambda_im = math.pi * n_idx + nrm((N_EVEN, S5_GROUPS, S5_P), 0.01)
    s5_log_dt = unif((N_EVEN, S5_GROUPS), math.log(S5_DT_MIN), math.log(S5_DT_MAX))
    s5_b_re = nrm((N_EVEN, S5_GROUPS, S5_P, S5_GROUP_CH), (2 * S5_GROUP_CH) ** -0.5)
    s5_b_im = nrm((N_EVEN, S5_GROUPS, S5_P, S5_GROUP_CH), (2 * S5_GROUP_CH) ** -0.5)
    s5_c_re = nrm((N_EVEN, S5_GROUPS, S5_GROUP_CH, S5_P), S5_P ** -0.5)
    s5_c_im = nrm((N_EVEN, S5_GROUPS, S5_GROUP_CH, S5_P), S5_P ** -0.5)
    s5_d = nrm((N_EVEN, S5_WIDTH), 1.0)
    s5_w_glu = nrm((N_EVEN, S5_WIDTH, S5_WIDTH), S5_WIDTH ** -0.5)
    s5_b_glu = nrm((N_EVEN, S5_WIDTH), 0.01)
    w_out_even = nrm((N_EVEN, GLA_V_WIDTH + S5_WIDTH, D_MODEL), (GLA_V_WIDTH + S5_WIDTH) ** -0.5)
    w_in_odd = nrm((N_ODD, D_MODEL, ODD_IN), D_MODEL ** -0.5)
    dn_conv_w = nrm((N_ODD, CONV_W, DN_CONV_DIM), CONV_W ** -0.5)
    dn_a_log = jnp.log(unif((N_ODD, DN_HEADS), 1.0, 16.0))
    dt = jnp.exp(unif((N_ODD, DN_HEADS), math.log(0.001), math.log(0.1)))
    dn_dt_bias = dt + jnp.log(-jnp.expm1(-dt))
    dn_norm_w = 1.0 + nrm((N_ODD, DN_DV), 0.02)
    w_out_odd = nrm((N_ODD, DN_V_WIDTH, D_MODEL), DN_V_WIDTH ** -0.5)
    return {
        'x_prompt': x_prompt, 'x_sample': x_sample,
        'state_gla': state_gla, 'state_s5_re': state_s5_re, 'state_s5_im': state_s5_im,
        'state_delta': state_delta, 'state_conv': state_conv,
        'norm_w': norm_w, 'final_norm_w': final_norm_w,
        'w_in_even': w_in_even, 'gla_w_gate_up': gla_w_gate_up, 'gla_b_gate': gla_b_gate,
        'gla_norm_w': gla_norm_w, 's5_lambda_re': s5_lambda_re, 's5_lambda_im': s5_lambda_im,
        's5_log_dt': s5_log_dt, 's5_b_re': s5_b_re, 's5_b_im': s5_b_im,
        's5_c_re': s5_c_re, 's5_c_im': s5_c_im, 's5_d': s5_d,
        's5_w_glu': s5_w_glu, 's5_b_glu': s5_b_glu, 'w_out_even': w_out_even,
        'w_in_odd': w_in_odd, 'dn_conv_w': dn_conv_w, 'dn_a_log': dn_a_log,
        'dn_dt_bias': dn_dt_bias, 'dn_norm_w': dn_norm_w, 'w_out_odd': w_out_odd,
    }


def reference(x_prompt, x_sample, state_gla, state_s5_re, state_s5_im, state_delta, state_conv,
              norm_w, final_norm_w, w_in_even, gla_w_gate_up, gla_b_gate, gla_norm_w,
              s5_lambda_re, s5_lambda_im, s5_log_dt, s5_b_re, s5_b_im, s5_c_re, s5_c_im, s5_d,
              s5_w_glu, s5_b_glu, w_out_even, w_in_odd, dn_conv_w, dn_a_log, dn_dt_bias,
              dn_norm_w, w_out_odd):
    f32 = jnp.float32
    nb = x_prompt.shape[0]
    xp, xs = x_prompt, x_sample
    gla_p, gla_s, s5r_p, s5i_p, s5r_s, s5i_s = [], [], [], [], [], []
    dn_p, dn_s, cv_p, cv_s = [], [], [], []
    for layer in range(DEPTH):
        i = layer // 2
        if layer % 2 == 0:
            prm = (w_in_even[i], gla_w_gate_up[i], gla_b_gate[i], gla_norm_w[i], s5_lambda_re[i],
                   s5_lambda_im[i], s5_log_dt[i], s5_b_re[i], s5_b_im[i], s5_c_re[i], s5_c_im[i],
                   s5_d[i], s5_w_glu[i], s5_b_glu[i], w_out_even[i])
            zg = jnp.zeros((nb, GLA_HEADS, GLA_DK, GLA_DV), f32)
            zs = jnp.zeros((nb, S5_GROUPS, S5_P), f32)
            dp, g_p, r_p, m_p = even_mixer(rmsnorm(xp, norm_w[layer]), zg, zs, zs, *prm)
            ds, g_s, r_s, m_s = even_mixer(rmsnorm(xs, norm_w[layer]), state_gla[i],
                                           state_s5_re[i], state_s5_im[i], *prm)
            xp = xp + dp
            xs = xs + ds
            gla_p.append(g_p)
            gla_s.append(g_s)
            s5r_p.append(r_p)
            s5i_p.append(m_p)
            s5r_s.append(r_s)
            s5i_s.append(m_s)
        else:
            prm = (w_in_odd[i], dn_conv_w[i], dn_a_log[i], dn_dt_bias[i], dn_norm_w[i], w_out_odd[i])
            zd = jnp.zeros((nb, DN_HEADS, DN_DK, DN_DV), f32)
            zc = jnp.zeros((nb, CONV_W - 1, DN_CONV_DIM), f32)
            dp, d_p, c_p = odd_mixer(rmsnorm(xp, norm_w[layer]), zd, zc, *prm)
            ds, d_s, c_s = odd_mixer(rmsnorm(xs, norm_w[layer]), state_delta[i], state_conv[i], *prm)
            xp = xp + dp
            xs = xs + ds
            dn_p.append(d_p)
            dn_s.append(d_s)
            cv_p.append(c_p)
            cv_s.append(c_s)
    y_prompt = rmsnorm(xp, final_norm_w)
    y_sample = rmsnorm(xs, final_norm_w)
    return (y_prompt, y_sample, jnp.stack(gla_p), jnp.stack(gla_s), jnp.stack(s5r_p),
            jnp.stack(s5i_p), jnp.stack(s5r_s), jnp.stack(s5i_s), jnp.stack(dn_p),
            jnp.stack(dn_s), jnp.stack(cv_p), jnp.stack(cv_s))
```

```python
import os
import numpy as np
from contextlib import ExitStack
import concourse.bass as bass
import concourse.mybir as mybir
from concourse.bass_utils import run_bass_kernel_spmd

F32 = mybir.dt.float32
BF16 = mybir.dt.bfloat16
I32 = mybir.dt.int32
AF = mybir.ActivationFunctionType
ALU = mybir.AluOpType

NCORES = 8
SEQ = 2048
NT = SEQ // 128
NS = 16
EPS = 1e-6
NEG = -1.0e5
STAGE = float(os.environ.get('KSTAGE', '99'))


class Buf:
    def __init__(self, name):
        self.name = name
        self.w = None
        self.r = []


class T:
    def __init__(self, k, name, shape, dtype, space="sbuf"):
        if space == "sbuf":
            self.t = k.es.enter_context(k.nc.sbuf_tensor("t_" + name, shape, dtype))
        else:
            self.t = k.es.enter_context(k.nc.psum_tensor("t_" + name, shape, dtype))
        self.b = Buf(name)
        self.b.psum = (space != "sbuf")

    def __getitem__(self, idx):
        return self.t[idx]


class V:
    def __init__(self, ap, name):
        self.ap = ap
        self.b = Buf(name)

    def __getitem__(self, idx):
        return self.ap[idx]


class K:
    def __init__(self):
        self.nc = bass.Bass("TRN2", target_bir_lowering=False)
        self.es = ExitStack()
        self.eng = {}
        for n in ["pe", "act", "dve", "pool", "sp"]:
            sem = self.es.enter_context(self.nc.semaphore("s_" + n))
            self.eng[n] = dict(sem=sem, cnt=0, waited={}, th=[])
        self.dma_sems = [self.es.enter_context(self.nc.semaphore(f"d{i}")) for i in range(32)]
        self.dma_use = [0] * 32
        self.ndma = 0
        self.ndma_q = {"sp": 0, "pool": 0}
        self.out_toks = []

    def _bufs(self, lst):
        out = []
        for x in lst:
            if isinstance(x, (T, V)):
                if getattr(x, "parts", None):
                    out.extend(x.parts)
                else:
                    out.append(x.b)
            else:
                out.append(x)
        return out

    def _deps(self, en, R, W):
        E = self.eng[en]
        deps = []
        for b in R:
            if b.w is not None:
                deps.append(b.w)
            if getattr(b, "psum", False):
                deps.extend(t_ for t_ in b.r if t_[2] != en)
        for b in W:
            if b.w is not None:
                deps.append(b.w)
            deps.extend(b.r)
        waits = {}
        for (sem, val, src) in deps:
            if src == en and en == "pe":
                continue
            key = id(sem)
            if E["waited"].get(key, 0) >= val:
                continue
            if key not in waits or waits[key][1] < val:
                waits[key] = (sem, val)
        for key, (sem, val) in waits.items():
            E["waited"][key] = val
            E["th"].append(lambda e, sem=sem, val=val: e.wait_ge(sem, val))

    def op(self, en, fn, R=(), W=()):
        R = self._bufs(R)
        W = self._bufs(W)
        E = self.eng[en]
        if E["cnt"] >= 30000:
            E.setdefault("past", []).append((E["sem"], E["cnt"]))
            self.nsem_extra = getattr(self, "nsem_extra", 0) + 1
            E["sem"] = self.es.enter_context(self.nc.semaphore(f"s_{en}_{self.nsem_extra}"))
            E["cnt"] = 0
        self._deps(en, R, W)
        E["cnt"] += 1
        sem = E["sem"]
        E["th"].append(lambda e, fn=fn, sem=sem: fn(e).then_inc(sem, 1))
        tok = (sem, E["cnt"], en)
        for b in W:
            b.w = tok
            b.r = []
        for b in R:
            b.r.append(tok)
            if len(b.r) > 12:
                b.r = b.r[-12:] if False else b.r

    def dma(self, fn, R=(), W=(), en="sp", out=False):
        R = self._bufs(R)
        W = self._bufs(W)
        E = self.eng[en]
        if en == "pool":
            i = 24 + self.ndma_q["pool"] % 8
        else:
            i = self.ndma_q["sp"] % 24
        self.ndma_q[en] += 1
        self.ndma += 1
        sem = self.dma_sems[i]
        prev = self.dma_use[i]
        self._deps(en, R, W)
        if prev > 0 and E["waited"].get(id(sem), 0) < prev:
            E["waited"][id(sem)] = prev
            E["th"].append(lambda e, sem=sem, prev=prev: e.wait_ge(sem, prev))
        val = prev + 16
        self.dma_use[i] = val
        E["th"].append(lambda e, fn=fn, sem=sem: fn(e).then_inc(sem, 16))
        tok = (sem, val, "dma")
        for b in W:
            b.w = tok
            b.r = []
        for b in R:
            b.r.append(tok)
        if out:
            self.out_toks.append(tok)
        return tok

    def barrier(self):
        toks = []
        for n, E in self.eng.items():
            if E["cnt"] > 0:
                toks.append((E["sem"], E["cnt"]))
            for (ps_, pc_) in E.get("past", []):
                toks.append((ps_, pc_))
        for i, sem in enumerate(self.dma_sems):
            if self.dma_use[i] > 0:
                toks.append((sem, self.dma_use[i]))
        for n, E in self.eng.items():
            for (sem, val) in toks:
                if sem is E["sem"] or any(sem is p_[0] for p_ in E.get("past", [])):
                    continue
                if E["waited"].get(id(sem), 0) >= val:
                    continue
                E["waited"][id(sem)] = val
                E["th"].append(lambda e, sem=sem, val=val: e.wait_ge(sem, val))

    def finish(self):
        E = self.eng["sp"]
        last = {}
        for (sem, val, _) in self.out_toks:
            if id(sem) not in last or last[id(sem)][1] < val:
                last[id(sem)] = (sem, val)
        for (sem, val) in last.values():
            E["th"].append(lambda e, sem=sem, val=val: e.wait_ge(sem, val))
        with self.nc.Block() as block:
            @block.tensor
            def _(e):
                for t in self.eng["pe"]["th"]:
                    t(e)

            @block.scalar
            def _(e):
                for t in self.eng["act"]["th"]:
                    t(e)

            @block.vector
            def _(e):
                for t in self.eng["dve"]["th"]:
                    t(e)

            @block.gpsimd
            def _(e):
                for t in self.eng["pool"]["th"]:
                    t(e)

            @block.sync
            def _(e):
                for t in self.eng["sp"]["th"]:
                    t(e)
        self.es.close()


CN = ["ident", "triuc", "c3", "masku", "trib", "c3b", "sela", "selb", "negs", "negi", "ones", "nones", "iota1"]


def make_consts():
    s = np.arange(128)[:, None]
    t = np.arange(128)[None, :]
    same = (s // 64) == (t // 64)
    c = {}
    c["ident"] = (s == t).astype(np.float32)
    c["triuc"] = (s <= t).astype(np.float32) * (-1.0 / 16.0)
    c["c3"] = (s > t).astype(np.float32) * (-1.0 / 16.0)
    c["masku"] = (s <= t).astype(np.float32)
    c["trib"] = ((s <= t) & same).astype(np.float32)
    c["c3b"] = ((s > t) & same).astype(np.float32)
    c["sela"] = np.broadcast_to((s < 64), (128, 128)).astype(np.float32)
    c["selb"] = np.broadcast_to((s >= 64), (128, 128)).astype(np.float32)
    c["negs"] = np.where((t < s) & same, 0.0, NEG).astype(np.float32)
    c["negi"] = np.where((s <= t) & same, 0.0, NEG).astype(np.float32)
    c["ones"] = np.ones((128, 128), np.float32)
    c["nones"] = -np.ones((128, 128), np.float32)
    c["iota1"] = np.broadcast_to(np.arange(1, 129, dtype=np.float32)[None, :], (128, 128)).copy()
    return np.stack([c[n] for n in CN], axis=1)


def build_program(nlayers=4, nt=NT, ns=NS, final=True):
    k = K()
    nc = k.nc

    def din(name, shape):
        return nc.dram_tensor(name, list(shape), F32, kind="ExternalInput").ap()

    def dout(name, shape):
        return nc.dram_tensor(name, list(shape), F32, kind="ExternalOutput").ap()

    D = {}
    D["xT"] = din("xT", [128, 8, SEQ + NS])
    D["consts"] = din("consts", [128, len(CN), 128])
    D["nw"] = din("nw", [128, 4, 8])
    D["fnw"] = din("fnw", [128, 8])
    for i in range(2):
        D[f"wie{i}"] = din(f"wie{i}", [128, 8, 4112])
        D[f"woe{i}"] = din(f"woe{i}", [128, 12, 1024])
        D[f"wgl{i}"] = din(f"wgl{i}", [128, 4, 512])
        D[f"wgu{i}"] = din(f"wgu{i}", [16, 512])
        D[f"bgt{i}"] = din(f"bgt{i}", [1, 512])
        D[f"gnw{i}"] = din(f"gnw{i}", [128, 2])
        D[f"bgf{i}"] = din(f"bgf{i}", [128, 4])
        for nm in ["lre", "lim", "ldt"]:
            D[f"{nm}{i}"] = din(f"{nm}{i}", [128, 16])
        for nm in ["bre", "bim", "cre", "cim"]:
            D[f"{nm}{i}"] = din(f"{nm}{i}", [128, 16, 128])
        D[f"s5d{i}"] = din(f"s5d{i}", [128, 4])
        D[f"bgl{i}"] = din(f"bgl{i}", [128, 4])
        D[f"sgla{i}"] = din(f"sgla{i}", [NS, 4, 128, 256])
        D[f"s5r{i}"] = din(f"s5r{i}", [128, 16, NS])
        D[f"s5i{i}"] = din(f"s5i{i}", [128, 16, NS])
        D[f"wio{i}"] = din(f"wio{i}", [128, 8, 4112])
        D[f"woo{i}"] = din(f"woo{i}", [128, 8, 1024])
        D[f"cvw{i}"] = din(f"cvw{i}", [128, 24, 4])
        D[f"alg{i}"] = din(f"alg{i}", [128, 8])
        D[f"dtb{i}"] = din(f"dtb{i}", [128, 8])
        D[f"dnw{i}"] = din(f"dnw{i}", [128, 1])
        D[f"sdel{i}"] = din(f"sdel{i}", [NS, 8, 128, 128])
        D[f"scv{i}"] = din(f"scv{i}", [128, 24, 3, NS])
    D["smask"] = din("smask", [128, 1])
    O = {}
    O["yT"] = dout("yT", [128, 8, SEQ + NS])
    for i in range(2):
        O[f"ogp{i}"] = dout(f"ogp{i}", [4, 128, 256])
        O[f"ogs{i}"] = dout(f"ogs{i}", [NS, 4, 128, 256])
        O[f"o5rp{i}"] = dout(f"o5rp{i}", [128, 16])
        O[f"o5ip{i}"] = dout(f"o5ip{i}", [128, 16])
        O[f"o5rs{i}"] = dout(f"o5rs{i}", [128, 16, NS])
        O[f"o5is{i}"] = dout(f"o5is{i}", [128, 16, NS])
        O[f"odp{i}"] = dout(f"odp{i}", [8, 128, 128])
        O[f"ods{i}"] = dout(f"ods{i}", [NS, 8, 128, 128])
        O[f"ocp{i}"] = dout(f"ocp{i}", [128, 24, 3])
        O[f"ocs{i}"] = dout(f"ocs{i}", [128, 24, 3, NS])

    def mm(out, lhsT, rhs, R, W, start=True, stop=True):
        k.op("pe", lambda e: e.matmul(out, lhsT=lhsT, rhs=rhs, start=start, stop=stop), R, W)

    def tr(out, in_, ident, R, W):
        k.op("pe", lambda e: e.transpose(out, in_, ident), R, W)

    def act(out, in_, func, R, W, scale=1.0, bias=None, accum=None):
        kw = dict(out=out, in_=in_, func=func, scale=scale)
        if bias is not None:
            kw["bias"] = bias
        if accum is not None:
            kw["accum_out"] = accum
        k.op("act", lambda e: e.activation(**kw), R, W)

    def tt(out, in0, in1, op, R, W, en="dve"):
        k.op(en, lambda e: e.tensor_tensor(out=out, in0=in0, in1=in1, op=op), R, W)

    def ts(out, in0, s1, op0, R, W, s2=None, op1=None, en="dve"):
        if op1 is None:
            k.op(en, lambda e: e.tensor_scalar(out=out, in0=in0, scalar1=s1, scalar2=None, op0=op0), R, W)
        else:
            k.op(en, lambda e: e.tensor_scalar(out=out, in0=in0, scalar1=s1, scalar2=s2, op0=op0, op1=op1), R, W)

    def stt(out, in0, scalar, in1, op0, op1, R, W):
        k.op("dve", lambda e: e.scalar_tensor_tensor(out=out, in0=in0, scalar=scalar, in1=in1, op0=op0, op1=op1), R, W)

    def cp(out, in_, R, W, en="dve"):
        if en == "act":
            k.op("act", lambda e: e.activation(out=out, in_=in_, func=AF.Copy), R, W)
        else:
            k.op(en, lambda e: e.tensor_copy(out=out, in_=in_), R, W)

    def ms(ap, val, W, en="dve"):
        k.op(en, lambda e: e.memset(ap, val), (), W)

    def ld(out, in_, W, en="sp"):
        return k.dma(lambda e: e.dma_start(out=out, in_=in_), (), W, en=en)

    def st(out, in_, R, en="sp"):
        return k.dma(lambda e: e.dma_start(out=out, in_=in_), R, (), en=en, out=True)

    CT = T(k, "CT", [128, len(CN), 128], F32)
    ld(CT[:], D["consts"], [CT])
    ci = {n: j for j, n in enumerate(CN)}

    def C(n):
        return CT[:, ci[n], :]

    onesb = T(k, "onesb", [128, 128], BF16)
    cp(onesb[:], C("ones"), [CT], [onesb])
    identb = T(k, "identb", [128, 128], BF16)
    cp(identb[:], C("ident"), [CT], [identb])
    nw = T(k, "nw", [128, 4, 8], F32)
    ld(nw[:], D["nw"], [nw])
    fnw = T(k, "fnw", [128, 8], F32)
    ld(fnw[:], D["fnw"], [fnw])
    smask = T(k, "smask", [128, 1], F32)
    ld(smask[:], D["smask"], [smask])
    epsc = T(k, "epsc", [128, 1], F32)
    ms(epsc[:], EPS, [epsc])
    onec = T(k, "onec", [128, 1], F32)
    ms(onec[:], 1.0, [onec])

    WIN = T(k, "WIN", [128, 8, 4112], BF16)
    WOUT = T(k, "WOUT", [128, 12, 1024], BF16)
    PS = [T(k, f"PS{j}", [128, 512], F32, "psum") for j in range(7)]
    PSB = T(k, "PSB", [128, 1024], BF16, "psum")
    xt = T(k, "xt", [128, 8, 128], F32)
    xt.parts = [Buf(f"xtp{j_}") for j_ in range(8)]
    sq = T(k, "sq", [128, 8, 128], BF16)
    lnv = T(k, "lnv", [128, 128], F32)
    rstd = T(k, "rstd", [128, 128], F32)
    hn = T(k, "hn", [128, 8, 128], BF16)
    catT = T(k, "catT", [128, 12, 128], BF16)

    def load_weights_bf16(dst, src, nk, ncols):
        if getattr(dst, "parts", None) is None:
            dst.parts = []
        j_ = 0
        for kt in range(nk):
            for c0 in range(0, ncols, 2048):
                c1 = min(ncols, c0 + 2048)
                if j_ >= len(dst.parts):
                    dst.parts.append(Buf(f"part{j_}"))
                k.dma(lambda e, kt=kt, c0=c0, c1=c1: e.dma_start(out=dst[:, kt, c0:c1], in_=src[:, kt, c0:c1]), (), [dst.parts[j_]], en="pool")
                j_ += 1

    XB = [Buf(f"xb{j}") for j in range(NT + NS + 1)]

    def norm_tile(xsrc_ap_fn, xb, L, padded):
        if not padded:
            k.dma(lambda e: e.dma_start(out=xt[:], in_=xsrc_ap_fn), [xb], [xt])
        else:
            ms(xt[:], 0.0, [xt])
            k.dma(lambda e: e.dma_start(out=xt[:, :, 0:1], in_=xsrc_ap_fn, allow_slow_non_contiguous=True), [xb], [xt])
        act(sq[:], xt[:], AF.Square, [xt], [sq])
        for kt in range(8):
            mm(PS[0][:, 0:128], onesb[:], sq[:, kt, :], [onesb, sq], [PS[0]], start=(kt == 0), stop=(kt == 7))
        act(lnv[:], PS[0][:, 0:128], AF.Ln, [PS[0], epsc], [lnv], scale=1.0 / 1024.0, bias=epsc[:])
        act(rstd[:], lnv[:], AF.Exp, [lnv], [rstd], scale=-0.5)
        for kt in range(8):
            stt(hn[:, kt, :], xt[:, kt, :], nw[:, L, kt:kt + 1], rstd[:], ALU.mult, ALU.mult, [xt, nw, rstd], [hn])

    def out_proj_add(nft, xdst_fn, xb, padded):
        for dm in range(8):
            P_ = PS[dm % 4]
            for ft in range(nft):
                mm(P_[:, 0:128], WOUT[:, ft, dm * 128:(dm + 1) * 128], catT[:, ft, :], [WOUT, catT], [P_],
                   start=(ft == 0), stop=(ft == nft - 1))
            tt(xt[:, dm, :], xt[:, dm, :], P_[:, 0:128], ALU.add, [xt.parts[dm], P_], [xt.parts[dm]])
        if padded:
            k.dma(lambda e: e.dma_start(out=xdst_fn, in_=xt[:, :, 0:1], allow_slow_non_contiguous=True), [xt], [xb], out=True)
        else:
            k.dma(lambda e: e.dma_start(out=xdst_fn, in_=xt[:]), [xt], [xb], out=True)

    def out_proj_s(nft):
        W = slice(0, NS)
        for dm in range(8):
            P_ = PS[dm % 4]
            for ft in range(nft):
                mm(P_[:, W], WOUT[:, ft, dm * 128:(dm + 1) * 128], catT[:, ft, W], [WOUT, catT], [P_],
                   start=(ft == 0), stop=(ft == nft - 1))
            tt(xt[:, dm, W], xt[:, dm, W], P_[:, W], ALU.add, [xt.parts[dm], P_], [xt.parts[dm]])
        k.dma(lambda e: e.dma_start(out=O["yT"][:, :, SEQ:SEQ + NS], in_=xt[:, :, W]), [xt], [XB[NT]], out=True)

    NFW = 16010
    NBW = 18300
    ARF = T(k, "ARF", [128, NFW], F32)
    ARB = T(k, "ARB", [128, NBW], BF16)
    angi = T(k, "angi", [128, 512], I32)

    def make_alloc():
        off = {"f": 0, "b": 0}

        def TT_(name, shape, dt=F32):
            n = int(np.prod(shape[1:]))
            if dt == BF16:
                ar, key, lim_ = ARB, "b", NBW
            else:
                ar, key, lim_ = ARF, "f", NFW
            o = off[key]
            off[key] += n
            assert off[key] <= lim_, (name, key, off[key])
            ap = ar.t[0:shape[0], o:o + n]
            if len(shape) == 3:
                ap = ap.rearrange("p (a b) -> p a b", a=shape[1])
            return V(ap, name)
        return TT_

    pref = {"L": -1}

    def prefetch_win(Lnext):
        if Lnext >= nlayers:
            return
        src = D[f"wie{Lnext // 2}"] if Lnext % 2 == 0 else D[f"wio{Lnext // 2}"]
        load_weights_bf16(WIN, src, 8, 4112)
        pref["L"] = Lnext

    def even_layer(L, i):
        k.barrier()
        if pref["L"] != L:
            load_weights_bf16(WIN, D[f"wie{i}"], 8, 4112)
        load_weights_bf16(WOUT, D[f"woe{i}"], 12, 1024)
        if True:
            TT_ = make_alloc()
            wgl = TT_("wgl", [128, 4, 512], BF16)
            load_weights_bf16(wgl, D[f"wgl{i}"], 4, 512)
            wgu = TT_("wgu", [16, 512], BF16)
            k.dma(lambda e: e.dma_start(out=wgu[:], in_=D[f"wgu{i}"]), (), [wgu], en="pool")
            bgt = TT_("bgt", [1, 512], BF16)
            k.dma(lambda e: e.dma_start(out=bgt[:], in_=D[f"bgt{i}"]), (), [bgt], en="pool")
            gnw = TT_("gnw", [128, 2])
            ld(gnw[:], D[f"gnw{i}"], [gnw])
            s5d = TT_("s5d", [128, 4])
            ld(s5d[:], D[f"s5d{i}"], [s5d])
            bgl = TT_("bgl", [128, 4])
            ld(bgl[:], D[f"bgl{i}"], [bgl])
            nbgf = TT_("nbgf", [128, 4])
            ld(nbgf[:], D[f"bgf{i}"], [nbgf])
            ts(nbgf[:], nbgf[:], -1.0, ALU.mult, [nbgf], [nbgf])
            bglh = TT_("bglh", [128, 4])
            ts(bglh[:], bgl[:], 0.5, ALU.mult, [bgl], [bglh])
            Bre = TT_("Bre", [128, 16, 128], BF16)
            Bim = TT_("Bim", [128, 16, 128], BF16)
            load_weights_bf16(Bre, D[f"bre{i}"], 16, 128)
            load_weights_bf16(Bim, D[f"bim{i}"], 16, 128)
            Cwr = TT_("Cwr", [128, 16, 128], BF16)
            Cwi = TT_("Cwi", [128, 16, 128], BF16)
            cst = TT_("cst", [128, 4, 128]); otmp_pre = TT_("otmp_pre", [128, 128])
            lre = TT_("lre", [128, 16]); lim = TT_("lim", [128, 16]); ldt = TT_("ldt", [128, 16])
            ld(lre[:], D[f"lre{i}"], [lre]); ld(lim[:], D[f"lim{i}"], [lim]); ld(ldt[:], D[f"ldt{i}"], [ldt])
            dt_ = TT_("dt_", [128, 16]); rho = TT_("rho", [128, 16]); th = TT_("th", [128, 16])
            act(dt_[:], ldt[:], AF.Exp, [ldt], [dt_])
            t1 = TT_("t1", [128, 16]); t2 = TT_("t2", [128, 16])
            tt(t1[:], lre[:], dt_[:], ALU.mult, [lre, dt_], [t1])
            act(rho[:], t1[:], AF.Exp, [t1], [rho])
            tt(th[:], lim[:], dt_[:], ALU.mult, [lim, dt_], [th])
            COS = TT_("COS", [128, 16, 128]); SIN = TT_("SIN", [128, 16, 128])
            ang = TT_("ang", [128, 128])
            zr = TT_("zr", [128, 256]); zi = TT_("zi", [128, 256]); ta = TT_("ta", [128, 256]); tb = TT_("tb", [128, 256])
            angf = V(ta.ap[:, 0:128], "angf"); angf.b = ta.b
            angd = V(tb.ap[:, 0:128], "angd"); angd.b = tb.b
            gr = TT_("gr", [128, 256]); gi = TT_("gi", [128, 256])

            def sin_chain(dst3, a2, tf, td, ti, shift):
                ops = []
                ops.append(lambda: ts(tf[:], a2[:], 1.0 / (2 * np.pi), ALU.mult, [a2], [tf], s2=shift / (2 * np.pi), op1=ALU.add))
                ops.append(lambda: cp(ti, tf[:], [tf], [angi]))
                ops.append(lambda: cp(td[:], ti, [angi], [td]))
                ops.append(lambda: tt(tf[:], tf[:], td[:], ALU.subtract, [tf, td], [tf]))
                ops.append(lambda: ts(td[:], tf[:], 0.5, ALU.is_gt, [tf], [td]))
                ops.append(lambda: tt(tf[:], tf[:], td[:], ALU.subtract, [tf, td], [tf]))
                ops.append(lambda: ts(td[:], tf[:], -0.5, ALU.is_lt, [tf], [td]))
                ops.append(lambda: tt(tf[:], tf[:], td[:], ALU.add, [tf, td], [tf]))
                ops.append(lambda: act(dst3, tf[:].rearrange("p (a b) -> p a b", a=2), AF.Sin, [tf], [COS, SIN], scale=2 * np.pi * 0.999999))
                return ops

            for g2 in range(8):
                for q_ in range(2):
                    gp = g2 * 2 + q_
                    ts(zr[:, q_ * 128:(q_ + 1) * 128], C("iota1"), th[:, gp:gp + 1], ALU.mult, [CT, th], [zr])
                ca = sin_chain(SIN[:, g2 * 2:(g2 + 1) * 2, :], zr, zi, ta, angi[:, 0:256], 0.0)
                cb = sin_chain(COS[:, g2 * 2:(g2 + 1) * 2, :], zr, tb, gr, angi[:, 256:512], np.pi / 2)
                for oa, ob in zip(ca, cb):
                    oa()
                    ob()
            ar = TT_("ar", [128, 16]); ai = TT_("ai", [128, 16])
            tt(ar[:], rho[:], COS[:, :, 0], ALU.mult, [rho, COS], [ar])
            tt(ai[:], rho[:], SIN[:, :, 0], ALU.mult, [rho, SIN], [ai])
            den = TT_("den", [128, 16]); wr = TT_("wr", [128, 16]); wi = TT_("wi", [128, 16]); am1 = TT_("am1", [128, 16])
            tt(t1[:], lre[:], lre[:], ALU.mult, [lre], [t1])
            tt(t2[:], lim[:], lim[:], ALU.mult, [lim], [t2])
            tt(den[:], t1[:], t2[:], ALU.add, [t1, t2], [den])
            k.op("dve", lambda e: e.reciprocal(out=den[:], in_=den[:]), [den], [den])
            ts(am1[:], ar[:], -1.0, ALU.add, [ar], [am1])
            tt(t1[:], am1[:], lre[:], ALU.mult, [am1, lre], [t1])
            tt(t2[:], ai[:], lim[:], ALU.mult, [ai, lim], [t2])
            tt(t1[:], t1[:], t2[:], ALU.add, [t1, t2], [t1])
            tt(wr[:], t1[:], den[:], ALU.mult, [t1, den], [wr])
            tt(t1[:], ai[:], lre[:], ALU.mult, [ai, lre], [t1])
            tt(t2[:], am1[:], lim[:], ALU.mult, [am1, lim], [t2])
            tt(t1[:], t1[:], t2[:], ALU.subtract, [t1, t2], [t1])
            tt(wi[:], t1[:], den[:], ALU.mult, [t1, den], [wi])
            iwr = TT_("iwr", [128, 16]); iwi = TT_("iwi", [128, 16])
            tt(t1[:], wr[:], wr[:], ALU.mult, [wr], [t1])
            tt(t2[:], wi[:], wi[:], ALU.mult, [wi], [t2])
            tt(t1[:], t1[:], t2[:], ALU.add, [t1, t2], [t1])
            k.op("dve", lambda e: e.reciprocal(out=t1[:], in_=t1[:]), [t1], [t1])
            tt(iwr[:], wr[:], t1[:], ALU.mult, [wr, t1], [iwr])
            tt(iwi[:], wi[:], t1[:], ALU.mult, [wi, t1], [iwi])
            ts(iwi[:], iwi[:], -1.0, ALU.mult, [iwi], [iwi])
            cstB = [Buf("cstA"), Buf("cstB")]
            for gp in range(16):
                q_ = gp % 2
                cb_ = cstB[q_]
                tA, tB = (ang, angf) if q_ == 0 else (angd, otmp_pre)
                ld(cst[:, 2 * q_, :], D[f"cre{i}"][:, gp, :], [cb_])
                ld(cst[:, 2 * q_ + 1, :], D[f"cim{i}"][:, gp, :], [cb_])
                ts(tA[:], cst[:, 2 * q_ + 1, :], wi[:, gp:gp + 1], ALU.mult, [cb_, wi], [tA])
                stt(Cwr[:, gp, :], cst[:, 2 * q_, :], wr[:, gp:gp + 1], tA[:], ALU.mult, ALU.subtract, [cb_, wr, tA], [Cwr])
                ts(tA[:], cst[:, 2 * q_ + 1, :], wr[:, gp:gp + 1], ALU.mult, [cb_, wr], [tA])
                stt(tB[:], cst[:, 2 * q_, :], wi[:, gp:gp + 1], tA[:], ALU.mult, ALU.add, [cb_, wi, tA], [tB])
                ts(Cwi[:, gp, :], tB[:], -1.0, ALU.mult, [tB], [Cwi])
            RHO = TT_("RHO", [128, 16, 128])
            for gp in range(16):
                ts(RHO[:, gp, :], C("ones"), rho[:, gp:gp + 1], ALU.mult, [CT, rho], [RHO])

            S = TT_("S", [128, 4, 256]); Sbf = TT_("Sbf", [128, 4, 256], BF16)
            hpr = TT_("hpr", [128, 16]); hpi = TT_("hpi", [128, 16])
            hs0 = TT_("hs0", [128, 16]); hs1 = TT_("hs1", [128, 16]); hs2 = TT_("hs2", [128, 16]); hs3 = TT_("hs3", [128, 16])
            gT = TT_("gT", [128, 8, 128]); uT = TT_("uT", [128, 4, 128]); uTb = TT_("uTb", [128, 4, 128], BF16)
            sgs = TT_("sgs", [128, 4, 128]); lrT = TT_("lrT", [16, 128], BF16)
            vtok = TT_("vtok", [128, 1024], BF16)
            sp = TT_("sp", [128, 512]); E1 = TT_("E1", [128, 4, 128]); E2 = TT_("E2", [128, 4, 128])
            E3 = TT_("E3", [128, 512]); qs = TT_("qs", [128, 4, 128], BF16); ks = TT_("ks", [128, 4, 128], BF16)
            khat = TT_("khat", [128, 512], BF16); ATm = TT_("ATm", [128, 128], BF16)
            sqo = TT_("sqo", [128, 2, 128], BF16); otmp = TT_("otmp", [128, 128])
            otr = [otmp] + [TT_(f"otmp{q_}", [128, 128]) for q_ in range(1, 4)]
            ATm2 = TT_("ATm2", [128, 128], BF16); sqo2 = TT_("sqo2", [128, 2, 128], BF16)
            Sh4 = [V(S.ap[:, h_, :], f"S4_{h_}") for h_ in range(4)]
            Sbfh4 = [V(Sbf.ap[:, h_, :], f"Sbf4_{h_}") for h_ in range(4)]
            otc = {"n": 0}

            def nxt_ot():
                otc["n"] += 1
                return otr[otc["n"] % 4]
            hrb4 = TT_("hrb4", [128, 4, 128], BF16); hib4 = TT_("hib4", [128, 4, 128], BF16)
            ta2 = TT_("ta2", [128, 256]); tb2 = TT_("tb2", [128, 256])
            hr2 = TT_("hr2", [128, 256]); hi2 = TT_("hi2", [128, 256])
            y5 = TT_("y5", [128, 4, 128]); y5b = TT_("y5b", [128, 4, 128], BF16)
            onesrow = TT_("onesrow", [1, 128], BF16)
            ms(onesrow[:], 1.0, [onesrow])
            ones1 = TT_("ones1", [128, 1]); ms(ones1[:], 1.0, [ones1])

            def run_tile(xsrc, xdst, xb, padded, sidx):
                norm_tile(xsrc, xb, L, padded)
                if padded:
                    for h in range(4):
                        ld(S[:, h, :], D[f"sgla{i}"][sidx, h], [S])
                    cp(Sbf[:], S[:], [S], [Sbf], en="act")
                    ld(hs0[:], D[f"s5r{i}"][sidx], [hs0])
                    ld(hs1[:], D[f"s5i{i}"][sidx], [hs1])
                    tt(hs2[:], hs0[:], iwr[:], ALU.mult, [hs0, iwr], [hs2])
                    tt(hs3[:], hs1[:], iwi[:], ALU.mult, [hs1, iwi], [hs3])
                    tt(hpr[:], hs2[:], hs3[:], ALU.subtract, [hs2, hs3], [hpr])
                    tt(hs2[:], hs0[:], iwi[:], ALU.mult, [hs0, iwi], [hs2])
                    tt(hs3[:], hs1[:], iwr[:], ALU.mult, [hs1, iwr], [hs3])
                    tt(hpi[:], hs2[:], hs3[:], ALU.add, [hs2, hs3], [hpi])

                def proj_f(col0, evac):
                    for kt in range(8):
                        mm(PS[1][:, 0:128], WIN[:, kt, col0:col0 + 128], hn[:, kt, :], [WIN, hn], [PS[1]], start=(kt == 0), stop=(kt == 7))
                    evac(PS[1][:, 0:128])

                if STAGE <= 1:
                    return
                rr = {"n": 0}
                banks = [PS[1], PS[3], PS[4], PS[5]]

                def proj_f(col0, evac):
                    P_ = banks[rr["n"] % 4]
                    rr["n"] += 1
                    for kt in range(8):
                        mm(P_[:, 0:128], WIN[:, kt, col0:col0 + 128], hn[:, kt, :], [WIN, hn], [P_], start=(kt == 0), stop=(kt == 7))
                    evac(P_[:, 0:128], P_)
                for j in range(4):
                    def ev_u(p, P_, j=j):
                        cp(uT[:, j, :], p, [P_], [uT], en="act")
                        cp(uTb[:, j, :], uT[:, j, :], [uT], [uTb], en="pool")
                    proj_f(3088 + j * 128, ev_u)

                def gen_proj_gla():
                    for kt in range(8):
                        mm(PS[1][0:16, 0:128], WIN[:, kt, 3072:3088], hn[:, kt, :], [WIN, hn], [PS[1]], start=(kt == 0), stop=(kt == 7))
                    cp(lrT[:], PS[1][0:16, 0:128], [PS[1]], [lrT])
                    yield
                    mm(PS[2][:, :], lrT[:], wgu[:], [lrT, wgu], [PS[2]], start=True, stop=False)
                    mm(PS[2][:, :], onesrow[:], bgt[:], [onesrow, bgt], [PS[2]], start=False, stop=True)
                    act(E3[:], PS[2][:, :], AF.Exp, [PS[2]], [E3], scale=-1.0)
                    yield
                    act(sp[:], E3[:], AF.Ln, [E3, ones1], [sp], bias=ones1[:])
                    yield
                    for h in range(4):
                        mm(PS[3][:, h * 128:(h + 1) * 128], sp[:, h * 128:(h + 1) * 128], C("triuc"), [sp, CT], [PS[3]])
                    mm(PS[2][:, :], C("c3"), sp[:], [CT, sp], [PS[2]])
                    act(E1[:], PS[3][:, :].rearrange("p (a b) -> p a b", a=4), AF.Exp, [PS[3]], [E1])
                    yield
                    act(E2[:], PS[3][:, :].rearrange("p (a b) -> p a b", a=4), AF.Exp, [PS[3]], [E2], scale=-1.0)
                    act(E3[:], PS[2][:, :], AF.Exp, [PS[2]], [E3])
                    yield
                    for h in range(4):
                        proj_f(h * 128, lambda p, P_, h=h: stt(qs[:, h, :], p, 128.0 ** -0.5, E1[:, h, :], ALU.mult, ALU.mult, [P_, E1], [qs]))
                        proj_f(512 + h * 128, lambda p, P_, h=h: tt(ks[:, h, :], p, E2[:, h, :], ALU.mult, [P_, E2], [ks]))
                        yield
                    for j in range(8):
                        proj_f(2048 + j * 128, lambda p, P_, j=j, : (lambda o_: (act(o_[:], p, AF.Tanh, [P_], [o_], scale=0.5), stt(gT[:, j, :], o_[:], 1.0, p, ALU.add, ALU.mult, [o_, P_], [gT])))(nxt_ot()))
                        ts(gT[:, j, :], gT[:, j, :], gnw[:, (j % 2):(j % 2) + 1], ALU.mult, [gT, gnw], [gT], s2=0.5, op1=ALU.mult, en="pool")
                        if j % 2 == 1:
                            yield
                    for j in range(4):
                        proj_f(3600 + j * 128, lambda p, P_, j=j: (lambda o_: (act(o_[:], p, AF.Tanh, [P_], [o_], scale=0.5), stt(sgs[:, j, :], o_[:], 1.0, p, ALU.add, ALU.mult, [o_, P_], [sgs])))(nxt_ot()))
                        if j % 2 == 1:
                            yield
                    for c in range(2):
                        for kt in range(8):
                            mm(PS[2][:, :], hn[:, kt, :], WIN[:, kt, 1024 + c * 512:1024 + (c + 1) * 512], [hn, WIN], [PS[2]], start=(kt == 0), stop=(kt == 7))
                        cp(vtok[:, c * 512:(c + 1) * 512], PS[2][:, :], [PS[2]], [vtok], en="act")
                        yield
                    for kt in range(8):
                        mm(PS[2][:, :], hn[:, kt, :], WIN[:, kt, 512:1024], [hn, WIN], [PS[2]], start=(kt == 0), stop=(kt == 7))
                    tt(khat[:], PS[2][:, :], E3[:], ALU.mult, [PS[2], E3], [khat])
                    yield
                    def gla_gen(par, PA, PB_, ATm_, sqo_, lnv_, rstd_):
                        for h in (par, par + 2):
                            S_h, Sbf_h = Sh4[h], Sbfh4[h]
                            mm(PA[:, 0:128], ks[:, h, :], qs[:, h, :], [ks, qs], [PA])
                            tt(ATm_[:], PA[:, 0:128], C("masku"), ALU.mult, [PA, CT], [ATm_])
                            yield
                            for half in range(2):
                                mm(PB_[:, half * 128:(half + 1) * 128], Sbf_h[:, half * 128:(half + 1) * 128], qs[:, h, :], [Sbf_h, qs], [PB_], start=True, stop=False)
                                mm(PB_[:, half * 128:(half + 1) * 128], vtok[:, h * 256 + half * 128:h * 256 + (half + 1) * 128], ATm_[:], [vtok, ATm_], [PB_], start=False, stop=True)
                            mm(PA[:, 256:512], khat[:, h * 128:(h + 1) * 128], vtok[:, h * 256:(h + 1) * 256], [khat, vtok], [PA])
                            stt(S_h[:, :], S_h[:, :], E1[:, h, 127:128], PA[:, 256:512], ALU.mult, ALU.add, [S_h, E1, PA], [S_h])
                            act(sqo_[:], PB_[:, 0:256].rearrange("p (a b) -> p a b", a=2), AF.Square, [PB_], [sqo_])
                            yield
                            cp(Sbf_h[:, :], S_h[:, :], [S_h], [Sbf_h], en="act")
                            mm(PA[:, 128:256], onesb[:], sqo_[:, 0, :], [onesb, sqo_], [PA], start=True, stop=False)
                            mm(PA[:, 128:256], onesb[:], sqo_[:, 1, :], [onesb, sqo_], [PA], start=False, stop=True)
                            act(lnv_[:], PA[:, 128:256], AF.Ln, [PA, epsc], [lnv_], scale=1.0 / 256.0, bias=epsc[:])
                            yield
                            act(rstd_[:], lnv_[:], AF.Exp, [lnv_], [rstd_], scale=-0.5)
                            yield
                            for half in range(2):
                                o_ = otr[par * 2 + half]
                                tt(o_[:], PB_[:, half * 128:(half + 1) * 128], rstd_[:], ALU.mult, [PB_, rstd_], [o_])
                                tt(catT[:, h * 2 + half, :], o_[:], gT[:, h * 2 + half, :], ALU.mult, [o_, gT], [catT], en="pool")
                            yield
                    subs = [gla_gen(0, PS[3], PS[4], ATm, sqo, lnv, rstd), gla_gen(1, PS[1], PS[2], ATm2, sqo2, otmp_pre, ang)]
                    while subs:
                        for g_ in list(subs):
                            try:
                                next(g_)
                            except StopIteration:
                                subs.remove(g_)
                        yield

                def gen_s5():
                    col = 127
                    v2 = lambda a_: a_[:, :].rearrange("p (a b) -> p a b", a=2)

                    def sA(c2):
                        ut = c2 // 2
                        for q in range(2):
                            gp = c2 * 2 + q
                            mm(PS[0][:, q * 128:(q + 1) * 128], Bre[:, gp, :], uTb[:, ut, :], [Bre, uTb], [PS[0]])
                            mm(PS[6][:, q * 128:(q + 1) * 128], Bim[:, gp, :], uTb[:, ut, :], [Bim, uTb], [PS[6]])
                        cs = COS[:, c2 * 2:(c2 + 1) * 2, :]
                        sn = SIN[:, c2 * 2:(c2 + 1) * 2, :]
                        p5 = PS[0][:, 0:256].rearrange("p (a b) -> p a b", a=2)
                        p6 = PS[6][:, 0:256].rearrange("p (a b) -> p a b", a=2)
                        tt(v2(ta), p5, cs, ALU.mult, [PS[0], COS], [ta])
                        tt(v2(tb), p6, sn, ALU.mult, [PS[6], SIN], [tb])
                        tt(zr[:], ta[:], tb[:], ALU.add, [ta, tb], [zr])
                        tt(v2(ta), p6, cs, ALU.mult, [PS[6], COS], [ta])
                        tt(v2(tb), p5, sn, ALU.mult, [PS[0], SIN], [tb])
                        tt(zi[:], ta[:], tb[:], ALU.subtract, [ta, tb], [zi])

                    def sB(c2):
                        for q in range(2):
                            gp = c2 * 2 + q
                            sl = slice(q * 128, (q + 1) * 128)
                            k.op("dve", lambda e, gp=gp, sl=sl: e.tensor_tensor_scan(out=gr[:, sl], data0=RHO[:, gp, :], data1=zr[:, sl], initial=hpr[:, gp:gp + 1], op0=ALU.mult, op1=ALU.add), [RHO, zr, hpr], [gr])
                            k.op("dve", lambda e, gp=gp, sl=sl: e.tensor_tensor_scan(out=gi[:, sl], data0=RHO[:, gp, :], data1=zi[:, sl], initial=hpi[:, gp:gp + 1], op0=ALU.mult, op1=ALU.add), [RHO, zi, hpi], [gi])

                    def sC(c2):
                        ut = c2 // 2
                        cs = COS[:, c2 * 2:(c2 + 1) * 2, :]
                        sn = SIN[:, c2 * 2:(c2 + 1) * 2, :]
                        tt(v2(ta2), v2(gr), cs, ALU.mult, [gr, COS], [ta2], en="pool")
                        tt(v2(tb2), v2(gi), sn, ALU.mult, [gi, SIN], [tb2], en="pool")
                        tt(hr2[:], ta2[:], tb2[:], ALU.subtract, [ta2, tb2], [hr2])
                        tt(v2(ta2), v2(gr), sn, ALU.mult, [gr, SIN], [ta2], en="pool")
                        tt(v2(tb2), v2(gi), cs, ALU.mult, [gi, COS], [tb2], en="pool")
                        tt(hi2[:], ta2[:], tb2[:], ALU.add, [ta2, tb2], [hi2])
                        hq = c2 % 2
                        cp(hrb4[:, hq * 2:(hq + 1) * 2, :], v2(hr2), [hr2], [hrb4], en="act")
                        cp(hib4[:, hq * 2:(hq + 1) * 2, :], v2(hi2), [hi2], [hib4], en="act")
                        cp(hpr[:, c2 * 2:(c2 + 1) * 2], v2(hr2)[:, :, col], [hr2], [hpr])
                        cp(hpi[:, c2 * 2:(c2 + 1) * 2], v2(hi2)[:, :, col], [hi2], [hpi])

                    def sD(c2):
                        ut = c2 // 2
                        for q in range(4):
                            gp = ut * 4 + q
                            mm(PS[0][:, 256:384], Cwr[:, gp, :], hrb4[:, q, :], [Cwr, hrb4], [PS[0]], start=(q == 0), stop=False)
                            mm(PS[0][:, 256:384], Cwi[:, gp, :], hib4[:, q, :], [Cwi, hib4], [PS[0]], start=False, stop=(q == 3))
                        stt(y5[:, ut, :], uT[:, ut, :], s5d[:, ut:ut + 1], PS[0][:, 256:384], ALU.mult, ALU.add, [uT, s5d, PS[0]], [y5])
                        yv = y5[:, ut, :]
                        tt(ta2[:, 0:128], yv, yv, ALU.mult, [y5], [ta2])
                        ts(ta2[:, 0:128], ta2[:, 0:128], 0.044715, ALU.mult, [ta2], [ta2], s2=1.0, op1=ALU.add)
                        tt(ta2[:, 0:128], ta2[:, 0:128], yv, ALU.mult, [ta2, y5], [ta2])
                        act(tb2[:, 0:128], ta2[:, 0:128], AF.Tanh, [ta2], [tb2], scale=0.7978845608028654)
                        ts(tb2[:, 0:128], tb2[:, 0:128], 1.0, ALU.add, [tb2], [tb2], s2=0.5, op1=ALU.mult)
                        tt(yv, yv, tb2[:, 0:128], ALU.mult, [y5, tb2], [y5])
                        cp(y5b[:, ut, :], y5[:, ut, :], [y5], [y5b], en="act")

                    for it in range(8 + 2):
                        if 0 <= it - 2 < 8:
                            sC(it - 2)
                            yield
                            if (it - 2) % 2 == 1:
                                sD(it - 2)
                                yield
                        if 0 <= it - 1 < 8:
                            sB(it - 1)
                            yield
                        if it < 8:
                            sA(it)
                            yield

                active = [gen_s5(), gen_proj_gla()]
                while active:
                    for g_ in list(active):
                        try:
                            next(g_)
                        except StopIteration:
                            active.remove(g_)
                gbk = [PS[5], PS[6], PS[3], PS[4]]
                for ot in range(4):
                    for kt in range(4):
                        mm(gbk[ot][:, 0:128], wgl[:, kt, ot * 128:(ot + 1) * 128], y5b[:, kt, :], [wgl, y5b], [gbk[ot]], start=(kt == 0), stop=(kt == 3))
                for ot in range(4):
                    act(otr[ot][:], gbk[ot][:, 0:128], AF.Tanh, [gbk[ot], bglh], [otr[ot]], scale=0.5, bias=bglh[:, ot:ot + 1])
                for ot in range(4):
                    stt(otr[ot][:], otr[ot][:], 1.0, y5[:, ot, :], ALU.add, ALU.mult, [otr[ot], y5], [otr[ot]])
                for ot in range(4):
                    stt(catT[:, 8 + ot, :], otr[ot][:], 0.25, sgs[:, ot, :], ALU.mult, ALU.mult, [otr[ot], sgs], [catT])
                out_proj_add(12, xdst, xb, padded)

            for h_ in range(4):
                ms(Sh4[h_][:, :], 0.0, [Sh4[h_]]); ms(Sbfh4[h_][:, :], 0.0, [Sbfh4[h_]])
            ms(hpr[:], 0.0, [hpr]); ms(hpi[:], 0.0, [hpi])
            xin = D["xT"] if L == 0 else O["yT"]
            for j in range(nt):
                run_tile(xin[:, :, j * 128:(j + 1) * 128], O["yT"][:, :, j * 128:(j + 1) * 128], XB[j], False, None)
            for h in range(4):
                st(O[f"ogp{i}"][h], Sh4[h][:, :], [Sh4[h]])

            def out_state(dre, dim):
                tt(hs0[:], hpr[:], wr[:], ALU.mult, [hpr, wr], [hs0])
                tt(hs1[:], hpi[:], wi[:], ALU.mult, [hpi, wi], [hs1])
                tt(hs2[:], hs0[:], hs1[:], ALU.subtract, [hs0, hs1], [hs2])
                tt(hs0[:], hpr[:], wi[:], ALU.mult, [hpr, wi], [hs0])
                tt(hs1[:], hpi[:], wr[:], ALU.mult, [hpi, wr], [hs1])
                tt(hs3[:], hs0[:], hs1[:], ALU.add, [hs0, hs1], [hs3])
                st(dre, hs2[:], [hs2])
                st(dim, hs3[:], [hs3])
            out_state(O[f"o5rp{i}"], O[f"o5ip{i}"])
            def even_samples():
                k.barrier()
                W = slice(0, NS)
                v16 = lambda ap: ap.rearrange("p (a b) -> p a b", a=16)
                bc = lambda t_: t_[:, :].unsqueeze(2).to_broadcast([128, 16, NS])
                k.dma(lambda e: e.dma_start(out=xt[:, :, W], in_=xin[:, :, SEQ:SEQ + NS]), [XB[NT]], [xt])
                act(sq[:, :, W], xt[:, :, W], AF.Square, [xt], [sq])
                for kt in range(8):
                    mm(PS[0][:, W], onesb[:], sq[:, kt, W], [onesb, sq], [PS[0]], start=(kt == 0), stop=(kt == 7))
                act(lnv[:, W], PS[0][:, W], AF.Ln, [PS[0], epsc], [lnv], scale=1.0 / 1024.0, bias=epsc[:])
                act(rstd[:, W], lnv[:, W], AF.Exp, [lnv], [rstd], scale=-0.5)
                for kt in range(8):
                    stt(hn[:, kt, W], xt[:, kt, W], nw[:, L, kt:kt + 1], rstd[:, W], ALU.mult, ALU.mult, [xt, nw, rstd], [hn])

                sb_ = [PS[1], PS[3], PS[5], PS[6]]
                sc_ = {"n": 0}

                def proj_s(col0, evac):
                    P_ = sb_[sc_["n"] % 4]
                    sc_["n"] += 1
                    for kt in range(8):
                        mm(P_[:, W], WIN[:, kt, col0:col0 + 128], hn[:, kt, W], [WIN, hn], [P_], start=(kt == 0), stop=(kt == 7))
                    evac(P_[:, W], P_)
                for kt in range(8):
                    mm(PS[1][0:16, W], WIN[:, kt, 3072:3088], hn[:, kt, W], [WIN, hn], [PS[1]], start=(kt == 0), stop=(kt == 7))
                cp(lrT[:, W], PS[1][0:16, W], [PS[1]], [lrT])
                for h in range(4):
                    mm(PS[2][:, h * 16:(h + 1) * 16], wgu[:, h * 128:(h + 1) * 128], lrT[:, W], [wgu, lrT], [PS[2]])
                    act(E1[:, h, W], PS[2][:, h * 16:(h + 1) * 16], AF.Exp, [PS[2], nbgf], [E1], scale=-1.0, bias=nbgf[:, h:h + 1])
                for h in range(4):
                    act(E1[:, h, W], E1[:, h, W], AF.Ln, [E1, ones1], [E1], bias=ones1[:])
                    act(E1[:, h, W], E1[:, h, W], AF.Exp, [E1], [E1], scale=-1.0 / 16.0)
                for h in range(4):
                    proj_s(h * 128, lambda p, P_, h=h: ts(E2[:, h, W], p, 128.0 ** -0.5, ALU.mult, [P_], [E2]))
                    proj_s(512 + h * 128, lambda p, P_, h=h: cp(E3[:, h * 16:(h + 1) * 16], p, [P_], [E3]))
                for j in range(8):
                    proj_s(2048 + j * 128, lambda p, P_, j=j: (lambda o_: (act(o_[:, W], p, AF.Tanh, [P_], [o_], scale=0.5), stt(gT[:, j, W], o_[:, W], 1.0, p, ALU.add, ALU.mult, [o_, P_], [gT])))(nxt_ot()))
                    ts(gT[:, j, W], gT[:, j, W], gnw[:, (j % 2):(j % 2) + 1], ALU.mult, [gT, gnw], [gT], s2=0.5, op1=ALU.mult)
                for j in range(4):
                    def ev_u(p, P_, j=j):
                        cp(uT[:, j, W], p, [P_], [uT], en="act")
                        cp(uTb[:, j, W], uT[:, j, W], [uT], [uTb])
                    proj_s(3088 + j * 128, ev_u)
                    proj_s(3600 + j * 128, lambda p, P_, j=j: (lambda o_: (act(o_[:, W], p, AF.Tanh, [P_], [o_], scale=0.5), stt(sgs[:, j, W], o_[:, W], 1.0, p, ALU.add, ALU.mult, [o_, P_], [sgs])))(nxt_ot()))
                vs2 = RHO[0:NS, 0:8, :].rearrange("p a b -> p (a b)")
                for c in range(2):
                    for kt in range(8):
                        mm(PS[2][0:NS, :], hn[:, kt, W], WIN[:, kt, 1024 + c * 512:1024 + (c + 1) * 512], [hn, WIN], [PS[2]], start=(kt == 0), stop=(kt == 7))
                    cp(vs2[:, c * 512:(c + 1) * 512], PS[2][0:NS, :], [PS[2]], [RHO])
                prefetch_win(L + 1)
                Ss = [V(S.ap[:, j, :], f"Ss{j}") for j in range(4)]
                sels = [V(RHO.ap[0:NS, 8 + q_, :], f"sel{q_}") for q_ in range(4)]
                gbanks = [PS[0], PS[1], PS[2], PS[6]]
                zzs = [zr, zi, ta, tb]

                def gla_stream(q_):
                    h = q_
                    P_ = gbanks[q_]; zz = zzs[q_]; selq = sels[q_]; Sj = Ss[q_]
                    for s in range(NS):
                        ts(selq[:, :], C("ones")[0:NS, :], C("ident")[0:NS, s:s + 1], ALU.mult, [CT], [selq], en="pool")
                        ld(Sj[:, :], D[f"sgla{i}"][s, h], [Sj])
                        mm(P_[:, 0:256], selq[:, :], vs2[:, h * 256:(h + 1) * 256], [selq, RHO], [P_])
                        yield
                        ts(zz[:], P_[:, 0:256], E3[:, h * 16 + s:h * 16 + s + 1], ALU.mult, [P_, E3], [zz])
                        yield
                        stt(Sj[:, :], Sj[:, :], E1[:, h, s:s + 1], zz[:], ALU.mult, ALU.add, [Sj, E1, zz], [Sj])
                        yield
                        for half in range(2):
                            c_ = (h * 2 + half) * 16 + s
                            mm(PS[4][:, c_:c_ + 1], Sj[:, half * 128:(half + 1) * 128], E2[:, h, s:s + 1], [Sj, E2], [PS[4]])
                        st(O[f"ogs{i}"][s, h], Sj[:, :], [Sj])
                        yield
                active = [gla_stream(q_) for q_ in range(4)]
                while active:
                    for g_ in list(active):
                        try:
                            next(g_)
                        except StopIteration:
                            active.remove(g_)
                act(sqo[:, 0, :], PS[4][:, 0:128], AF.Square, [PS[4]], [sqo])
                for h in range(4):
                    mm(PS[3][:, h * 16:(h + 1) * 16], onesb[:], sqo[:, 0, (h * 2) * 16:(h * 2 + 1) * 16], [onesb, sqo], [PS[3]], start=True, stop=False)
                    mm(PS[3][:, h * 16:(h + 1) * 16], onesb[:], sqo[:, 0, (h * 2 + 1) * 16:(h * 2 + 2) * 16], [onesb, sqo], [PS[3]], start=False, stop=True)
                act(lnv[:, 0:64], PS[3][:, 0:64], AF.Ln, [PS[3], epsc], [lnv], scale=1.0 / 256.0, bias=epsc[:])
                act(rstd[:, 0:64], lnv[:, 0:64], AF.Exp, [lnv], [rstd], scale=-0.5)
                for h in range(4):
                    for half in range(2):
                        j = h * 2 + half
                        tt(otmp[:, W], PS[4][:, j * 16:(j + 1) * 16], rstd[:, h * 16:(h + 1) * 16], ALU.mult, [PS[4], rstd], [otmp])
                        tt(catT[:, j, W], otmp[:, W], gT[:, j, W], ALU.mult, [otmp, gT], [catT])
                ld(gr[:, :], D[f"s5r{i}"].rearrange("p a b -> p (a b)"), [gr])
                ld(gi[:, :], D[f"s5i{i}"].rearrange("p a b -> p (a b)"), [gi])
                for gp in range(16):
                    mm(PS[5][:, gp * 16:(gp + 1) * 16], Bre[:, gp, :], uTb[:, gp // 4, W], [Bre, uTb], [PS[5]])
                    mm(PS[6][:, gp * 16:(gp + 1) * 16], Bim[:, gp, :], uTb[:, gp // 4, W], [Bim, uTb], [PS[6]])

                def cmul(dr, di, xr, xi, cr, ci):
                    tt(v16(ta[:, :]), v16(xr[:, :]), bc(cr), ALU.mult, [xr, cr], [ta])
                    tt(v16(tb[:, :]), v16(xi[:, :]), bc(ci), ALU.mult, [xi, ci], [tb])
                    tt(dr[:, :], ta[:, :], tb[:, :], ALU.subtract, [ta, tb], [dr])
                    tt(v16(ta[:, :]), v16(xr[:, :]), bc(ci), ALU.mult, [xr, ci], [ta])
                    tt(v16(tb[:, :]), v16(xi[:, :]), bc(cr), ALU.mult, [xi, cr], [tb])
                    tt(di[:, :], ta[:, :], tb[:, :], ALU.add, [ta, tb], [di])
                cmul(zr, zi, gr, gi, iwr, iwi)
                cmul(gr, gi, zr, zi, ar, ai)
                tt(gr[:, :], gr[:, :], PS[5][:, 0:256], ALU.add, [gr, PS[5]], [gr])
                tt(gi[:, :], gi[:, :], PS[6][:, 0:256], ALU.add, [gi, PS[6]], [gi])
                hrv = hrb4[:, 0:2, :].rearrange("p a b -> p (a b)")
                hiv = hib4[:, 0:2, :].rearrange("p a b -> p (a b)")
                cp(hrv, gr[:, :], [gr], [hrb4], en="act")
                cp(hiv, gi[:, :], [gi], [hib4], en="act")
                cmul(zr, zi, gr, gi, wr, wi)
                st(O[f"o5rs{i}"].rearrange("p a b -> p (a b)"), zr[:, :], [zr])
                st(O[f"o5is{i}"].rearrange("p a b -> p (a b)"), zi[:, :], [zi])
                for ut in range(4):
                    for q in range(4):
                        gp = ut * 4 + q
                        mm(PS[3][:, 256 + ut * 16:256 + (ut + 1) * 16], Cwr[:, gp, :], hrv[:, gp * 16:(gp + 1) * 16], [Cwr, hrb4], [PS[3]], start=(q == 0), stop=False)
                        mm(PS[3][:, 256 + ut * 16:256 + (ut + 1) * 16], Cwi[:, gp, :], hiv[:, gp * 16:(gp + 1) * 16], [Cwi, hib4], [PS[3]], start=False, stop=(q == 3))
                    stt(y5[:, ut, W], uT[:, ut, W], s5d[:, ut:ut + 1], PS[3][:, 256 + ut * 16:256 + (ut + 1) * 16], ALU.mult, ALU.add, [uT, s5d, PS[3]], [y5])
                for ut in range(4):
                    yv = y5[:, ut, W]
                    tt(ta[:, W], yv, yv, ALU.mult, [y5], [ta])
                    ts(ta[:, W], ta[:, W], 0.044715, ALU.mult, [ta], [ta], s2=1.0, op1=ALU.add)
                    tt(ta[:, W], ta[:, W], yv, ALU.mult, [ta, y5], [ta])
                    act(tb[:, W], ta[:, W], AF.Tanh, [ta], [tb], scale=0.7978845608028654)
                    ts(tb[:, W], tb[:, W], 1.0, ALU.add, [tb], [tb], s2=0.5, op1=ALU.mult)
                    tt(yv, yv, tb[:, W], ALU.mult, [y5, tb], [y5])
                    cp(y5b[:, ut, W], y5[:, ut, W], [y5], [y5b], en="act")
                for ot in range(4):
                    for kt in range(4):
                        mm(PS[5][:, W], wgl[:, kt, ot * 128:(ot + 1) * 128], y5b[:, kt, W], [wgl, y5b], [PS[5]], start=(kt == 0), stop=(kt == 3))
                    act(otmp[:, W], PS[5][:, W], AF.Tanh, [PS[5], bglh], [otmp], scale=0.5, bias=bglh[:, ot:ot + 1])
                    stt(otmp[:, W], otmp[:, W], 1.0, y5[:, ot, W], ALU.add, ALU.mult, [otmp, y5], [otmp])
                    stt(catT[:, 8 + ot, W], otmp[:, W], 0.25, sgs[:, ot, W], ALU.mult, ALU.mult, [otmp, sgs], [catT])
                out_proj_s(12)
            if ns > 0:
                even_samples()


    def odd_layer(L, i):
        k.barrier()
        if pref["L"] != L:
            load_weights_bf16(WIN, D[f"wio{i}"], 8, 4112)
        load_weights_bf16(WOUT, D[f"woo{i}"], 8, 1024)
        if True:
            TT_ = make_alloc()
            cvw = TT_("cvw", [128, 24, 4]); ld(cvw[:], D[f"cvw{i}"], [cvw])
            alg = TT_("alg", [128, 8]); ld(alg[:], D[f"alg{i}"], [alg])
            dtb = TT_("dtb", [128, 8]); ld(dtb[:], D[f"dtb{i}"], [dtb])
            dnw = TT_("dnw", [128, 1]); ld(dnw[:], D[f"dnw{i}"], [dnw])
            ts(dnw[:], dnw[:], 0.5, ALU.mult, [dnw], [dnw])
            nea = TT_("nea", [128, 8])
            act(nea[:], alg[:], AF.Exp, [alg], [nea])
            ts(nea[:], nea[:], -1.0, ALU.mult, [nea], [nea])
            ones1 = TT_("ones1", [128, 1]); ms(ones1[:], 1.0, [ones1])
            S = TT_("S", [128, 8, 128]); Sbf = TT_("Sbf", [128, 8, 128], BF16)
            cbuf = TT_("cbuf", [128, 24, 131])
            acc = TT_("acc", [128, 24, 128]); acthr = TT_("acthr", [128, 1152])
            acth = V(acthr.ap[:, 0:1152].rearrange("p (a b) -> p a b", a=24), "acth"); acth.b = acthr.b
            acth8 = V(acthr.ap[:, 0:1024].rearrange("p (a b) -> p a b", a=8), "acth8"); acth8.b = acthr.b
            zs = TT_("zs", [128, 8, 128])
            ba = TT_("ba", [128, 16])
            beta = TT_("beta", [128, 8]); g = TT_("g", [128, 8]); gam = TT_("gam", [128, 8]); ghat = TT_("ghat", [128, 8])
            GA = TT_("GA", [128, 8]); GB = TT_("GB", [128, 8]); nGb = TT_("nGb", [128, 8]); nbeta = TT_("nbeta", [128, 8])
            sqk = TT_("sqk", [128, 128], BF16); rn = TT_("rn", [128, 128]); l1 = TT_("l1", [128, 128])
            qTb = TT_("qTb", [128, 128], BF16); kTb = TT_("kTb", [128, 128], BF16); vTb = TT_("vTb", [128, 128], BF16)
            ktk = TT_("ktk", [128, 128]); khat = TT_("khat", [128, 128], BF16); vtk = TT_("vtk", [128, 128]); Vb = TT_("Vb", [128, 128])
            gbc = TT_("gbc", [128, 128]); gbcn = TT_("gbcn", [128, 128])
            Dst = TT_("Dst", [128, 128]); DTi = TT_("DTi", [128, 128])
            M0 = TT_("M0", [128, 128]); N0 = TT_("N0", [128, 128]); M1 = TT_("M1", [128, 128]); N1 = TT_("N1", [128, 128])
            TTm = TT_("TTm", [128, 128]); TTb = TT_("TTb", [128, 128], BF16); PT = TT_("PT", [128, 128], BF16)
            Rb = TT_("Rb", [128, 128], BF16); Vn = TT_("Vn", [128, 128], BF16)
            qsg = TT_("qsg", [128, 128]); Osb = TT_("Osb", [128, 128]); Onb = TT_("Onb", [128, 128], BF16)
            ssq = TT_("ssq", [128, 1]); rs1 = TT_("rs1", [128, 1])
            sqr = [sqk] + [TT_(f"sqk{q_}", [128, 128], BF16) for q_ in range(1, 4)]
            l1r = [l1, rn, ktk, vtk]
            vTr = [vTb, TT_("vTb1", [128, 128], BF16)]
            Onr = [Onb, TT_("Onb1", [128, 128], BF16)]
            gbr = [(gbc, gbcn), (Dst, DTi)]
            qTa = TT_("qTa", [128, 8, 128], BF16); kTa = TT_("kTa", [128, 8, 128], BF16)
            kha = TT_("kha", [128, 8, 128], BF16); Vba = TT_("Vba", [128, 8, 128], BF16)
            Dsa = TT_("Dsa", [128, 8, 128], BF16); DTa = TT_("DTa", [128, 8, 128], BF16)
            qTB = [Buf(f"qTB{h_}") for h_ in range(8)]; kTB = [Buf(f"kTB{h_}") for h_ in range(8)]
            khB = [Buf(f"khB{h_}") for h_ in range(8)]; VbB = [Buf(f"VbB{h_}") for h_ in range(8)]
            DsB = [Buf(f"DsB{h_}") for h_ in range(8)]; DTB = [Buf(f"DTB{h_}") for h_ in range(8)]
            HS = [dict(M0=M0, N0=N0, M1=M1, N1=N1, TTm=TTm, qsg=qsg, Osb=Osb, TTb=TTb, PT=PT, Rb=Rb, Vn=Vn, ssq=ssq, rs1=rs1, P=PS[1])]
            for q_ in range(1, 4):
                d_ = {n_: TT_(f"{n_}_{q_}", [128, 128]) for n_ in ["M0", "N0", "M1", "N1", "TTm", "qsg", "Osb"]}
                d_.update({n_: TT_(f"{n_}_{q_}", [128, 128], BF16) for n_ in ["TTb", "PT", "Rb", "Vn"]})
                d_.update(ssq=TT_(f"ssq_{q_}", [128, 1]), rs1=TT_(f"rs1_{q_}", [128, 1]), P=PS[1 + q_])
                HS.append(d_)
            cbp = [Buf(f"cbp{c_}") for c_ in range(24)]
            acp = [Buf(f"acp{c_}") for c_ in range(24)]
            ctmp = TT_("ctmp", [128, 128])
            thb = [TT_(f"thb{q_}", [128, 128]) for q_ in range(4)]
            Sh = [V(S.ap[:, h_, :], f"Sh{h_}") for h_ in range(8)]
            Sbfh = [V(Sbf.ap[:, h_, :], f"Sbfh{h_}") for h_ in range(8)]

            def run_tile(xsrc, xdst, xb, padded, sidx):
                norm_tile(xsrc, xb, L, padded)
                if padded:
                    for h in range(8):
                        ld(S[:, h, :], D[f"sdel{i}"][sidx, h], [S])
                    cp(Sbf[:], S[:], [S], [Sbf], en="act")
                    ld(cbuf[:, :, 0:3], D[f"scv{i}"][sidx], [cbuf])
                def chain_gen():
                    for kt in range(8):
                        mm(PS[2][:, 0:16], hn[:, kt, :], WIN[:, kt, 4096:4112], [hn, WIN], [PS[2]], start=(kt == 0), stop=(kt == 7))
                    cp(ba[:], PS[2][:, 0:16], [PS[2]], [ba])
                    yield
                    act(beta[:], ba[:, 0:8], AF.Tanh, [ba], [beta], scale=0.5)
                    tt(g[:], ba[:, 8:16], dtb[:], ALU.add, [ba, dtb], [g])
                    yield
                    ts(beta[:], beta[:], 0.5, ALU.mult, [beta], [beta], s2=0.5, op1=ALU.add)
                    act(g[:], g[:], AF.Exp, [g], [g])
                    yield
                    act(g[:], g[:], AF.Ln, [g, ones1], [g], bias=ones1[:])
                    ts(nbeta[:], beta[:], -1.0, ALU.mult, [beta], [nbeta])
                    yield
                    tt(g[:], g[:], nea[:], ALU.mult, [g, nea], [g])
                    yield
                    mm(PS[2][:, 0:8], C("trib"), g[:], [CT, g], [PS[2]])
                    mm(PS[2][:, 8:16], C("c3b"), g[:], [CT, g], [PS[2]])
                    mm(PS[2][:, 16:24], C("sela"), g[:], [CT, g], [PS[2]])
                    mm(PS[2][:, 24:32], C("selb"), g[:], [CT, g], [PS[2]])
                    yield
                    act(gam[:], PS[2][:, 0:8], AF.Exp, [PS[2]], [gam])
                    act(ghat[:], PS[2][:, 8:16], AF.Exp, [PS[2]], [ghat])
                    act(GA[:], PS[2][:, 16:24], AF.Exp, [PS[2]], [GA])
                    act(GB[:], PS[2][:, 24:32], AF.Exp, [PS[2]], [GB])
                    yield
                    tt(nGb[:], nbeta[:], gam[:], ALU.mult, [nbeta, gam], [nGb])
                    yield
                chain = chain_gen()
                pb = [PS[1], PS[4], PS[5], PS[6]]

                def stA(ct):
                    P_ = pb[ct % 4]
                    for kt in range(8):
                        mm(P_[:, 0:128], WIN[:, kt, ct * 128:(ct + 1) * 128], hn[:, kt, :], [WIN, hn], [P_], start=(kt == 0), stop=(kt == 7))
                    cp(cbuf[:, ct, 3:131], P_[:, 0:128], [P_], [cbp[ct]], en="act")

                def stBg(gq):
                    cts = [gq * 4 + q_ for q_ in range(4)]
                    for ct in cts:
                        ts(acc[:, ct, :], cbuf[:, ct, 0:128], cvw[:, ct, 0:1], ALU.mult, [cbp[ct], cvw], [acp[ct]])
                    for w_ in range(1, 4):
                        for ct in cts:
                            stt(acc[:, ct, :], cbuf[:, ct, w_:w_ + 128], cvw[:, ct, w_:w_ + 1], acc[:, ct, :], ALU.mult, ALU.add, [cbp[ct], cvw, acp[ct]], [acp[ct]])

                def stCg(gq):
                    for q_ in range(4):
                        ct = gq * 4 + q_
                        act(thb[q_][:], acc[:, ct, :], AF.Tanh, [acp[ct]], [thb[q_]], scale=0.5)

                def stDg(gq):
                    for q_ in range(4):
                        e_ = "pool" if q_ % 2 == 1 else "dve"
                        ts(thb[q_][:], thb[q_][:], 1.0, ALU.add, [thb[q_]], [thb[q_]], s2=0.5, op1=ALU.mult, en=e_)
                    for q_ in range(4):
                        ct = gq * 4 + q_
                        e_ = "pool" if q_ % 2 == 1 else "dve"
                        tt(acc[:, ct, :], acc[:, ct, :], thb[q_][:], ALU.mult, [acp[ct], thb[q_]], [acp[ct]], en=e_)

                for gq in range(6 + 3):
                    next(chain, None)
                    if gq < 6:
                        for q_ in range(4):
                            stA(gq * 4 + q_)
                    if 0 <= gq - 1 < 6:
                        stBg(gq - 1)
                    if 0 <= gq - 3 < 6:
                        stDg(gq - 3)
                    if 0 <= gq - 2 < 6:
                        stCg(gq - 2)
                for j in range(8):
                    P_ = pb[j % 4]
                    for kt in range(8):
                        mm(P_[:, 0:128], WIN[:, kt, 3072 + j * 128:3072 + (j + 1) * 128], hn[:, kt, :], [WIN, hn], [P_], start=(kt == 0), stop=(kt == 7))
                    th_ = thb[j % 2]
                    act(th_[:], P_[:, 0:128], AF.Tanh, [P_], [th_], scale=0.5)
                    stt(zs[:, j, :], th_[:], 1.0, P_[:, 0:128], ALU.add, ALU.mult, [th_, P_], [zs])
                for _ in chain:
                    pass
                if padded:
                    st(O[f"ocs{i}"][sidx], cbuf[:, :, 1:4], [cbuf])
                else:
                    cp(cbuf[:, :, 0:3], cbuf[:, :, 128:131], cbp, cbp)
                l2tiles = [(h_, 0) for h_ in range(8)] + [(h_, 1) for h_ in range(8)]
                l2banks = [PS[1], PS[2]]

                def l2A(t_):
                    h_, isk = l2tiles[t_]
                    ct = 8 * isk + h_
                    sq_ = sqr[t_ % 4]
                    act(sq_[:], acc[:, ct, :], AF.Square, [acp[ct]], [sq_])
                    P_ = l2banks[t_ % 2]
                    r_ = ((t_ // 2) % 4) * 128
                    mm(P_[:, r_:r_ + 128], onesb[:], sq_[:], [onesb, sq_], [P_])

                def l2B(t_):
                    P_ = l2banks[t_ % 2]
                    r_ = ((t_ // 2) % 4) * 128
                    l_ = l1r[t_ % 4]
                    act(l_[:], P_[:, r_:r_ + 128], AF.Ln, [P_, epsc], [l_], bias=epsc[:])

                def l2C(t_):
                    l_ = l1r[t_ % 4]
                    act(l_[:], l_[:], AF.Exp, [l_], [l_], scale=-0.5)

                def l2D(t_):
                    h_, isk = l2tiles[t_]
                    ct = 8 * isk + h_
                    l_ = l1r[t_ % 4]
                    if isk:
                        stt(kTa[:, h_, :], acc[:, ct, :], 1.0, l_[:], ALU.mult, ALU.mult, [acp[ct], l_], [kTB[h_]])
                    else:
                        stt(qTa[:, h_, :], acc[:, ct, :], 128.0 ** -0.5, l_[:], ALU.mult, ALU.mult, [acp[ct], l_], [qTB[h_]])

                for it in range(16 + 3):
                    if it < 16:
                        l2A(it)
                    if 0 <= it - 3 < 16:
                        l2D(it - 3)
                    if 0 <= it - 2 < 16:
                        l2C(it - 2)
                    if 0 <= it - 1 < 16:
                        l2B(it - 1)
                dbanks = [PS[3], PS[4], PS[5], PS[6]]

                def trA(h_):
                    v_ = vTr[h_ % 2]
                    cp(v_[:], acc[:, 16 + h_, :], [acp[16 + h_]], [v_], en="pool")
                    r_ = (h_ % 4) * 256
                    tr(PSB[:, r_:r_ + 128], kTa[:, h_, :], identb[:], [kTB[h_], identb], [PSB])
                    tr(PSB[:, r_ + 128:r_ + 256], v_[:], identb[:], [v_, identb], [PSB])

                def trB(h_):
                    r_ = (h_ % 4) * 256
                    ts(kha[:, h_, :], PSB[:, r_:r_ + 128], ghat[:, h_:h_ + 1], ALU.mult, [PSB, ghat], [khB[h_]])
                    ts(Vba[:, h_, :], PSB[:, r_ + 128:r_ + 256], beta[:, h_:h_ + 1], ALU.mult, [PSB, beta], [VbB[h_]])

                def dcA(h_):
                    gb_, gn_ = gbr[h_ % 2]
                    ts(gb_[:], C("trib"), g[:, h_:h_ + 1], ALU.mult, [CT, g], [gb_], en="pool")
                    ts(gn_[:], gb_[:], -1.0, ALU.mult, [gb_], [gn_], en="pool")

                def dcB(h_):
                    gb_, gn_ = gbr[h_ % 2]
                    P_ = dbanks[h_ % 4]
                    mm(P_[:, 0:128], gb_[:], C("ones"), [gb_, CT], [P_], start=True, stop=False)
                    mm(P_[:, 0:128], C("nones"), gb_[:], [CT, gb_], [P_], start=False, stop=False)
                    mm(P_[:, 0:128], C("ident"), C("negs"), [CT], [P_], start=False, stop=True)
                    mm(P_[:, 128:256], C("ones"), gb_[:], [gb_, CT], [P_], start=True, stop=False)
                    mm(P_[:, 128:256], gn_[:], C("ones"), [CT, gn_], [P_], start=False, stop=False)
                    mm(P_[:, 128:256], C("ident"), C("negi"), [CT], [P_], start=False, stop=True)

                def dcC(h_):
                    P_ = dbanks[h_ % 4]
                    act(Dsa[:, h_, :], P_[:, 0:128], AF.Exp, [P_], [DsB[h_]])
                    act(DTa[:, h_, :], P_[:, 128:256], AF.Exp, [P_], [DTB[h_]])

                for it in range(8 + 2):
                    if it < 8:
                        trA(it)
                        dcA(it)
                    if 0 <= it - 1 < 8:
                        dcB(it - 1)
                        trB(it - 1)
                    if 0 <= it - 2 < 8:
                        dcC(it - 2)

                def head_gen(h, B):
                    M0, N0, M1, N1, TTm, qsg, Osb = (B[n_] for n_ in ["M0", "N0", "M1", "N1", "TTm", "qsg", "Osb"])
                    TTb, PT, Rb, Vn = (B[n_] for n_ in ["TTb", "PT", "Rb", "Vn"])
                    ssq, rs1, P_ = B["ssq"], B["rs1"], B["P"]
                    S_h, Sbf_h = Sh[h], Sbfh[h]
                    kT_, qT_ = kTa[:, h, :], qTa[:, h, :]
                    R0, R1, R2, R3 = (P_[:, 0:128], P_[:, 128:256], P_[:, 256:384], P_[:, 384:512])
                    mm(R2, kT_, kT_, [kTB[h]], [P_])
                    stt(M0[:], R2, nbeta[:, h:h + 1], Dsa[:, h, :], ALU.mult, ALU.mult, [P_, nbeta, DsB[h]], [M0])
                    yield
                    tr(R3, M0[:], C("ident"), [M0, CT], [P_])
                    cp(N0[:], R3, [P_], [N0], en="act")
                    yield
                    tt(TTm[:], N0[:], C("ident"), ALU.add, [N0, CT], [TTm])
                    bufs_ = [(M0, N0), (M1, N1)]
                    mm(R0, N0[:], M0[:], [N0, M0], [P_])
                    mm(R1, M0[:], N0[:], [M0, N0], [P_])
                    cp(M1[:], R0, [P_], [M1], en="act")
                    cp(N1[:], R1, [P_], [N1], en="act")
                    yield
                    for lvl in range(5):
                        Mn, Nn = bufs_[(lvl + 1) % 2]
                        Mo, No = bufs_[lvl % 2]
                        if lvl < 4:
                            mm(R0, Nn[:], Mn[:], [Nn, Mn], [P_])
                            mm(R1, Mn[:], Nn[:], [Mn, Nn], [P_])
                        mm(R2, Mn[:], TTm[:], [Mn, TTm], [P_])
                        if lvl < 4:
                            cp(Mo[:], R0, [P_], [Mo], en="act")
                            cp(No[:], R1, [P_], [No], en="act")
                        tt(TTm[:], TTm[:], R2, ALU.add, [TTm, P_], [TTm])
                        yield
                    cp(TTb[:], TTm[:], [TTm], [TTb], en="act")
                    mm(R3, kT_, qT_, [kTB[h], qTB[h]], [P_])
                    tt(PT[:], R3, DTa[:, h, :], ALU.mult, [P_, DTB[h]], [PT])
                    ms(Rb[:], 0.0, [Rb], en="pool"); ms(Vn[:], 0.0, [Vn], en="pool")
                    yield
                    for blk in range(2):
                        rs = slice(blk * 64, (blk + 1) * 64)
                        Gc = GA if blk == 0 else GB
                        mm(R0, kT_, Sbf_h[:, :], [kTB[h], Sbf_h], [P_])
                        mm(R2, qT_, Sbf_h[:, :], [qTB[h], Sbf_h], [P_])
                        stt(Rb[rs, :], P_[rs, 0:128], nGb[rs, h:h + 1], Vba[rs, h, :], ALU.mult, ALU.add, [P_, nGb, VbB[h]], [Rb])
                        k.op("act", lambda e, rs=rs, h=h: e.activation(out=qsg[rs, :], in_=P_[rs, 256:384], func=AF.Copy, scale=gam[rs, h:h + 1]), [P_, gam], [qsg])
                        yield
                        mm(R1, TTb[:], Rb[:], [TTb, Rb], [P_])
                        cp(Vn[rs, :], P_[rs, 128:256], [P_], [Vn])
                        yield
                        mm(R3, PT[:], Vn[:], [PT, Vn], [P_])
                        mm(R0, kha[rs, h, :], Vn[rs, :], [khB[h], Vn], [P_])
                        tt(Osb[rs, :], qsg[rs, :], P_[rs, 384:512], ALU.add, [qsg, P_], [Osb])
                        stt(S_h[:, :], S_h[:, :], Gc[:, h:h + 1], R0, ALU.mult, ALU.add, [S_h, Gc, P_], [S_h])
                        yield
                        cp(Sbf_h[:, :], S_h[:, :], [S_h], [Sbf_h], en="act")
                        yield
                    act(qsg[:], Osb[:], AF.Square, [Osb], [qsg, ssq], accum=ssq[:])
                    yield
                    act(rs1[:], ssq[:], AF.Ln, [ssq, epsc], [rs1], scale=1.0 / 128.0, bias=epsc[:])
                    act(rs1[:], rs1[:], AF.Exp, [rs1], [rs1], scale=-0.5)
                    yield
                    On_ = Onr[h % 2]
                    ts(On_[:], Osb[:], rs1[:, 0:1], ALU.mult, [Osb, rs1], [On_])
                    r_ = (h % 4) * 256
                    tr(PSB[:, r_:r_ + 128], On_[:], identb[:], [On_, identb], [PSB])
                    stt(catT[:, h, :], PSB[:, r_:r_ + 128], dnw[:, 0:1], zs[:, h, :], ALU.mult, ALU.mult, [PSB, dnw, zs], [catT])
                    yield

                for hq in range(0, 8, 4):
                    active = [head_gen(hq + q_, HS[q_]) for q_ in range(4)]
                    while active:
                        for g_ in list(active):
                            try:
                                next(g_)
                            except StopIteration:
                                active.remove(g_)
                out_proj_add(8, xdst, xb, padded)

            for h_ in range(8):
                ms(Sh[h_][:, :], 0.0, [Sh[h_]]); ms(Sbfh[h_][:, :], 0.0, [Sbfh[h_]])
            ms(cbuf[:], 0.0, cbp)
            for j in range(nt):
                run_tile(O["yT"][:, :, j * 128:(j + 1) * 128], O["yT"][:, :, j * 128:(j + 1) * 128], XB[j], False, None)
            for h in range(8):
                st(O[f"odp{i}"][h], Sh[h][:, :], [Sh[h]])
            st(O[f"ocp{i}"], cbuf[:, :, 0:3], cbp)
            def odd_samples():
                k.barrier()
                W = slice(0, NS)
                k.dma(lambda e: e.dma_start(out=xt[:, :, W], in_=O["yT"][:, :, SEQ:SEQ + NS]), [XB[NT]], [xt])
                act(sq[:, :, W], xt[:, :, W], AF.Square, [xt], [sq])
                for kt in range(8):
                    mm(PS[0][:, W], onesb[:], sq[:, kt, W], [onesb, sq], [PS[0]], start=(kt == 0), stop=(kt == 7))
                act(lnv[:, W], PS[0][:, W], AF.Ln, [PS[0], epsc], [lnv], scale=1.0 / 1024.0, bias=epsc[:])
                act(rstd[:, W], lnv[:, W], AF.Exp, [lnv], [rstd], scale=-0.5)
                for kt in range(8):
                    stt(hn[:, kt, W], xt[:, kt, W], nw[:, L, kt:kt + 1], rstd[:, W], ALU.mult, ALU.mult, [xt, nw, rstd], [hn])
                sb_ = [PS[1], PS[3], PS[5], PS[6]]
                for ct in range(24):
                    P_ = sb_[ct % 4]
                    for kt in range(8):
                        mm(P_[:, W], WIN[:, kt, ct * 128:(ct + 1) * 128], hn[:, kt, W], [WIN, hn], [P_], start=(kt == 0), stop=(kt == 7))
                    cp(acc[:, ct, W], P_[:, W], [P_], [acc], en="act")
                for j in range(8):
                    P_ = sb_[j % 4]
                    t_ = thb[j % 4]
                    for kt in range(8):
                        mm(P_[:, W], WIN[:, kt, 3072 + j * 128:3072 + (j + 1) * 128], hn[:, kt, W], [WIN, hn], [P_], start=(kt == 0), stop=(kt == 7))
                    act(t_[:, W], P_[:, W], AF.Tanh, [P_], [t_], scale=0.5)
                    stt(zs[:, j, W], t_[:, W], 1.0, P_[:, W], ALU.add, ALU.mult, [t_, P_], [zs])
                for kt in range(8):
                    mm(PS[2][0:NS, 0:16], hn[:, kt, W], WIN[:, kt, 4096:4112], [hn, WIN], [PS[2]], start=(kt == 0), stop=(kt == 7))
                cp(ba[0:NS, :], PS[2][0:NS, 0:16], [PS[2]], [ba])
                prefetch_win(L + 1)
                k.dma(lambda e: e.dma_start(out=cbuf[:, :, 0:48], in_=D[f"scv{i}"].rearrange("p c i s -> p c (i s)")), (), [cbuf])
                wb = lambda i_: cvw[:, :, i_:i_ + 1].to_broadcast([128, 24, NS])
                tt(acth[:, :, W], cbuf[:, :, 0:16], wb(0), ALU.mult, [cbuf, cvw], [acth])
                for i_ in (1, 2):
                    tt(acth[:, :, 16:32], cbuf[:, :, 16 * i_:16 * i_ + 16], wb(i_), ALU.mult, [cbuf, cvw], [acth])
                    tt(acth[:, :, W], acth[:, :, W], acth[:, :, 16:32], ALU.add, [acth], [acth])
                tt(acth[:, :, 16:32], acc[:, :, W], wb(3), ALU.mult, [acc, cvw], [acth])
                tt(acth[:, :, W], acth[:, :, W], acth[:, :, 16:32], ALU.add, [acth], [acth])
                k.dma(lambda e: e.dma_start(out=O[f"ocs{i}"].rearrange("p c i s -> p c (i s)")[:, :, 0:32], in_=cbuf[:, :, 16:48]), [cbuf], (), out=True)
                k.dma(lambda e: e.dma_start(out=O[f"ocs{i}"][:, :, 2, :], in_=acc[:, :, W]), [acc], (), out=True)
                act(acth[:, :, 32:48], acth[:, :, W], AF.Tanh, [acth], [acth], scale=0.5)
                ts(acth[:, :, 32:48], acth[:, :, 32:48], 0.5, ALU.mult, [acth], [acth], s2=0.5, op1=ALU.add)
                tt(acth[:, :, W], acth[:, :, W], acth[:, :, 32:48], ALU.mult, [acth], [acth])
                v8 = lambda ap: ap.rearrange("p (a b) -> p a b", a=8)
                qn = v8(qsg[:, :]); kn = v8(Osb[:, :])
                for (c0, dstv, dstT, scl) in ((0, qn, qsg, 128.0 ** -0.5), (8, kn, Osb, 1.0)):
                    act(v8(sqk[:, :]), acth[:, c0:c0 + 8, W], AF.Square, [acth], [sqk])
                    mm(PS[3][:, 0:128], onesb[:], sqk[:, :], [onesb, sqk], [PS[3]])
                    act(l1[:, :], PS[3][:, 0:128], AF.Ln, [PS[3], epsc], [l1], bias=epsc[:])
                    act(rn[:, :], l1[:, :], AF.Exp, [l1], [rn], scale=-0.5)
                    stt(dstv, acth[:, c0:c0 + 8, W], scl, v8(rn[:, :]), ALU.mult, ALU.mult, [acth, rn], [dstT])
                P16 = slice(0, NS)
                act(beta[P16, :], ba[P16, 0:8], AF.Tanh, [ba], [beta], scale=0.5)
                ts(beta[P16, :], beta[P16, :], 0.5, ALU.mult, [beta], [beta], s2=0.5, op1=ALU.add)
                tt(g[P16, :], ba[P16, 8:16], dtb[P16, :], ALU.add, [ba, dtb], [g])
                act(g[P16, :], g[P16, :], AF.Exp, [g], [g])
                act(g[P16, :], g[P16, :], AF.Ln, [g, ones1], [g], bias=ones1[P16, :])
                tt(g[P16, :], g[P16, :], nea[P16, :], ALU.mult, [g, nea], [g])
                act(gam[P16, :], g[P16, :], AF.Exp, [g], [gam])
                eye3 = C("ident")[P16, 0:NS].unsqueeze(1).to_broadcast([NS, 8, NS])
                for (src, dstb) in ((gam, gbc), (beta, gbcn)):
                    tt(v8(Dst[P16, :]), src[P16, :].unsqueeze(2).to_broadcast([NS, 8, NS]), eye3, ALU.mult, [src, CT], [Dst])
                    mm(PS[2][:, 128:256], C("ones")[P16, :], Dst[P16, :], [CT, Dst], [PS[2]])
                    cp(dstb[:, :], PS[2][:, 128:256], [PS[2]], [dstb])
                ts(DTi[:, :], gbc[:, :], -1.0, ALU.mult, [gbc], [DTi])
                Ss = [V(S.ap[:, j, :], f"Sd{j}") for j in range(8)]
                sbanks = [PS[0], PS[1], PS[2], PS[3], PS[5], PS[6]]
                NSTR = 6
                tt(Dst[:, :], DTi[:, :], gbcn[:, :], ALU.mult, [DTi, gbcn], [Dst])
                tt(v8(TTm[:, :]), acth[:, 16:24, W], v8(gbcn[:, :]), ALU.mult, [acth, gbcn], [TTm])
                vbs = [HS[q_ % 4]["M0" if q_ < 4 else "M1"] for q_ in range(NSTR)]
                t2s = [HS[q_ % 4]["N0" if q_ < 4 else "N1"] for q_ in range(NSTR)]
                vcs = [HS[q_ % 4]["ssq" if q_ < 4 else "rs1"] for q_ in range(NSTR)]

                def dn_stream(q_):
                    P_ = sbanks[q_]; vb_ = vbs[q_]; t2_ = t2s[q_]; vc = vcs[q_]
                    Sj = Ss[q_]
                    for idx in range(q_, NS * 8, NSTR):
                        s, h = idx // 8, idx % 8
                        c_ = h * 16 + s
                        ld(Sj[:, :], D[f"sdel{i}"][s, h], [Sj])
                        mm(P_[:, 0:1], Sj[:, :], kn[:, h, s:s + 1], [Sj, Osb], [P_])
                        yield
                        stt(vc[:, 0:1], P_[:, 0:1], Dst[:, c_:c_ + 1], TTm[:, c_:c_ + 1], ALU.mult, ALU.add, [P_, Dst, TTm], [vc])
                        yield
                        ts(vb_[:, :], C("ones"), vc[:, 0:1], ALU.mult, [CT, vc], [vb_], en="pool")
                        mm(P_[:, 128:256], vb_[:, :], C("ident"), [vb_, CT], [P_])
                        yield
                        ts(t2_[:, :], P_[:, 128:256], kn[:, h, s:s + 1], ALU.mult, [P_, Osb], [t2_])
                        yield
                        stt(Sj[:, :], Sj[:, :], gbc[:, c_:c_ + 1], t2_[:, :], ALU.mult, ALU.add, [Sj, gbc, t2_], [Sj])
                        yield
                        mm(PS[4][:, c_:c_ + 1], Sj[:, :], qn[:, h, s:s + 1], [Sj, qsg], [PS[4]])
                        st(O[f"ods{i}"][s, h], Sj[:, :], [Sj])
                        yield
                active = [dn_stream(q_) for q_ in range(NSTR)]
                while active:
                    for g_ in list(active):
                        try:
                            next(g_)
                        except StopIteration:
                            active.remove(g_)
                act(sqk[:, :], PS[4][:, 0:128], AF.Square, [PS[4]], [sqk])
                mm(PS[3][:, 0:128], onesb[:], sqk[:, :], [onesb, sqk], [PS[3]])
                act(l1[:, :], PS[3][:, 0:128], AF.Ln, [PS[3], epsc], [l1], scale=1.0 / 128.0, bias=epsc[:])
                act(rn[:, :], l1[:, :], AF.Exp, [l1], [rn], scale=-0.5)
                tt(l1[:, :], PS[4][:, 0:128], rn[:, :], ALU.mult, [PS[4], rn], [l1])
                for h in range(8):
                    stt(catT[:, h, W], l1[:, h * 16:(h + 1) * 16], dnw[:, 0:1], zs[:, h, W], ALU.mult, ALU.mult, [l1, dnw, zs], [catT])
                out_proj_s(8)
            if ns > 0:
                odd_samples()


    yb = Buf("yT_dram")
    for L in range(nlayers):
        if L % 2 == 0:
            even_layer(L, L // 2)
        else:
            odd_layer(L, L // 2)
    if final:
        k.barrier()
        NBF = 4
        fx = [V(ARF.t[:, j_ * 1024:(j_ + 1) * 1024].rearrange("p (a b) -> p a b", a=8), f"fx{j_}") for j_ in range(NBF)]
        fsq = [V(ARB.t[:, j_ * 1024:(j_ + 1) * 1024].rearrange("p (a b) -> p a b", a=8), f"fsq{j_}") for j_ in range(NBF)]
        fln = [V(ARF.t[:, 4096 + j_ * 256:4096 + j_ * 256 + 128], f"fln{j_}") for j_ in range(NBF)]
        frs = [V(ARF.t[:, 4096 + j_ * 256 + 128:4096 + (j_ + 1) * 256], f"frs{j_}") for j_ in range(NBF)]
        tiles_f = [(O["yT"][:, :, j * 128:(j + 1) * 128], 128, XB[j]) for j in range(nt)]
        if ns > 0:
            tiles_f.append((O["yT"][:, :, SEQ:SEQ + NS], NS, XB[NT]))
        nf = len(tiles_f)

        def f_load(t_):
            ap, w_, xb_ = tiles_f[t_]
            X_ = fx[t_ % NBF]
            dst_ = X_[:, :, 0:w_]
            k.dma(lambda e, dst_=dst_, ap=ap: e.dma_start(out=dst_, in_=ap), [xb_], [X_])

        def f_comp(t_):
            ap, w_, xb_ = tiles_f[t_]
            q_ = t_ % NBF
            X_, Q_, L_, R_, P_ = fx[q_], fsq[q_], fln[q_], frs[q_], PS[q_]
            act(Q_[:, :, 0:w_], X_[:, :, 0:w_], AF.Square, [X_], [Q_])
            for kt in range(8):
                mm(P_[:, 0:w_], onesb[:], Q_[:, kt, 0:w_], [onesb, Q_], [P_], start=(kt == 0), stop=(kt == 7))
            act(L_[:, 0:w_], P_[:, 0:w_], AF.Ln, [P_, epsc], [L_], scale=1.0 / 1024.0, bias=epsc[:])
            act(R_[:, 0:w_], L_[:, 0:w_], AF.Exp, [L_], [R_], scale=-0.5)
            for kt in range(8):
                stt(X_[:, kt, 0:w_], X_[:, kt, 0:w_], fnw[:, kt:kt + 1], R_[:, 0:w_], ALU.mult, ALU.mult, [X_, fnw, R_], [X_])

        def f_store(t_):
            ap, w_, xb_ = tiles_f[t_]
            X_ = fx[t_ % NBF]
            src_ = X_[:, :, 0:w_]
            k.dma(lambda e, src_=src_, ap=ap: e.dma_start(out=ap, in_=src_), [X_], [xb_], out=True)

        for it in range(nf + 2):
            if it < nf:
                f_load(it)
            if 0 <= it - 1 < nf:
                f_comp(it - 1)
            if 0 <= it - 2 < nf:
                f_store(it - 2)
    k.finish()
    return nc


def _tile_rows(w, nk):
    return np.ascontiguousarray(w.reshape(nk, 128, -1).transpose(1, 0, 2))


def prepare_inputs(inp):
    f = lambda a: np.ascontiguousarray(np.asarray(a, dtype=np.float32))
    x_prompt = f(inp["x_prompt"]); x_sample = f(inp["x_sample"])
    consts = make_consts()
    nw = np.ascontiguousarray(f(inp["norm_w"]).reshape(4, 8, 128).transpose(2, 0, 1))
    fnw = np.ascontiguousarray(f(inp["final_norm_w"]).reshape(8, 128).T)
    smask = np.zeros((128, 1), np.float32); smask[0, 0] = 1.0
    shared = {"consts": consts, "nw": nw, "fnw": fnw, "smask": smask}
    for i in range(2):
        shared[f"wie{i}"] = _tile_rows(f(inp["w_in_even"][i]), 8)
        shared[f"woe{i}"] = _tile_rows(f(inp["w_out_even"][i]), 12)
        shared[f"wgl{i}"] = _tile_rows(f(inp["s5_w_glu"][i]), 4)
        shared[f"wgu{i}"] = f(inp["gla_w_gate_up"][i])
        shared[f"bgt{i}"] = f(inp["gla_b_gate"][i]).reshape(1, 512)
        shared[f"gnw{i}"] = np.ascontiguousarray(f(inp["gla_norm_w"][i]).reshape(2, 128).T)
        shared[f"bgf{i}"] = np.ascontiguousarray(f(inp["gla_b_gate"][i]).reshape(4, 128).T)
        chm = lambda a: np.ascontiguousarray(a.reshape(16, 2, 64).transpose(1, 2, 0).reshape(128, 16))
        shared[f"lre{i}"] = chm(f(inp["s5_lambda_re"][i]))
        shared[f"lim{i}"] = chm(f(inp["s5_lambda_im"][i]))
        shared[f"ldt{i}"] = chm(np.repeat(f(inp["s5_log_dt"][i])[:, None], 64, axis=1))
        for nm, src in (("bre", "s5_b_re"), ("bim", "s5_b_im")):
            b = f(inp[src][i])
            out = np.zeros((128, 16, 128), np.float32)
            for g in range(32):
                gp, g2 = g // 2, g % 2
                out[(g % 8) * 16:(g % 8) * 16 + 16, gp, g2 * 64:(g2 + 1) * 64] = b[g].T
            shared[f"{nm}{i}"] = out
        for nm, src in (("cre", "s5_c_re"), ("cim", "s5_c_im")):
            c_ = f(inp[src][i])
            out = np.zeros((128, 16, 128), np.float32)
            for g in range(32):
                gp, g2 = g // 2, g % 2
                out[g2 * 64:(g2 + 1) * 64, gp, (g % 8) * 16:(g % 8) * 16 + 16] = c_[g].T
            shared[f"{nm}{i}"] = out
        shared[f"s5d{i}"] = np.ascontiguousarray(f(inp["s5_d"][i]).reshape(4, 128).T)
        shared[f"bgl{i}"] = np.ascontiguousarray(f(inp["s5_b_glu"][i]).reshape(4, 128).T)
        shared[f"wio{i}"] = _tile_rows(f(inp["w_in_odd"][i]), 8)
        shared[f"woo{i}"] = _tile_rows(f(inp["w_out_odd"][i]), 8)
        shared[f"cvw{i}"] = np.ascontiguousarray(f(inp["dn_conv_w"][i]).reshape(4, 24, 128).transpose(2, 1, 0))
        shared[f"alg{i}"] = np.ascontiguousarray(np.broadcast_to(f(inp["dn_a_log"][i])[None, :], (128, 8)))
        shared[f"dtb{i}"] = np.ascontiguousarray(np.broadcast_to(f(inp["dn_dt_bias"][i])[None, :], (128, 8)))
        shared[f"dnw{i}"] = f(inp["dn_norm_w"][i]).reshape(128, 1)
    in_maps = []
    for c in range(NCORES):
        m = dict(shared)
        xs = np.concatenate([x_prompt[c], x_sample[c * NS:(c + 1) * NS, 0]], axis=0)
        m["xT"] = np.ascontiguousarray(xs.reshape(SEQ + NS, 8, 128).transpose(2, 1, 0))
        sl = slice(c * NS, (c + 1) * NS)
        for i in range(2):
            m[f"sgla{i}"] = f(inp["state_gla"][i][sl])
            for nm, src in (("s5r", "state_s5_re"), ("s5i", "state_s5_im")):
                a = f(inp[src][i][sl])
                m[f"{nm}{i}"] = np.ascontiguousarray(a.reshape(NS, 16, 2, 64).transpose(2, 3, 1, 0).reshape(128, 16, NS))
            m[f"sdel{i}"] = f(inp["state_delta"][i][sl])
            a = f(inp["state_conv"][i][sl])
            m[f"scv{i}"] = np.ascontiguousarray(a.reshape(NS, 3, 24, 128).transpose(3, 2, 1, 0))
        in_maps.append(m)
    return in_maps


def kernel(**inp):
    in_maps = prepare_inputs(inp)
    nc = build_program()
    res = run_bass_kernel_spmd(nc, in_maps, core_ids=list(range(NCORES)))
    return assemble(res.results)


def assemble(R):
    y_prompt = np.zeros((8, SEQ, 1024), np.float32); y_sample = np.zeros((128, 1, 1024), np.float32)
    gla_p = np.zeros((2, 8, 4, 128, 256), np.float32); gla_s = np.zeros((2, 128, 4, 128, 256), np.float32)
    s5r_p = np.zeros((2, 8, 32, 64), np.float32); s5i_p = np.zeros_like(s5r_p)
    s5r_s = np.zeros((2, 128, 32, 64), np.float32); s5i_s = np.zeros_like(s5r_s)
    dn_p = np.zeros((2, 8, 8, 128, 128), np.float32); dn_s = np.zeros((2, 128, 8, 128, 128), np.float32)
    cv_p = np.zeros((2, 8, 3, 3072), np.float32); cv_s = np.zeros((2, 128, 3, 3072), np.float32)
    unch = lambda a: a.reshape(2, 64, 16).transpose(2, 0, 1).reshape(32, 64)
    for c in range(NCORES):
        r = R[c]
        yt = r["yT"].transpose(2, 1, 0).reshape(SEQ + NS, 1024)
        y_prompt[c] = yt[:SEQ]; y_sample[c * NS:(c + 1) * NS, 0] = yt[SEQ:]
        sl = slice(c * NS, (c + 1) * NS)
        for i in range(2):
            gla_p[i, c] = r[f"ogp{i}"]; gla_s[i, sl] = r[f"ogs{i}"]
            s5r_p[i, c] = unch(r[f"o5rp{i}"]); s5i_p[i, c] = unch(r[f"o5ip{i}"])
            for s in range(NS):
                s5r_s[i, c * NS + s] = unch(r[f"o5rs{i}"][:, :, s]); s5i_s[i, c * NS + s] = unch(r[f"o5is{i}"][:, :, s])
            dn_p[i, c] = r[f"odp{i}"]; dn_s[i, sl] = r[f"ods{i}"]
            cv_p[i, c] = r[f"ocp{i}"].transpose(2, 1, 0).reshape(3, 3072)
            cv_s[i, sl] = r[f"ocs{i}"].transpose(3, 2, 1, 0).reshape(NS, 3, 3072)
    return (y_prompt, y_sample, gla_p, gla_s, s5r_p, s5i_p, s5r_s, s5i_s, dn_p, dn_s, cv_p, cv_s)
```

```python
import os
import numpy as np
from contextlib import ExitStack
import concourse.bass as bass
import concourse.mybir as mybir
from concourse.bass_utils import run_bass_kernel_spmd

F32 = mybir.dt.float32
BF16 = mybir.dt.bfloat16
I32 = mybir.dt.int32
AF = mybir.ActivationFunctionType
ALU = mybir.AluOpType

NCORES = 8
SEQ = 2048
NT = SEQ // 128
NS = 16
EPS = 1e-6
NEG = -1.0e5
STAGE = float(os.environ.get('KSTAGE', '99'))


class Buf:
    def __init__(self, name):
        self.name = name
        self.w = None
        self.r = []


class T:
    def __init__(self, k, name, shape, dtype, space="sbuf"):
        if space == "sbuf":
            self.t = k.es.enter_context(k.nc.sbuf_tensor("t_" + name, shape, dtype))
        else:
            self.t = k.es.enter_context(k.nc.psum_tensor("t_" + name, shape, dtype))
        self.b = Buf(name)
        self.b.psum = (space != "sbuf")

    def __getitem__(self, idx):
        return self.t[idx]


class V:
    def __init__(self, ap, name):
        self.ap = ap
        self.b = Buf(name)

    def __getitem__(self, idx):
        return self.ap[idx]


class K:
    def __init__(self):
        self.nc = bass.Bass("TRN2", target_bir_lowering=False)
        self.es = ExitStack()
        self.eng = {}
        for n in ["pe", "act", "dve", "pool", "sp"]:
            sem = self.es.enter_context(self.nc.semaphore("s_" + n))
            self.eng[n] = dict(sem=sem, cnt=0, waited={}, th=[])
        self.dma_sems = [self.es.enter_context(self.nc.semaphore(f"d{i}")) for i in range(32)]
        self.dma_use = [0] * 32
        self.ndma = 0
        self.ndma_q = {"sp": 0, "pool": 0}
        self.out_toks = []

    def _bufs(self, lst):
        out = []
        for x in lst:
            if isinstance(x, (T, V)):
                if getattr(x, "parts", None):
                    out.extend(x.parts)
                else:
                    out.append(x.b)
            else:
                out.append(x)
        return out

    def _deps(self, en, R, W):
        E = self.eng[en]
        deps = []
        for b in R:
            if b.w is not None:
                deps.append(b.w)
            if getattr(b, "psum", False):
                deps.extend(t_ for t_ in b.r if t_[2] != en)
        for b in W:
            if b.w is not None:
                deps.append(b.w)
            deps.extend(b.r)
        waits = {}
        for (sem, val, src) in deps:
            if src == en and en == "pe":
                continue
            key = id(sem)
            if E["waited"].get(key, 0) >= val:
                continue
            if key not in waits or waits[key][1] < val:
                waits[key] = (sem, val)
        for key, (sem, val) in waits.items():
            E["waited"][key] = val
            E["th"].append(lambda e, sem=sem, val=val: e.wait_ge(sem, val))

    def op(self, en, fn, R=(), W=()):
        R = self._bufs(R)
        W = self._bufs(W)
        E = self.eng[en]
        if E["cnt"] >= 30000:
            E.setdefault("past", []).append((E["sem"], E["cnt"]))
            self.nsem_extra = getattr(self, "nsem_extra", 0) + 1
            E["sem"] = self.es.enter_context(self.nc.semaphore(f"s_{en}_{self.nsem_extra}"))
            E["cnt"] = 0
        self._deps(en, R, W)
        E["cnt"] += 1
        sem = E["sem"]
        E["th"].append(lambda e, fn=fn, sem=sem: fn(e).then_inc(sem, 1))
        tok = (sem, E["cnt"], en)
        for b in W:
            b.w = tok
            b.r = []
        for b in R:
            b.r.append(tok)
            if len(b.r) > 12:
                b.r = b.r[-12:] if False else b.r

    def dma(self, fn, R=(), W=(), en="sp", out=False):
        R = self._bufs(R)
        W = self._bufs(W)
        E = self.eng[en]
        if en == "pool":
            i = 24 + self.ndma_q["pool"] % 8
        else:
            i = self.ndma_q["sp"] % 24
        self.ndma_q[en] += 1
        self.ndma += 1
        sem = self.dma_sems[i]
        prev = self.dma_use[i]
        self._deps(en, R, W)
        if prev > 0 and E["waited"].get(id(sem), 0) < prev:
            E["waited"][id(sem)] = prev
            E["th"].append(lambda e, sem=sem, prev=prev: e.wait_ge(sem, prev))
        val = prev + 16
        self.dma_use[i] = val
        E["th"].append(lambda e, fn=fn, sem=sem: fn(e).then_inc(sem, 16))
        tok = (sem, val, "dma")
        for b in W:
            b.w = tok
            b.r = []
        for b in R:
            b.r.append(tok)
        if out:
            self.out_toks.append(tok)
        return tok

    def barrier(self):
        toks = []
        for n, E in self.eng.items():
            if E["cnt"] > 0:
                toks.append((E["sem"], E["cnt"]))
            for (ps_, pc_) in E.get("past", []):
                toks.append((ps_, pc_))
        for i, sem in enumerate(self.dma_sems):
            if self.dma_use[i] > 0:
                toks.append((sem, self.dma_use[i]))
        for n, E in self.eng.items():
            for (sem, val) in toks:
                if sem is E["sem"] or any(sem is p_[0] for p_ in E.get("past", [])):
                    continue
                if E["waited"].get(id(sem), 0) >= val:
                    continue
                E["waited"][id(sem)] = val
                E["th"].append(lambda e, sem=sem, val=val: e.wait_ge(sem, val))

    def finish(self):
        E = self.eng["sp"]
        last = {}
        for (sem, val, _) in self.out_toks:
            if id(sem) not in last or last[id(sem)][1] < val:
                last[id(sem)] = (sem, val)
        for (sem, val) in last.values():
            E["th"].append(lambda e, sem=sem, val=val: e.wait_ge(sem, val))
        with self.nc.Block() as block:
            @block.tensor
            def _(e):
                for t in self.eng["pe"]["th"]:
                    t(e)

            @block.scalar
            def _(e):
                for t in self.eng["act"]["th"]:
                    t(e)

            @block.vector
            def _(e):
                for t in self.eng["dve"]["th"]:
                    t(e)

            @block.gpsimd
            def _(e):
                for t in self.eng["pool"]["th"]:
                    t(e)

            @block.sync
            def _(e):
                for t in self.eng["sp"]["th"]:
                    t(e)
        self.es.close()


CN = ["ident", "triuc", "c3", "masku", "trib", "c3b", "sela", "selb", "negs", "negi", "ones", "nones", "iota1"]


def make_consts():
    s = np.arange(128)[:, None]
    t = np.arange(128)[None, :]
    same = (s // 64) == (t // 64)
    c = {}
    c["ident"] = (s == t).astype(np.float32)
    c["triuc"] = (s <= t).astype(np.float32) * (-1.0 / 16.0)
    c["c3"] = (s > t).astype(np.float32) * (-1.0 / 16.0)
    c["masku"] = (s <= t).astype(np.float32)
    c["trib"] = ((s <= t) & same).astype(np.float32)
    c["c3b"] = ((s > t) & same).astype(np.float32)
    c["sela"] = np.broadcast_to((s < 64), (128, 128)).astype(np.float32)
    c["selb"] = np.broadcast_to((s >= 64), (128, 128)).astype(np.float32)
    c["negs"] = np.where((t < s) & same, 0.0, NEG).astype(np.float32)
    c["negi"] = np.where((s <= t) & same, 0.0, NEG).astype(np.float32)
    c["ones"] = np.ones((128, 128), np.float32)
    c["nones"] = -np.ones((128, 128), np.float32)
    c["iota1"] = np.broadcast_to(np.arange(1, 129, dtype=np.float32)[None, :], (128, 128)).copy()
    return np.stack([c[n] for n in CN], axis=1)


def build_program(nlayers=4, nt=NT, ns=NS, final=True):
    k = K()
    nc = k.nc

    def din(name, shape):
        return nc.dram_tensor(name, list(shape), F32, kind="ExternalInput").ap()

    def dout(name, shape):
        return nc.dram_tensor(name, list(shape), F32, kind="ExternalOutput").ap()

    D = {}
    D["xT"] = din("xT", [128, 8, SEQ + NS])
    D["consts"] = din("consts", [128, len(CN), 128])
    D["nw"] = din("nw", [128, 4, 8])
    D["fnw"] = din("fnw", [128, 8])
    for i in range(2):
        D[f"wie{i}"] = din(f"wie{i}", [128, 8, 4112])
        D[f"woe{i}"] = din(f"woe{i}", [128, 12, 1024])
        D[f"wgl{i}"] = din(f"wgl{i}", [128, 4, 512])
        D[f"wgu{i}"] = din(f"wgu{i}", [16, 512])
        D[f"bgt{i}"] = din(f"bgt{i}", [1, 512])
        D[f"gnw{i}"] = din(f"gnw{i}", [128, 2])
        D[f"bgf{i}"] = din(f"bgf{i}", [128, 4])
        for nm in ["lre", "lim", "ldt"]:
            D[f"{nm}{i}"] = din(f"{nm}{i}", [128, 16])
        for nm in ["bre", "bim", "cre", "cim"]:
            D[f"{nm}{i}"] = din(f"{nm}{i}", [128, 16, 128])
        D[f"s5d{i}"] = din(f"s5d{i}", [128, 4])
        D[f"bgl{i}"] = din(f"bgl{i}", [128, 4])
        D[f"sgla{i}"] = din(f"sgla{i}", [NS, 4, 128, 256])
        D[f"s5r{i}"] = din(f"s5r{i}", [128, 16, NS])
        D[f"s5i{i}"] = din(f"s5i{i}", [128, 16, NS])
        D[f"wio{i}"] = din(f"wio{i}", [128, 8, 4112])
        D[f"woo{i}"] = din(f"woo{i}", [128, 8, 1024])
        D[f"cvw{i}"] = din(f"cvw{i}", [128, 24, 4])
        D[f"alg{i}"] = din(f"alg{i}", [128, 8])
        D[f"dtb{i}"] = din(f"dtb{i}", [128, 8])
        D[f"dnw{i}"] = din(f"dnw{i}", [128, 1])
        D[f"sdel{i}"] = din(f"sdel{i}", [NS, 8, 128, 128])
        D[f"scv{i}"] = din(f"scv{i}", [128, 24, 3, NS])
    D["smask"] = din("smask", [128, 1])
    O = {}
    O["yT"] = dout("yT", [128, 8, SEQ + NS])
    for i in range(2):
        O[f"ogp{i}"] = dout(f"ogp{i}", [4, 128, 256])
        O[f"ogs{i}"] = dout(f"ogs{i}", [NS, 4, 128, 256])
        O[f"o5rp{i}"] = dout(f"o5rp{i}", [128, 16])
        O[f"o5ip{i}"] = dout(f"o5ip{i}", [128, 16])
        O[f"o5rs{i}"] = dout(f"o5rs{i}", [128, 16, NS])
        O[f"o5is{i}"] = dout(f"o5is{i}", [128, 16, NS])
        O[f"odp{i}"] = dout(f"odp{i}", [8, 128, 128])
        O[f"ods{i}"] = dout(f"ods{i}", [NS, 8, 128, 128])
        O[f"ocp{i}"] = dout(f"ocp{i}", [128, 24, 3])
        O[f"ocs{i}"] = dout(f"ocs{i}", [128, 24, 3, NS])

    def mm(out, lhsT, rhs, R, W, start=True, stop=True):
        k.op("pe", lambda e: e.matmul(out, lhsT=lhsT, rhs=rhs, start=start, stop=stop), R, W)

    def tr(out, in_, ident, R, W):
        k.op("pe", lambda e: e.transpose(out, in_, ident), R, W)

    def act(out, in_, func, R, W, scale=1.0, bias=None, accum=None):
        kw = dict(out=out, in_=in_, func=func, scale=scale)
        if bias is not None:
            kw["bias"] = bias
        if accum is not None:
            kw["accum_out"] = accum
        k.op("act", lambda e: e.activation(**kw), R, W)

    def tt(out, in0, in1, op, R, W, en="dve"):
        k.op(en, lambda e: e.tensor_tensor(out=out, in0=in0, in1=in1, op=op), R, W)

    def ts(out, in0, s1, op0, R, W, s2=None, op1=None, en="dve"):
        if op1 is None:
            k.op(en, lambda e: e.tensor_scalar(out=out, in0=in0, scalar1=s1, scalar2=None, op0=op0), R, W)
        else:
            k.op(en, lambda e: e.tensor_scalar(out=out, in0=in0, scalar1=s1, scalar2=s2, op0=op0, op1=op1), R, W)

    def stt(out, in0, scalar, in1, op0, op1, R, W):
        k.op("dve", lambda e: e.scalar_tensor_tensor(out=out, in0=in0, scalar=scalar, in1=in1, op0=op0, op1=op1), R, W)

    def cp(out, in_, R, W, en="dve"):
        if en == "act":
            k.op("act", lambda e: e.activation(out=out, in_=in_, func=AF.Copy), R, W)
        else:
            k.op(en, lambda e: e.tensor_copy(out=out, in_=in_), R, W)

    def ms(ap, val, W, en="dve"):
        k.op(en, lambda e: e.memset(ap, val), (), W)

    def ld(out, in_, W, en="sp"):
        return k.dma(lambda e: e.dma_start(out=out, in_=in_), (), W, en=en)

    def st(out, in_, R, en="sp"):
        return k.dma(lambda e: e.dma_start(out=out, in_=in_), R, (), en=en, out=True)

    CT = T(k, "CT", [128, len(CN), 128], F32)
    ld(CT[:], D["consts"], [CT])
    ci = {n: j for j, n in enumerate(CN)}

    def C(n):
        return CT[:, ci[n], :]

    onesb = T(k, "onesb", [128, 128], BF16)
    cp(onesb[:], C("ones"), [CT], [onesb])
    identb = T(k, "identb", [128, 128], BF16)
    cp(identb[:], C("ident"), [CT], [identb])
    nw = T(k, "nw", [128, 4, 8], F32)
    ld(nw[:], D["nw"], [nw])
    fnw = T(k, "fnw", [128, 8], F32)
    ld(fnw[:], D["fnw"], [fnw])
    smask = T(k, "smask", [128, 1], F32)
    ld(smask[:], D["smask"], [smask])
    epsc = T(k, "epsc", [128, 1], F32)
    ms(epsc[:], EPS, [epsc])
    onec = T(k, "onec", [128, 1], F32)
    ms(onec[:], 1.0, [onec])

    WIN = T(k, "WIN", [128, 8, 4112], BF16)
    WOUT = T(k, "WOUT", [128, 12, 1024], BF16)
    PS = [T(k, f"PS{j}", [128, 512], F32, "psum") for j in range(7)]
    PSB = T(k, "PSB", [128, 1024], BF16, "psum")
    xt = T(k, "xt", [128, 8, 128], F32)
    xt.parts = [Buf(f"xtp{j_}") for j_ in range(8)]
    sq = T(k, "sq", [128, 8, 128], BF16)
    lnv = T(k, "lnv", [128, 128], F32)
    rstd = T(k, "rstd", [128, 128], F32)
    hn = T(k, "hn", [128, 8, 128], BF16)
    catT = T(k, "catT", [128, 12, 128], BF16)

    def load_weights_bf16(dst, src, nk, ncols):
        if getattr(dst, "parts", None) is None:
            dst.parts = []
        j_ = 0
        for kt in range(nk):
            for c0 in range(0, ncols, 2048):
                c1 = min(ncols, c0 + 2048)
                if j_ >= len(dst.parts):
                    dst.parts.append(Buf(f"part{j_}"))
                k.dma(lambda e, kt=kt, c0=c0, c1=c1: e.dma_start(out=dst[:, kt, c0:c1], in_=src[:, kt, c0:c1]), (), [dst.parts[j_]], en="pool")
                j_ += 1

    XB = [Buf(f"xb{j}") for j in range(NT + NS + 1)]

    def norm_tile(xsrc_ap_fn, xb, L, padded):
        if not padded:
            k.dma(lambda e: e.dma_start(out=xt[:], in_=xsrc_ap_fn), [xb], [xt])
        else:
            ms(xt[:], 0.0, [xt])
            k.dma(lambda e: e.dma_start(out=xt[:, :, 0:1], in_=xsrc_ap_fn, allow_slow_non_contiguous=True), [xb], [xt])
        act(sq[:], xt[:], AF.Square, [xt], [sq])
        for kt in range(8):
            mm(PS[0][:, 0:128], onesb[:], sq[:, kt, :], [onesb, sq], [PS[0]], start=(kt == 0), stop=(kt == 7))
        act(lnv[:], PS[0][:, 0:128], AF.Ln, [PS[0], epsc], [lnv], scale=1.0 / 1024.0, bias=epsc[:])
        act(rstd[:], lnv[:], AF.Exp, [lnv], [rstd], scale=-0.5)
        for kt in range(8):
            stt(hn[:, kt, :], xt[:, kt, :], nw[:, L, kt:kt + 1], rstd[:], ALU.mult, ALU.mult, [xt, nw, rstd], [hn])

    def out_proj_add(nft, xdst_fn, xb, padded):
        for dm in range(8):
            P_ = PS[dm % 4]
            for ft in range(nft):
                mm(P_[:, 0:128], WOUT[:, ft, dm * 128:(dm + 1) * 128], catT[:, ft, :], [WOUT, catT], [P_],
                   start=(ft == 0), stop=(ft == nft - 1))
            tt(xt[:, dm, :], xt[:, dm, :], P_[:, 0:128], ALU.add, [xt.parts[dm], P_], [xt.parts[dm]])
        if padded:
            k.dma(lambda e: e.dma_start(out=xdst_fn, in_=xt[:, :, 0:1], allow_slow_non_contiguous=True), [xt], [xb], out=True)
        else:
            k.dma(lambda e: e.dma_start(out=xdst_fn, in_=xt[:]), [xt], [xb], out=True)

    def out_proj_s(nft):
        W = slice(0, NS)
        for dm in range(8):
            for ft in range(nft):
                mm(PS[0][:, W], WOUT[:, ft, dm * 128:(dm + 1) * 128], catT[:, ft, W], [WOUT, catT], [PS[0]],
                   start=(ft == 0), stop=(ft == nft - 1))
            tt(xt[:, dm, W], xt[:, dm, W], PS[0][:, W], ALU.add, [xt, PS[0]], [xt])
        k.dma(lambda e: e.dma_start(out=O["yT"][:, :, SEQ:SEQ + NS], in_=xt[:, :, W]), [xt], [XB[NT]], out=True)

    NFW = 16010
    NBW = 18300
    ARF = T(k, "ARF", [128, NFW], F32)
    ARB = T(k, "ARB", [128, NBW], BF16)
    angi = T(k, "angi", [128, 512], I32)

    def make_alloc():
        off = {"f": 0, "b": 0}

        def TT_(name, shape, dt=F32):
            n = int(np.prod(shape[1:]))
            if dt == BF16:
                ar, key, lim_ = ARB, "b", NBW
            else:
                ar, key, lim_ = ARF, "f", NFW
            o = off[key]
            off[key] += n
            assert off[key] <= lim_, (name, key, off[key])
            ap = ar.t[0:shape[0], o:o + n]
            if len(shape) == 3:
                ap = ap.rearrange("p (a b) -> p a b", a=shape[1])
            return V(ap, name)
        return TT_

    pref = {"L": -1}

    def prefetch_win(Lnext):
        if Lnext >= nlayers:
            return
        src = D[f"wie{Lnext // 2}"] if Lnext % 2 == 0 else D[f"wio{Lnext // 2}"]
        load_weights_bf16(WIN, src, 8, 4112)
        pref["L"] = Lnext

    def even_layer(L, i):
        k.barrier()
        if pref["L"] != L:
            load_weights_bf16(WIN, D[f"wie{i}"], 8, 4112)
        load_weights_bf16(WOUT, D[f"woe{i}"], 12, 1024)
        if True:
            TT_ = make_alloc()
            wgl = TT_("wgl", [128, 4, 512], BF16)
            load_weights_bf16(wgl, D[f"wgl{i}"], 4, 512)
            wgu = TT_("wgu", [16, 512], BF16)
            k.dma(lambda e: e.dma_start(out=wgu[:], in_=D[f"wgu{i}"]), (), [wgu], en="pool")
            bgt = TT_("bgt", [1, 512], BF16)
            k.dma(lambda e: e.dma_start(out=bgt[:], in_=D[f"bgt{i}"]), (), [bgt], en="pool")
            gnw = TT_("gnw", [128, 2])
            ld(gnw[:], D[f"gnw{i}"], [gnw])
            s5d = TT_("s5d", [128, 4])
            ld(s5d[:], D[f"s5d{i}"], [s5d])
            bgl = TT_("bgl", [128, 4])
            ld(bgl[:], D[f"bgl{i}"], [bgl])
            nbgf = TT_("nbgf", [128, 4])
            ld(nbgf[:], D[f"bgf{i}"], [nbgf])
            ts(nbgf[:], nbgf[:], -1.0, ALU.mult, [nbgf], [nbgf])
            bglh = TT_("bglh", [128, 4])
            ts(bglh[:], bgl[:], 0.5, ALU.mult, [bgl], [bglh])
            Bre = TT_("Bre", [128, 16, 128], BF16)
            Bim = TT_("Bim", [128, 16, 128], BF16)
            load_weights_bf16(Bre, D[f"bre{i}"], 16, 128)
            load_weights_bf16(Bim, D[f"bim{i}"], 16, 128)
            Cwr = TT_("Cwr", [128, 16, 128], BF16)
            Cwi = TT_("Cwi", [128, 16, 128], BF16)
            cst = TT_("cst", [128, 4, 128]); otmp_pre = TT_("otmp_pre", [128, 128])
            lre = TT_("lre", [128, 16]); lim = TT_("lim", [128, 16]); ldt = TT_("ldt", [128, 16])
            ld(lre[:], D[f"lre{i}"], [lre]); ld(lim[:], D[f"lim{i}"], [lim]); ld(ldt[:], D[f"ldt{i}"], [ldt])
            dt_ = TT_("dt_", [128, 16]); rho = TT_("rho", [128, 16]); th = TT_("th", [128, 16])
            act(dt_[:], ldt[:], AF.Exp, [ldt], [dt_])
            t1 = TT_("t1", [128, 16]); t2 = TT_("t2", [128, 16])
            tt(t1[:], lre[:], dt_[:], ALU.mult, [lre, dt_], [t1])
            act(rho[:], t1[:], AF.Exp, [t1], [rho])
            tt(th[:], lim[:], dt_[:], ALU.mult, [lim, dt_], [th])
            COS = TT_("COS", [128, 16, 128]); SIN = TT_("SIN", [128, 16, 128])
            ang = TT_("ang", [128, 128])
            zr = TT_("zr", [128, 256]); zi = TT_("zi", [128, 256]); ta = TT_("ta", [128, 256]); tb = TT_("tb", [128, 256])
            angf = V(ta.ap[:, 0:128], "angf"); angf.b = ta.b
            angd = V(tb.ap[:, 0:128], "angd"); angd.b = tb.b
            gr = TT_("gr", [128, 256]); gi = TT_("gi", [128, 256])

            def sin_chain(dst3, a2, tf, td, ti, shift):
                ops = []
                ops.append(lambda: ts(tf[:], a2[:], 1.0 / (2 * np.pi), ALU.mult, [a2], [tf], s2=shift / (2 * np.pi), op1=ALU.add))
                ops.append(lambda: cp(ti, tf[:], [tf], [angi]))
                ops.append(lambda: cp(td[:], ti, [angi], [td]))
                ops.append(lambda: tt(tf[:], tf[:], td[:], ALU.subtract, [tf, td], [tf]))
                ops.append(lambda: ts(td[:], tf[:], 0.5, ALU.is_gt, [tf], [td]))
                ops.append(lambda: tt(tf[:], tf[:], td[:], ALU.subtract, [tf, td], [tf]))
                ops.append(lambda: ts(td[:], tf[:], -0.5, ALU.is_lt, [tf], [td]))
                ops.append(lambda: tt(tf[:], tf[:], td[:], ALU.add, [tf, td], [tf]))
                ops.append(lambda: act(dst3, tf[:].rearrange("p (a b) -> p a b", a=2), AF.Sin, [tf], [COS, SIN], scale=2 * np.pi * 0.999999))
                return ops

            for g2 in range(8):
                for q_ in range(2):
                    gp = g2 * 2 + q_
                    ts(zr[:, q_ * 128:(q_ + 1) * 128], C("iota1"), th[:, gp:gp + 1], ALU.mult, [CT, th], [zr])
                ca = sin_chain(SIN[:, g2 * 2:(g2 + 1) * 2, :], zr, zi, ta, angi[:, 0:256], 0.0)
                cb = sin_chain(COS[:, g2 * 2:(g2 + 1) * 2, :], zr, tb, gr, angi[:, 256:512], np.pi / 2)
                for oa, ob in zip(ca, cb):
                    oa()
                    ob()
            ar = TT_("ar", [128, 16]); ai = TT_("ai", [128, 16])
            tt(ar[:], rho[:], COS[:, :, 0], ALU.mult, [rho, COS], [ar])
            tt(ai[:], rho[:], SIN[:, :, 0], ALU.mult, [rho, SIN], [ai])
            den = TT_("den", [128, 16]); wr = TT_("wr", [128, 16]); wi = TT_("wi", [128, 16]); am1 = TT_("am1", [128, 16])
            tt(t1[:], lre[:], lre[:], ALU.mult, [lre], [t1])
            tt(t2[:], lim[:], lim[:], ALU.mult, [lim], [t2])
            tt(den[:], t1[:], t2[:], ALU.add, [t1, t2], [den])
            k.op("dve", lambda e: e.reciprocal(out=den[:], in_=den[:]), [den], [den])
            ts(am1[:], ar[:], -1.0, ALU.add, [ar], [am1])
            tt(t1[:], am1[:], lre[:], ALU.mult, [am1, lre], [t1])
            tt(t2[:], ai[:], lim[:], ALU.mult, [ai, lim], [t2])
            tt(t1[:], t1[:], t2[:], ALU.add, [t1, t2], [t1])
            tt(wr[:], t1[:], den[:], ALU.mult, [t1, den], [wr])
            tt(t1[:], ai[:], lre[:], ALU.mult, [ai, lre], [t1])
            tt(t2[:], am1[:], lim[:], ALU.mult, [am1, lim], [t2])
            tt(t1[:], t1[:], t2[:], ALU.subtract, [t1, t2], [t1])
            tt(wi[:], t1[:], den[:], ALU.mult, [t1, den], [wi])
            iwr = TT_("iwr", [128, 16]); iwi = TT_("iwi", [128, 16])
            tt(t1[:], wr[:], wr[:], ALU.mult, [wr], [t1])
            tt(t2[:], wi[:], wi[:], ALU.mult, [wi], [t2])
            tt(t1[:], t1[:], t2[:], ALU.add, [t1, t2], [t1])
            k.op("dve", lambda e: e.reciprocal(out=t1[:], in_=t1[:]), [t1], [t1])
            tt(iwr[:], wr[:], t1[:], ALU.mult, [wr, t1], [iwr])
            tt(iwi[:], wi[:], t1[:], ALU.mult, [wi, t1], [iwi])
            ts(iwi[:], iwi[:], -1.0, ALU.mult, [iwi], [iwi])
            cstB = [Buf("cstA"), Buf("cstB")]
            for gp in range(16):
                q_ = gp % 2
                cb_ = cstB[q_]
                tA, tB = (ang, angf) if q_ == 0 else (angd, otmp_pre)
                ld(cst[:, 2 * q_, :], D[f"cre{i}"][:, gp, :], [cb_])
                ld(cst[:, 2 * q_ + 1, :], D[f"cim{i}"][:, gp, :], [cb_])
                ts(tA[:], cst[:, 2 * q_ + 1, :], wi[:, gp:gp + 1], ALU.mult, [cb_, wi], [tA])
                stt(Cwr[:, gp, :], cst[:, 2 * q_, :], wr[:, gp:gp + 1], tA[:], ALU.mult, ALU.subtract, [cb_, wr, tA], [Cwr])
                ts(tA[:], cst[:, 2 * q_ + 1, :], wr[:, gp:gp + 1], ALU.mult, [cb_, wr], [tA])
                stt(tB[:], cst[:, 2 * q_, :], wi[:, gp:gp + 1], tA[:], ALU.mult, ALU.add, [cb_, wi, tA], [tB])
                ts(Cwi[:, gp, :], tB[:], -1.0, ALU.mult, [tB], [Cwi])
            RHO = TT_("RHO", [128, 16, 128])
            for gp in range(16):
                ts(RHO[:, gp, :], C("ones"), rho[:, gp:gp + 1], ALU.mult, [CT, rho], [RHO])

            S = TT_("S", [128, 4, 256]); Sbf = TT_("Sbf", [128, 4, 256], BF16)
            hpr = TT_("hpr", [128, 16]); hpi = TT_("hpi", [128, 16])
            hs0 = TT_("hs0", [128, 16]); hs1 = TT_("hs1", [128, 16]); hs2 = TT_("hs2", [128, 16]); hs3 = TT_("hs3", [128, 16])
            gT = TT_("gT", [128, 8, 128]); uT = TT_("uT", [128, 4, 128]); uTb = TT_("uTb", [128, 4, 128], BF16)
            sgs = TT_("sgs", [128, 4, 128]); lrT = TT_("lrT", [16, 128], BF16)
            vtok = TT_("vtok", [128, 1024], BF16)
            sp = TT_("sp", [128, 512]); E1 = TT_("E1", [128, 4, 128]); E2 = TT_("E2", [128, 4, 128])
            E3 = TT_("E3", [128, 512]); qs = TT_("qs", [128, 4, 128], BF16); ks = TT_("ks", [128, 4, 128], BF16)
            khat = TT_("khat", [128, 512], BF16); ATm = TT_("ATm", [128, 128], BF16)
            sqo = TT_("sqo", [128, 2, 128], BF16); otmp = TT_("otmp", [128, 128])
            otr = [otmp] + [TT_(f"otmp{q_}", [128, 128]) for q_ in range(1, 4)]
            ATm2 = TT_("ATm2", [128, 128], BF16); sqo2 = TT_("sqo2", [128, 2, 128], BF16)
            Sh4 = [V(S.ap[:, h_, :], f"S4_{h_}") for h_ in range(4)]
            Sbfh4 = [V(Sbf.ap[:, h_, :], f"Sbf4_{h_}") for h_ in range(4)]
            otc = {"n": 0}

            def nxt_ot():
                otc["n"] += 1
                return otr[otc["n"] % 4]
            hrb4 = TT_("hrb4", [128, 4, 128], BF16); hib4 = TT_("hib4", [128, 4, 128], BF16)
            ta2 = TT_("ta2", [128, 256]); tb2 = TT_("tb2", [128, 256])
            hr2 = TT_("hr2", [128, 256]); hi2 = TT_("hi2", [128, 256])
            y5 = TT_("y5", [128, 4, 128]); y5b = TT_("y5b", [128, 4, 128], BF16)
            onesrow = TT_("onesrow", [1, 128], BF16)
            ms(onesrow[:], 1.0, [onesrow])
            ones1 = TT_("ones1", [128, 1]); ms(ones1[:], 1.0, [ones1])

            def run_tile(xsrc, xdst, xb, padded, sidx):
                norm_tile(xsrc, xb, L, padded)
                if padded:
                    for h in range(4):
                        ld(S[:, h, :], D[f"sgla{i}"][sidx, h], [S])
                    cp(Sbf[:], S[:], [S], [Sbf], en="act")
                    ld(hs0[:], D[f"s5r{i}"][sidx], [hs0])
                    ld(hs1[:], D[f"s5i{i}"][sidx], [hs1])
                    tt(hs2[:], hs0[:], iwr[:], ALU.mult, [hs0, iwr], [hs2])
                    tt(hs3[:], hs1[:], iwi[:], ALU.mult, [hs1, iwi], [hs3])
                    tt(hpr[:], hs2[:], hs3[:], ALU.subtract, [hs2, hs3], [hpr])
                    tt(hs2[:], hs0[:], iwi[:], ALU.mult, [hs0, iwi], [hs2])
                    tt(hs3[:], hs1[:], iwr[:], ALU.mult, [hs1, iwr], [hs3])
                    tt(hpi[:], hs2[:], hs3[:], ALU.add, [hs2, hs3], [hpi])

                def proj_f(col0, evac):
                    for kt in range(8):
                        mm(PS[1][:, 0:128], WIN[:, kt, col0:col0 + 128], hn[:, kt, :], [WIN, hn], [PS[1]], start=(kt == 0), stop=(kt == 7))
                    evac(PS[1][:, 0:128])

                if STAGE <= 1:
                    return
                rr = {"n": 0}
                banks = [PS[1], PS[3], PS[4], PS[5]]

                def proj_f(col0, evac):
                    P_ = banks[rr["n"] % 4]
                    rr["n"] += 1
                    for kt in range(8):
                        mm(P_[:, 0:128], WIN[:, kt, col0:col0 + 128], hn[:, kt, :], [WIN, hn], [P_], start=(kt == 0), stop=(kt == 7))
                    evac(P_[:, 0:128], P_)
                for j in range(4):
                    def ev_u(p, P_, j=j):
                        cp(uT[:, j, :], p, [P_], [uT], en="act")
                        cp(uTb[:, j, :], uT[:, j, :], [uT], [uTb], en="pool")
                    proj_f(3088 + j * 128, ev_u)

                def gen_proj_gla():
                    for kt in range(8):
                        mm(PS[1][0:16, 0:128], WIN[:, kt, 3072:3088], hn[:, kt, :], [WIN, hn], [PS[1]], start=(kt == 0), stop=(kt == 7))
                    cp(lrT[:], PS[1][0:16, 0:128], [PS[1]], [lrT])
                    yield
                    mm(PS[2][:, :], lrT[:], wgu[:], [lrT, wgu], [PS[2]], start=True, stop=False)
                    mm(PS[2][:, :], onesrow[:], bgt[:], [onesrow, bgt], [PS[2]], start=False, stop=True)
                    act(E3[:], PS[2][:, :], AF.Exp, [PS[2]], [E3], scale=-1.0)
                    yield
                    act(sp[:], E3[:], AF.Ln, [E3, ones1], [sp], bias=ones1[:])
                    yield
                    for h in range(4):
                        mm(PS[3][:, h * 128:(h + 1) * 128], sp[:, h * 128:(h + 1) * 128], C("triuc"), [sp, CT], [PS[3]])
                    mm(PS[2][:, :], C("c3"), sp[:], [CT, sp], [PS[2]])
                    act(E1[:], PS[3][:, :].rearrange("p (a b) -> p a b", a=4), AF.Exp, [PS[3]], [E1])
                    yield
                    act(E2[:], PS[3][:, :].rearrange("p (a b) -> p a b", a=4), AF.Exp, [PS[3]], [E2], scale=-1.0)
                    act(E3[:], PS[2][:, :], AF.Exp, [PS[2]], [E3])
                    yield
                    for h in range(4):
                        proj_f(h * 128, lambda p, P_, h=h: stt(qs[:, h, :], p, 128.0 ** -0.5, E1[:, h, :], ALU.mult, ALU.mult, [P_, E1], [qs]))
                        proj_f(512 + h * 128, lambda p, P_, h=h: tt(ks[:, h, :], p, E2[:, h, :], ALU.mult, [P_, E2], [ks]))
                        yield
                    for j in range(8):
                        proj_f(2048 + j * 128, lambda p, P_, j=j, : (lambda o_: (act(o_[:], p, AF.Tanh, [P_], [o_], scale=0.5), stt(gT[:, j, :], o_[:], 1.0, p, ALU.add, ALU.mult, [o_, P_], [gT])))(nxt_ot()))
                        ts(gT[:, j, :], gT[:, j, :], gnw[:, (j % 2):(j % 2) + 1], ALU.mult, [gT, gnw], [gT], s2=0.5, op1=ALU.mult, en="pool")
                        if j % 2 == 1:
                            yield
                    for j in range(4):
                        proj_f(3600 + j * 128, lambda p, P_, j=j: (lambda o_: (act(o_[:], p, AF.Tanh, [P_], [o_], scale=0.5), stt(sgs[:, j, :], o_[:], 1.0, p, ALU.add, ALU.mult, [o_, P_], [sgs])))(nxt_ot()))
                        if j % 2 == 1:
                            yield
                    for c in range(2):
                        for kt in range(8):
                            mm(PS[2][:, :], hn[:, kt, :], WIN[:, kt, 1024 + c * 512:1024 + (c + 1) * 512], [hn, WIN], [PS[2]], start=(kt == 0), stop=(kt == 7))
                        cp(vtok[:, c * 512:(c + 1) * 512], PS[2][:, :], [PS[2]], [vtok], en="act")
                        yield
                    for kt in range(8):
                        mm(PS[2][:, :], hn[:, kt, :], WIN[:, kt, 512:1024], [hn, WIN], [PS[2]], start=(kt == 0), stop=(kt == 7))
                    tt(khat[:], PS[2][:, :], E3[:], ALU.mult, [PS[2], E3], [khat])
                    yield
                    def gla_gen(par, PA, PB_, ATm_, sqo_, lnv_, rstd_):
                        for h in (par, par + 2):
                            S_h, Sbf_h = Sh4[h], Sbfh4[h]
                            mm(PA[:, 0:128], ks[:, h, :], qs[:, h, :], [ks, qs], [PA])
                            tt(ATm_[:], PA[:, 0:128], C("masku"), ALU.mult, [PA, CT], [ATm_])
                            yield
                            for half in range(2):
                                mm(PB_[:, half * 128:(half + 1) * 128], Sbf_h[:, half * 128:(half + 1) * 128], qs[:, h, :], [Sbf_h, qs], [PB_], start=True, stop=False)
                                mm(PB_[:, half * 128:(half + 1) * 128], vtok[:, h * 256 + half * 128:h * 256 + (half + 1) * 128], ATm_[:], [vtok, ATm_], [PB_], start=False, stop=True)
                            mm(PA[:, 256:512], khat[:, h * 128:(h + 1) * 128], vtok[:, h * 256:(h + 1) * 256], [khat, vtok], [PA])
                            stt(S_h[:, :], S_h[:, :], E1[:, h, 127:128], PA[:, 256:512], ALU.mult, ALU.add, [S_h, E1, PA], [S_h])
                            act(sqo_[:], PB_[:, 0:256].rearrange("p (a b) -> p a b", a=2), AF.Square, [PB_], [sqo_])
                            yield
                            cp(Sbf_h[:, :], S_h[:, :], [S_h], [Sbf_h], en="act")
                            mm(PA[:, 128:256], onesb[:], sqo_[:, 0, :], [onesb, sqo_], [PA], start=True, stop=False)
                            mm(PA[:, 128:256], onesb[:], sqo_[:, 1, :], [onesb, sqo_], [PA], start=False, stop=True)
                            act(lnv_[:], PA[:, 128:256], AF.Ln, [PA, epsc], [lnv_], scale=1.0 / 256.0, bias=epsc[:])
                            yield
                            act(rstd_[:], lnv_[:], AF.Exp, [lnv_], [rstd_], scale=-0.5)
                            yield
                            for half in range(2):
                                o_ = otr[par * 2 + half]
                                tt(o_[:], PB_[:, half * 128:(half + 1) * 128], rstd_[:], ALU.mult, [PB_, rstd_], [o_])
                                tt(catT[:, h * 2 + half, :], o_[:], gT[:, h * 2 + half, :], ALU.mult, [o_, gT], [catT], en="pool")
                            yield
                    subs = [gla_gen(0, PS[3], PS[4], ATm, sqo, lnv, rstd), gla_gen(1, PS[1], PS[2], ATm2, sqo2, otmp_pre, ang)]
                    while subs:
                        for g_ in list(subs):
                            try:
                                next(g_)
                            except StopIteration:
                                subs.remove(g_)
                        yield

                def gen_s5():
                    col = 127
                    v2 = lambda a_: a_[:, :].rearrange("p (a b) -> p a b", a=2)

                    def sA(c2):
                        ut = c2 // 2
                        for q in range(2):
                            gp = c2 * 2 + q
                            mm(PS[0][:, q * 128:(q + 1) * 128], Bre[:, gp, :], uTb[:, ut, :], [Bre, uTb], [PS[0]])
                            mm(PS[6][:, q * 128:(q + 1) * 128], Bim[:, gp, :], uTb[:, ut, :], [Bim, uTb], [PS[6]])
                        cs = COS[:, c2 * 2:(c2 + 1) * 2, :]
                        sn = SIN[:, c2 * 2:(c2 + 1) * 2, :]
                        p5 = PS[0][:, 0:256].rearrange("p (a b) -> p a b", a=2)
                        p6 = PS[6][:, 0:256].rearrange("p (a b) -> p a b", a=2)
                        tt(v2(ta), p5, cs, ALU.mult, [PS[0], COS], [ta])
                        tt(v2(tb), p6, sn, ALU.mult, [PS[6], SIN], [tb])
                        tt(zr[:], ta[:], tb[:], ALU.add, [ta, tb], [zr])
                        tt(v2(ta), p6, cs, ALU.mult, [PS[6], COS], [ta])
                        tt(v2(tb), p5, sn, ALU.mult, [PS[0], SIN], [tb])
                        tt(zi[:], ta[:], tb[:], ALU.subtract, [ta, tb], [zi])

                    def sB(c2):
                        for q in range(2):
                            gp = c2 * 2 + q
                            sl = slice(q * 128, (q + 1) * 128)
                            k.op("dve", lambda e, gp=gp, sl=sl: e.tensor_tensor_scan(out=gr[:, sl], data0=RHO[:, gp, :], data1=zr[:, sl], initial=hpr[:, gp:gp + 1], op0=ALU.mult, op1=ALU.add), [RHO, zr, hpr], [gr])
                            k.op("dve", lambda e, gp=gp, sl=sl: e.tensor_tensor_scan(out=gi[:, sl], data0=RHO[:, gp, :], data1=zi[:, sl], initial=hpi[:, gp:gp + 1], op0=ALU.mult, op1=ALU.add), [RHO, zi, hpi], [gi])

                    def sC(c2):
                        ut = c2 // 2
                        cs = COS[:, c2 * 2:(c2 + 1) * 2, :]
                        sn = SIN[:, c2 * 2:(c2 + 1) * 2, :]
                        tt(v2(ta2), v2(gr), cs, ALU.mult, [gr, COS], [ta2], en="pool")
                        tt(v2(tb2), v2(gi), sn, ALU.mult, [gi, SIN], [tb2], en="pool")
                        tt(hr2[:], ta2[:], tb2[:], ALU.subtract, [ta2, tb2], [hr2])
                        tt(v2(ta2), v2(gr), sn, ALU.mult, [gr, SIN], [ta2], en="pool")
                        tt(v2(tb2), v2(gi), cs, ALU.mult, [gi, COS], [tb2], en="pool")
                        tt(hi2[:], ta2[:], tb2[:], ALU.add, [ta2, tb2], [hi2])
                        hq = c2 % 2
                        cp(hrb4[:, hq * 2:(hq + 1) * 2, :], v2(hr2), [hr2], [hrb4], en="act")
                        cp(hib4[:, hq * 2:(hq + 1) * 2, :], v2(hi2), [hi2], [hib4], en="act")
                        cp(hpr[:, c2 * 2:(c2 + 1) * 2], v2(hr2)[:, :, col], [hr2], [hpr])
                        cp(hpi[:, c2 * 2:(c2 + 1) * 2], v2(hi2)[:, :, col], [hi2], [hpi])

                    def sD(c2):
                        ut = c2 // 2
                        for q in range(4):
                            gp = ut * 4 + q
                            mm(PS[0][:, 256:384], Cwr[:, gp, :], hrb4[:, q, :], [Cwr, hrb4], [PS[0]], start=(q == 0), stop=False)
                            mm(PS[0][:, 256:384], Cwi[:, gp, :], hib4[:, q, :], [Cwi, hib4], [PS[0]], start=False, stop=(q == 3))
                        stt(y5[:, ut, :], uT[:, ut, :], s5d[:, ut:ut + 1], PS[0][:, 256:384], ALU.mult, ALU.add, [uT, s5d, PS[0]], [y5])
                        yv = y5[:, ut, :]
                        tt(ta2[:, 0:128], yv, yv, ALU.mult, [y5], [ta2])
                        ts(ta2[:, 0:128], ta2[:, 0:128], 0.044715, ALU.mult, [ta2], [ta2], s2=1.0, op1=ALU.add)
                        tt(ta2[:, 0:128], ta2[:, 0:128], yv, ALU.mult, [ta2, y5], [ta2])
                        act(tb2[:, 0:128], ta2[:, 0:128], AF.Tanh, [ta2], [tb2], scale=0.7978845608028654)
                        ts(tb2[:, 0:128], tb2[:, 0:128], 1.0, ALU.add, [tb2], [tb2], s2=0.5, op1=ALU.mult)
                        tt(yv, yv, tb2[:, 0:128], ALU.mult, [y5, tb2], [y5])
                        cp(y5b[:, ut, :], y5[:, ut, :], [y5], [y5b], en="act")

                    for it in range(8 + 2):
                        if 0 <= it - 2 < 8:
                            sC(it - 2)
                            yield
                            if (it - 2) % 2 == 1:
                                sD(it - 2)
                                yield
                        if 0 <= it - 1 < 8:
                            sB(it - 1)
                            yield
                        if it < 8:
                            sA(it)
                            yield

                active = [gen_s5(), gen_proj_gla()]
                while active:
                    for g_ in list(active):
                        try:
                            next(g_)
                        except StopIteration:
                            active.remove(g_)
                gbk = [PS[5], PS[6], PS[3], PS[4]]
                for ot in range(4):
                    for kt in range(4):
                        mm(gbk[ot][:, 0:128], wgl[:, kt, ot * 128:(ot + 1) * 128], y5b[:, kt, :], [wgl, y5b], [gbk[ot]], start=(kt == 0), stop=(kt == 3))
                for ot in range(4):
                    act(otr[ot][:], gbk[ot][:, 0:128], AF.Tanh, [gbk[ot], bglh], [otr[ot]], scale=0.5, bias=bglh[:, ot:ot + 1])
                for ot in range(4):
                    stt(otr[ot][:], otr[ot][:], 1.0, y5[:, ot, :], ALU.add, ALU.mult, [otr[ot], y5], [otr[ot]])
                for ot in range(4):
                    stt(catT[:, 8 + ot, :], otr[ot][:], 0.25, sgs[:, ot, :], ALU.mult, ALU.mult, [otr[ot], sgs], [catT])
                out_proj_add(12, xdst, xb, padded)

            for h_ in range(4):
                ms(Sh4[h_][:, :], 0.0, [Sh4[h_]]); ms(Sbfh4[h_][:, :], 0.0, [Sbfh4[h_]])
            ms(hpr[:], 0.0, [hpr]); ms(hpi[:], 0.0, [hpi])
            xin = D["xT"] if L == 0 else O["yT"]
            for j in range(nt):
                run_tile(xin[:, :, j * 128:(j + 1) * 128], O["yT"][:, :, j * 128:(j + 1) * 128], XB[j], False, None)
            for h in range(4):
                st(O[f"ogp{i}"][h], Sh4[h][:, :], [Sh4[h]])

            def out_state(dre, dim):
                tt(hs0[:], hpr[:], wr[:], ALU.mult, [hpr, wr], [hs0])
                tt(hs1[:], hpi[:], wi[:], ALU.mult, [hpi, wi], [hs1])
                tt(hs2[:], hs0[:], hs1[:], ALU.subtract, [hs0, hs1], [hs2])
                tt(hs0[:], hpr[:], wi[:], ALU.mult, [hpr, wi], [hs0])
                tt(hs1[:], hpi[:], wr[:], ALU.mult, [hpi, wr], [hs1])
                tt(hs3[:], hs0[:], hs1[:], ALU.add, [hs0, hs1], [hs3])
                st(dre, hs2[:], [hs2])
                st(dim, hs3[:], [hs3])
            out_state(O[f"o5rp{i}"], O[f"o5ip{i}"])
            def even_samples():
                k.barrier()
                W = slice(0, NS)
                v16 = lambda ap: ap.rearrange("p (a b) -> p a b", a=16)
                bc = lambda t_: t_[:, :].unsqueeze(2).to_broadcast([128, 16, NS])
                k.dma(lambda e: e.dma_start(out=xt[:, :, W], in_=xin[:, :, SEQ:SEQ + NS]), [XB[NT]], [xt])
                act(sq[:, :, W], xt[:, :, W], AF.Square, [xt], [sq])
                for kt in range(8):
                    mm(PS[0][:, W], onesb[:], sq[:, kt, W], [onesb, sq], [PS[0]], start=(kt == 0), stop=(kt == 7))
                act(lnv[:, W], PS[0][:, W], AF.Ln, [PS[0], epsc], [lnv], scale=1.0 / 1024.0, bias=epsc[:])
                act(rstd[:, W], lnv[:, W], AF.Exp, [lnv], [rstd], scale=-0.5)
                for kt in range(8):
                    stt(hn[:, kt, W], xt[:, kt, W], nw[:, L, kt:kt + 1], rstd[:, W], ALU.mult, ALU.mult, [xt, nw, rstd], [hn])

                sb_ = [PS[1], PS[3], PS[5], PS[6]]
                sc_ = {"n": 0}

                def proj_s(col0, evac):
                    P_ = sb_[sc_["n"] % 4]
                    sc_["n"] += 1
                    for kt in range(8):
                        mm(P_[:, W], WIN[:, kt, col0:col0 + 128], hn[:, kt, W], [WIN, hn], [P_], start=(kt == 0), stop=(kt == 7))
                    evac(P_[:, W], P_)
                for kt in range(8):
                    mm(PS[1][0:16, W], WIN[:, kt, 3072:3088], hn[:, kt, W], [WIN, hn], [PS[1]], start=(kt == 0), stop=(kt == 7))
                cp(lrT[:, W], PS[1][0:16, W], [PS[1]], [lrT])
                for h in range(4):
                    mm(PS[2][:, h * 16:(h + 1) * 16], wgu[:, h * 128:(h + 1) * 128], lrT[:, W], [wgu, lrT], [PS[2]])
                    act(E1[:, h, W], PS[2][:, h * 16:(h + 1) * 16], AF.Exp, [PS[2], nbgf], [E1], scale=-1.0, bias=nbgf[:, h:h + 1])
                for h in range(4):
                    act(E1[:, h, W], E1[:, h, W], AF.Ln, [E1, ones1], [E1], bias=ones1[:])
                    act(E1[:, h, W], E1[:, h, W], AF.Exp, [E1], [E1], scale=-1.0 / 16.0)
                for h in range(4):
                    proj_s(h * 128, lambda p, P_, h=h: ts(E2[:, h, W], p, 128.0 ** -0.5, ALU.mult, [P_], [E2]))
                    proj_s(512 + h * 128, lambda p, P_, h=h: cp(E3[:, h * 16:(h + 1) * 16], p, [P_], [E3]))
                for j in range(8):
                    proj_s(2048 + j * 128, lambda p, P_, j=j: (lambda o_: (act(o_[:, W], p, AF.Tanh, [P_], [o_], scale=0.5), stt(gT[:, j, W], o_[:, W], 1.0, p, ALU.add, ALU.mult, [o_, P_], [gT])))(nxt_ot()))
                    ts(gT[:, j, W], gT[:, j, W], gnw[:, (j % 2):(j % 2) + 1], ALU.mult, [gT, gnw], [gT], s2=0.5, op1=ALU.mult)
                for j in range(4):
                    def ev_u(p, P_, j=j):
                        cp(uT[:, j, W], p, [P_], [uT], en="act")
                        cp(uTb[:, j, W], uT[:, j, W], [uT], [uTb])
                    proj_s(3088 + j * 128, ev_u)
                    proj_s(3600 + j * 128, lambda p, P_, j=j: (lambda o_: (act(o_[:, W], p, AF.Tanh, [P_], [o_], scale=0.5), stt(sgs[:, j, W], o_[:, W], 1.0, p, ALU.add, ALU.mult, [o_, P_], [sgs])))(nxt_ot()))
                vs2 = RHO[0:NS, 0:8, :].rearrange("p a b -> p (a b)")
                for c in range(2):
                    for kt in range(8):
                        mm(PS[2][0:NS, :], hn[:, kt, W], WIN[:, kt, 1024 + c * 512:1024 + (c + 1) * 512], [hn, WIN], [PS[2]], start=(kt == 0), stop=(kt == 7))
                    cp(vs2[:, c * 512:(c + 1) * 512], PS[2][0:NS, :], [PS[2]], [RHO])
                prefetch_win(L + 1)
                Ss = [V(S.ap[:, j, :], f"Ss{j}") for j in range(4)]
                sels = [V(RHO.ap[0:NS, 8 + q_, :], f"sel{q_}") for q_ in range(4)]
                gbanks = [PS[0], PS[1], PS[2], PS[6]]
                zzs = [zr, zi, ta, tb]

                def gla_stream(q_):
                    h = q_
                    P_ = gbanks[q_]; zz = zzs[q_]; selq = sels[q_]; Sj = Ss[q_]
                    for s in range(NS):
                        ts(selq[:, :], C("ones")[0:NS, :], C("ident")[0:NS, s:s + 1], ALU.mult, [CT], [selq], en="pool")
                        ld(Sj[:, :], D[f"sgla{i}"][s, h], [Sj])
                        mm(P_[:, 0:256], selq[:, :], vs2[:, h * 256:(h + 1) * 256], [selq, RHO], [P_])
                        yield
                        ts(zz[:], P_[:, 0:256], E3[:, h * 16 + s:h * 16 + s + 1], ALU.mult, [P_, E3], [zz])
                        yield
                        stt(Sj[:, :], Sj[:, :], E1[:, h, s:s + 1], zz[:], ALU.mult, ALU.add, [Sj, E1, zz], [Sj])
                        yield
                        for half in range(2):
                            c_ = (h * 2 + half) * 16 + s
                            mm(PS[4][:, c_:c_ + 1], Sj[:, half * 128:(half + 1) * 128], E2[:, h, s:s + 1], [Sj, E2], [PS[4]])
                        st(O[f"ogs{i}"][s, h], Sj[:, :], [Sj])
                        yield
                active = [gla_stream(q_) for q_ in range(4)]
                while active:
                    for g_ in list(active):
                        try:
                            next(g_)
                        except StopIteration:
                            active.remove(g_)
                act(sqo[:, 0, :], PS[4][:, 0:128], AF.Square, [PS[4]], [sqo])
                for h in range(4):
                    mm(PS[3][:, h * 16:(h + 1) * 16], onesb[:], sqo[:, 0, (h * 2) * 16:(h * 2 + 1) * 16], [onesb, sqo], [PS[3]], start=True, stop=False)
                    mm(PS[3][:, h * 16:(h + 1) * 16], onesb[:], sqo[:, 0, (h * 2 + 1) * 16:(h * 2 + 2) * 16], [onesb, sqo], [PS[3]], start=False, stop=True)
                act(lnv[:, 0:64], PS[3][:, 0:64], AF.Ln, [PS[3], epsc], [lnv], scale=1.0 / 256.0, bias=epsc[:])
                act(rstd[:, 0:64], lnv[:, 0:64], AF.Exp, [lnv], [rstd], scale=-0.5)
                for h in range(4):
                    for half in range(2):
                        j = h * 2 + half
                        tt(otmp[:, W], PS[4][:, j * 16:(j + 1) * 16], rstd[:, h * 16:(h + 1) * 16], ALU.mult, [PS[4], rstd], [otmp])
                        tt(catT[:, j, W], otmp[:, W], gT[:, j, W], ALU.mult, [otmp, gT], [catT])
                ld(gr[:, :], D[f"s5r{i}"].rearrange("p a b -> p (a b)"), [gr])
                ld(gi[:, :], D[f"s5i{i}"].rearrange("p a b -> p (a b)"), [gi])
                for gp in range(16):
                    mm(PS[5][:, gp * 16:(gp + 1) * 16], Bre[:, gp, :], uTb[:, gp // 4, W], [Bre, uTb], [PS[5]])
                    mm(PS[6][:, gp * 16:(gp + 1) * 16], Bim[:, gp, :], uTb[:, gp // 4, W], [Bim, uTb], [PS[6]])

                def cmul(dr, di, xr, xi, cr, ci):
                    tt(v16(ta[:, :]), v16(xr[:, :]), bc(cr), ALU.mult, [xr, cr], [ta])
                    tt(v16(tb[:, :]), v16(xi[:, :]), bc(ci), ALU.mult, [xi, ci], [tb])
                    tt(dr[:, :], ta[:, :], tb[:, :], ALU.subtract, [ta, tb], [dr])
                    tt(v16(ta[:, :]), v16(xr[:, :]), bc(ci), ALU.mult, [xr, ci], [ta])
                    tt(v16(tb[:, :]), v16(xi[:, :]), bc(cr), ALU.mult, [xi, cr], [tb])
                    tt(di[:, :], ta[:, :], tb[:, :], ALU.add, [ta, tb], [di])
                cmul(zr, zi, gr, gi, iwr, iwi)
                cmul(gr, gi, zr, zi, ar, ai)
                tt(gr[:, :], gr[:, :], PS[5][:, 0:256], ALU.add, [gr, PS[5]], [gr])
                tt(gi[:, :], gi[:, :], PS[6][:, 0:256], ALU.add, [gi, PS[6]], [gi])
                hrv = hrb4[:, 0:2, :].rearrange("p a b -> p (a b)")
                hiv = hib4[:, 0:2, :].rearrange("p a b -> p (a b)")
                cp(hrv, gr[:, :], [gr], [hrb4], en="act")
                cp(hiv, gi[:, :], [gi], [hib4], en="act")
                cmul(zr, zi, gr, gi, wr, wi)
                st(O[f"o5rs{i}"].rearrange("p a b -> p (a b)"), zr[:, :], [zr])
                st(O[f"o5is{i}"].rearrange("p a b -> p (a b)"), zi[:, :], [zi])
                for ut in range(4):
                    for q in range(4):
                        gp = ut * 4 + q
                        mm(PS[3][:, 256 + ut * 16:256 + (ut + 1) * 16], Cwr[:, gp, :], hrv[:, gp * 16:(gp + 1) * 16], [Cwr, hrb4], [PS[3]], start=(q == 0), stop=False)
                        mm(PS[3][:, 256 + ut * 16:256 + (ut + 1) * 16], Cwi[:, gp, :], hiv[:, gp * 16:(gp + 1) * 16], [Cwi, hib4], [PS[3]], start=False, stop=(q == 3))
                    stt(y5[:, ut, W], uT[:, ut, W], s5d[:, ut:ut + 1], PS[3][:, 256 + ut * 16:256 + (ut + 1) * 16], ALU.mult, ALU.add, [uT, s5d, PS[3]], [y5])
                for ut in range(4):
                    yv = y5[:, ut, W]
                    tt(ta[:, W], yv, yv, ALU.mult, [y5], [ta])
                    ts(ta[:, W], ta[:, W], 0.044715, ALU.mult, [ta], [ta], s2=1.0, op1=ALU.add)
                    tt(ta[:, W], ta[:, W], yv, ALU.mult, [ta, y5], [ta])
                    act(tb[:, W], ta[:, W], AF.Tanh, [ta], [tb], scale=0.7978845608028654)
                    ts(tb[:, W], tb[:, W], 1.0, ALU.add, [tb], [tb], s2=0.5, op1=ALU.mult)
                    tt(yv, yv, tb[:, W], ALU.mult, [y5, tb], [y5])
                    cp(y5b[:, ut, W], y5[:, ut, W], [y5], [y5b], en="act")
                for ot in range(4):
                    for kt in range(4):
                        mm(PS[5][:, W], wgl[:, kt, ot * 128:(ot + 1) * 128], y5b[:, kt, W], [wgl, y5b], [PS[5]], start=(kt == 0), stop=(kt == 3))
                    act(otmp[:, W], PS[5][:, W], AF.Tanh, [PS[5], bglh], [otmp], scale=0.5, bias=bglh[:, ot:ot + 1])
                    stt(otmp[:, W], otmp[:, W], 1.0, y5[:, ot, W], ALU.add, ALU.mult, [otmp, y5], [otmp])
                    stt(catT[:, 8 + ot, W], otmp[:, W], 0.25, sgs[:, ot, W], ALU.mult, ALU.mult, [otmp, sgs], [catT])
                out_proj_s(12)
            if ns > 0:
                even_samples()


    def odd_layer(L, i):
        k.barrier()
        if pref["L"] != L:
            load_weights_bf16(WIN, D[f"wio{i}"], 8, 4112)
        load_weights_bf16(WOUT, D[f"woo{i}"], 8, 1024)
        if True:
            TT_ = make_alloc()
            cvw = TT_("cvw", [128, 24, 4]); ld(cvw[:], D[f"cvw{i}"], [cvw])
            alg = TT_("alg", [128, 8]); ld(alg[:], D[f"alg{i}"], [alg])
            dtb = TT_("dtb", [128, 8]); ld(dtb[:], D[f"dtb{i}"], [dtb])
            dnw = TT_("dnw", [128, 1]); ld(dnw[:], D[f"dnw{i}"], [dnw])
            ts(dnw[:], dnw[:], 0.5, ALU.mult, [dnw], [dnw])
            nea = TT_("nea", [128, 8])
            act(nea[:], alg[:], AF.Exp, [alg], [nea])
            ts(nea[:], nea[:], -1.0, ALU.mult, [nea], [nea])
            ones1 = TT_("ones1", [128, 1]); ms(ones1[:], 1.0, [ones1])
            S = TT_("S", [128, 8, 128]); Sbf = TT_("Sbf", [128, 8, 128], BF16)
            cbuf = TT_("cbuf", [128, 24, 131])
            acc = TT_("acc", [128, 24, 128]); acthr = TT_("acthr", [128, 1152])
            acth = V(acthr.ap[:, 0:1152].rearrange("p (a b) -> p a b", a=24), "acth"); acth.b = acthr.b
            acth8 = V(acthr.ap[:, 0:1024].rearrange("p (a b) -> p a b", a=8), "acth8"); acth8.b = acthr.b
            zs = TT_("zs", [128, 8, 128])
            ba = TT_("ba", [128, 16])
            beta = TT_("beta", [128, 8]); g = TT_("g", [128, 8]); gam = TT_("gam", [128, 8]); ghat = TT_("ghat", [128, 8])
            GA = TT_("GA", [128, 8]); GB = TT_("GB", [128, 8]); nGb = TT_("nGb", [128, 8]); nbeta = TT_("nbeta", [128, 8])
            sqk = TT_("sqk", [128, 128], BF16); rn = TT_("rn", [128, 128]); l1 = TT_("l1", [128, 128])
            qTb = TT_("qTb", [128, 128], BF16); kTb = TT_("kTb", [128, 128], BF16); vTb = TT_("vTb", [128, 128], BF16)
            ktk = TT_("ktk", [128, 128]); khat = TT_("khat", [128, 128], BF16); vtk = TT_("vtk", [128, 128]); Vb = TT_("Vb", [128, 128])
            gbc = TT_("gbc", [128, 128]); gbcn = TT_("gbcn", [128, 128])
            Dst = TT_("Dst", [128, 128]); DTi = TT_("DTi", [128, 128])
            M0 = TT_("M0", [128, 128]); N0 = TT_("N0", [128, 128]); M1 = TT_("M1", [128, 128]); N1 = TT_("N1", [128, 128])
            TTm = TT_("TTm", [128, 128]); TTb = TT_("TTb", [128, 128], BF16); PT = TT_("PT", [128, 128], BF16)
            Rb = TT_("Rb", [128, 128], BF16); Vn = TT_("Vn", [128, 128], BF16)
            qsg = TT_("qsg", [128, 128]); Osb = TT_("Osb", [128, 128]); Onb = TT_("Onb", [128, 128], BF16)
            ssq = TT_("ssq", [128, 1]); rs1 = TT_("rs1", [128, 1])
            sqr = [sqk] + [TT_(f"sqk{q_}", [128, 128], BF16) for q_ in range(1, 4)]
            l1r = [l1, rn, ktk, vtk]
            vTr = [vTb, TT_("vTb1", [128, 128], BF16)]
            Onr = [Onb, TT_("Onb1", [128, 128], BF16)]
            gbr = [(gbc, gbcn), (Dst, DTi)]
            qTa = TT_("qTa", [128, 8, 128], BF16); kTa = TT_("kTa", [128, 8, 128], BF16)
            kha = TT_("kha", [128, 8, 128], BF16); Vba = TT_("Vba", [128, 8, 128], BF16)
            Dsa = TT_("Dsa", [128, 8, 128], BF16); DTa = TT_("DTa", [128, 8, 128], BF16)
            qTB = [Buf(f"qTB{h_}") for h_ in range(8)]; kTB = [Buf(f"kTB{h_}") for h_ in range(8)]
            khB = [Buf(f"khB{h_}") for h_ in range(8)]; VbB = [Buf(f"VbB{h_}") for h_ in range(8)]
            DsB = [Buf(f"DsB{h_}") for h_ in range(8)]; DTB = [Buf(f"DTB{h_}") for h_ in range(8)]
            HS = [dict(M0=M0, N0=N0, M1=M1, N1=N1, TTm=TTm, qsg=qsg, Osb=Osb, TTb=TTb, PT=PT, Rb=Rb, Vn=Vn, ssq=ssq, rs1=rs1, P=PS[1])]
            for q_ in range(1, 4):
                d_ = {n_: TT_(f"{n_}_{q_}", [128, 128]) for n_ in ["M0", "N0", "M1", "N1", "TTm", "qsg", "Osb"]}
                d_.update({n_: TT_(f"{n_}_{q_}", [128, 128], BF16) for n_ in ["TTb", "PT", "Rb", "Vn"]})
                d_.update(ssq=TT_(f"ssq_{q_}", [128, 1]), rs1=TT_(f"rs1_{q_}", [128, 1]), P=PS[1 + q_])
                HS.append(d_)
            cbp = [Buf(f"cbp{c_}") for c_ in range(24)]
            acp = [Buf(f"acp{c_}") for c_ in range(24)]
            ctmp = TT_("ctmp", [128, 128])
            thb = [TT_(f"thb{q_}", [128, 128]) for q_ in range(4)]
            Sh = [V(S.ap[:, h_, :], f"Sh{h_}") for h_ in range(8)]
            Sbfh = [V(Sbf.ap[:, h_, :], f"Sbfh{h_}") for h_ in range(8)]

            def run_tile(xsrc, xdst, xb, padded, sidx):
                norm_tile(xsrc, xb, L, padded)
                if padded:
                    for h in range(8):
                        ld(S[:, h, :], D[f"sdel{i}"][sidx, h], [S])
                    cp(Sbf[:], S[:], [S], [Sbf], en="act")
                    ld(cbuf[:, :, 0:3], D[f"scv{i}"][sidx], [cbuf])
                def chain_gen():
                    for kt in range(8):
                        mm(PS[2][:, 0:16], hn[:, kt, :], WIN[:, kt, 4096:4112], [hn, WIN], [PS[2]], start=(kt == 0), stop=(kt == 7))
                    cp(ba[:], PS[2][:, 0:16], [PS[2]], [ba])
                    yield
                    act(beta[:], ba[:, 0:8], AF.Tanh, [ba], [beta], scale=0.5)
                    tt(g[:], ba[:, 8:16], dtb[:], ALU.add, [ba, dtb], [g])
                    yield
                    ts(beta[:], beta[:], 0.5, ALU.mult, [beta], [beta], s2=0.5, op1=ALU.add)
                    act(g[:], g[:], AF.Exp, [g], [g])
                    yield
                    act(g[:], g[:], AF.Ln, [g, ones1], [g], bias=ones1[:])
                    ts(nbeta[:], beta[:], -1.0, ALU.mult, [beta], [nbeta])
                    yield
                    tt(g[:], g[:], nea[:], ALU.mult, [g, nea], [g])
                    yield
                    mm(PS[2][:, 0:8], C("trib"), g[:], [CT, g], [PS[2]])
                    mm(PS[2][:, 8:16], C("c3b"), g[:], [CT, g], [PS[2]])
                    mm(PS[2][:, 16:24], C("sela"), g[:], [CT, g], [PS[2]])
                    mm(PS[2][:, 24:32], C("selb"), g[:], [CT, g], [PS[2]])
                    yield
                    act(gam[:], PS[2][:, 0:8], AF.Exp, [PS[2]], [gam])
                    act(ghat[:], PS[2][:, 8:16], AF.Exp, [PS[2]], [ghat])
                    act(GA[:], PS[2][:, 16:24], AF.Exp, [PS[2]], [GA])
                    act(GB[:], PS[2][:, 24:32], AF.Exp, [PS[2]], [GB])
                    yield
                    tt(nGb[:], nbeta[:], gam[:], ALU.mult, [nbeta, gam], [nGb])
                    yield
                chain = chain_gen()
                pb = [PS[1], PS[4], PS[5], PS[6]]

                def stA(ct):
                    P_ = pb[ct % 4]
                    for kt in range(8):
                        mm(P_[:, 0:128], WIN[:, kt, ct * 128:(ct + 1) * 128], hn[:, kt, :], [WIN, hn], [P_], start=(kt == 0), stop=(kt == 7))
                    cp(cbuf[:, ct, 3:131], P_[:, 0:128], [P_], [cbp[ct]], en="act")

                def stBg(gq):
                    cts = [gq * 4 + q_ for q_ in range(4)]
                    for ct in cts:
                        ts(acc[:, ct, :], cbuf[:, ct, 0:128], cvw[:, ct, 0:1], ALU.mult, [cbp[ct], cvw], [acp[ct]])
                    for w_ in range(1, 4):
                        for ct in cts:
                            stt(acc[:, ct, :], cbuf[:, ct, w_:w_ + 128], cvw[:, ct, w_:w_ + 1], acc[:, ct, :], ALU.mult, ALU.add, [cbp[ct], cvw, acp[ct]], [acp[ct]])

                def stCg(gq):
                    for q_ in range(4):
                        ct = gq * 4 + q_
                        act(thb[q_][:], acc[:, ct, :], AF.Tanh, [acp[ct]], [thb[q_]], scale=0.5)

                def stDg(gq):
                    for q_ in range(4):
                        e_ = "pool" if q_ % 2 == 1 else "dve"
                        ts(thb[q_][:], thb[q_][:], 1.0, ALU.add, [thb[q_]], [thb[q_]], s2=0.5, op1=ALU.mult, en=e_)
                    for q_ in range(4):
                        ct = gq * 4 + q_
                        e_ = "pool" if q_ % 2 == 1 else "dve"
                        tt(acc[:, ct, :], acc[:, ct, :], thb[q_][:], ALU.mult, [acp[ct], thb[q_]], [acp[ct]], en=e_)

                for gq in range(6 + 3):
                    next(chain, None)
                    if gq < 6:
                        for q_ in range(4):
                            stA(gq * 4 + q_)
                    if 0 <= gq - 1 < 6:
                        stBg(gq - 1)
                    if 0 <= gq - 3 < 6:
                        stDg(gq - 3)
                    if 0 <= gq - 2 < 6:
                        stCg(gq - 2)
                for j in range(8):
                    P_ = pb[j % 4]
                    for kt in range(8):
                        mm(P_[:, 0:128], WIN[:, kt, 3072 + j * 128:3072 + (j + 1) * 128], hn[:, kt, :], [WIN, hn], [P_], start=(kt == 0), stop=(kt == 7))
                    th_ = thb[j % 2]
                    act(th_[:], P_[:, 0:128], AF.Tanh, [P_], [th_], scale=0.5)
                    stt(zs[:, j, :], th_[:], 1.0, P_[:, 0:128], ALU.add, ALU.mult, [th_, P_], [zs])
                for _ in chain:
                    pass
                if padded:
                    st(O[f"ocs{i}"][sidx], cbuf[:, :, 1:4], [cbuf])
                else:
                    cp(cbuf[:, :, 0:3], cbuf[:, :, 128:131], cbp, cbp)
                l2tiles = [(h_, 0) for h_ in range(8)] + [(h_, 1) for h_ in range(8)]
                l2banks = [PS[1], PS[2]]

                def l2A(t_):
                    h_, isk = l2tiles[t_]
                    ct = 8 * isk + h_
                    sq_ = sqr[t_ % 4]
                    act(sq_[:], acc[:, ct, :], AF.Square, [acp[ct]], [sq_])
                    P_ = l2banks[t_ % 2]
                    r_ = ((t_ // 2) % 4) * 128
                    mm(P_[:, r_:r_ + 128], onesb[:], sq_[:], [onesb, sq_], [P_])

                def l2B(t_):
                    P_ = l2banks[t_ % 2]
                    r_ = ((t_ // 2) % 4) * 128
                    l_ = l1r[t_ % 4]
                    act(l_[:], P_[:, r_:r_ + 128], AF.Ln, [P_, epsc], [l_], bias=epsc[:])

                def l2C(t_):
                    l_ = l1r[t_ % 4]
                    act(l_[:], l_[:], AF.Exp, [l_], [l_], scale=-0.5)

                def l2D(t_):
                    h_, isk = l2tiles[t_]
                    ct = 8 * isk + h_
                    l_ = l1r[t_ % 4]
                    if isk:
                        stt(kTa[:, h_, :], acc[:, ct, :], 1.0, l_[:], ALU.mult, ALU.mult, [acp[ct], l_], [kTB[h_]])
                    else:
                        stt(qTa[:, h_, :], acc[:, ct, :], 128.0 ** -0.5, l_[:], ALU.mult, ALU.mult, [acp[ct], l_], [qTB[h_]])

                for it in range(16 + 3):
                    if it < 16:
                        l2A(it)
                    if 0 <= it - 3 < 16:
                        l2D(it - 3)
                    if 0 <= it - 2 < 16:
                        l2C(it - 2)
                    if 0 <= it - 1 < 16:
                        l2B(it - 1)
                dbanks = [PS[3], PS[4], PS[5], PS[6]]

                def trA(h_):
                    v_ = vTr[h_ % 2]
                    cp(v_[:], acc[:, 16 + h_, :], [acp[16 + h_]], [v_], en="pool")
                    r_ = (h_ % 4) * 256
                    tr(PSB[:, r_:r_ + 128], kTa[:, h_, :], identb[:], [kTB[h_], identb], [PSB])
                    tr(PSB[:, r_ + 128:r_ + 256], v_[:], identb[:], [v_, identb], [PSB])

                def trB(h_):
                    r_ = (h_ % 4) * 256
                    ts(kha[:, h_, :], PSB[:, r_:r_ + 128], ghat[:, h_:h_ + 1], ALU.mult, [PSB, ghat], [khB[h_]])
                    ts(Vba[:, h_, :], PSB[:, r_ + 128:r_ + 256], beta[:, h_:h_ + 1], ALU.mult, [PSB, beta], [VbB[h_]])

                def dcA(h_):
                    gb_, gn_ = gbr[h_ % 2]
                    ts(gb_[:], C("trib"), g[:, h_:h_ + 1], ALU.mult, [CT, g], [gb_], en="pool")
                    ts(gn_[:], gb_[:], -1.0, ALU.mult, [gb_], [gn_], en="pool")

                def dcB(h_):
                    gb_, gn_ = gbr[h_ % 2]
                    P_ = dbanks[h_ % 4]
                    mm(P_[:, 0:128], gb_[:], C("ones"), [gb_, CT], [P_], start=True, stop=False)
                    mm(P_[:, 0:128], C("nones"), gb_[:], [CT, gb_], [P_], start=False, stop=False)
                    mm(P_[:, 0:128], C("ident"), C("negs"), [CT], [P_], start=False, stop=True)
                    mm(P_[:, 128:256], C("ones"), gb_[:], [gb_, CT], [P_], start=True, stop=False)
                    mm(P_[:, 128:256], gn_[:], C("ones"), [CT, gn_], [P_], start=False, stop=False)
                    mm(P_[:, 128:256], C("ident"), C("negi"), [CT], [P_], start=False, stop=True)

                def dcC(h_):
                    P_ = dbanks[h_ % 4]
                    act(Dsa[:, h_, :], P_[:, 0:128], AF.Exp, [P_], [DsB[h_]])
                    act(DTa[:, h_, :], P_[:, 128:256], AF.Exp, [P_], [DTB[h_]])

                for it in range(8 + 2):
                    if it < 8:
                        trA(it)
                        dcA(it)
                    if 0 <= it - 1 < 8:
                        dcB(it - 1)
                        trB(it - 1)
                    if 0 <= it - 2 < 8:
                        dcC(it - 2)

                def head_gen(h, B):
                    M0, N0, M1, N1, TTm, qsg, Osb = (B[n_] for n_ in ["M0", "N0", "M1", "N1", "TTm", "qsg", "Osb"])
                    TTb, PT, Rb, Vn = (B[n_] for n_ in ["TTb", "PT", "Rb", "Vn"])
                    ssq, rs1, P_ = B["ssq"], B["rs1"], B["P"]
                    S_h, Sbf_h = Sh[h], Sbfh[h]
                    kT_, qT_ = kTa[:, h, :], qTa[:, h, :]
                    R0, R1, R2, R3 = (P_[:, 0:128], P_[:, 128:256], P_[:, 256:384], P_[:, 384:512])
                    mm(R2, kT_, kT_, [kTB[h]], [P_])
                    mm(R1, kT_, qT_, [kTB[h], qTB[h]], [P_])
                    stt(M0[:], R2, nbeta[:, h:h + 1], Dsa[:, h, :], ALU.mult, ALU.mult, [P_, nbeta, DsB[h]], [M0])
                    tt(PT[:], R1, DTa[:, h, :], ALU.mult, [P_, DTB[h]], [PT])
                    ms(Rb[:], 0.0, [Rb], en="pool"); ms(Vn[:], 0.0, [Vn], en="pool")
                    yield
                    tr(R3, M0[:], C("ident"), [M0, CT], [P_])
                    cp(N0[:], R3, [P_], [N0], en="act")
                    yield
                    tt(TTm[:], N0[:], C("ident"), ALU.add, [N0, CT], [TTm])
                    bufs_ = [(M0, N0), (M1, N1)]
                    mm(R0, N0[:], M0[:], [N0, M0], [P_])
                    mm(R1, M0[:], N0[:], [M0, N0], [P_])
                    cp(M1[:], R0, [P_], [M1], en="act")
                    cp(N1[:], R1, [P_], [N1], en="act")
                    yield
                    for lvl in range(5):
                        Mn, Nn = bufs_[(lvl + 1) % 2]
                        Mo, No = bufs_[lvl % 2]
                        if lvl < 4:
                            mm(R0, Nn[:], Mn[:], [Nn, Mn], [P_])
                            mm(R1, Mn[:], Nn[:], [Mn, Nn], [P_])
                        mm(R2, Mn[:], TTm[:], [Mn, TTm], [P_])
                        if lvl < 4:
                            cp(Mo[:], R0, [P_], [Mo], en="act")
                            cp(No[:], R1, [P_], [No], en="act")
                        tt(TTm[:], TTm[:], R2, ALU.add, [TTm, P_], [TTm])
                        yield
                    cp(TTb[:], TTm[:], [TTm], [TTb], en="act")
                    yield
                    for blk in range(2):
                        rs = slice(blk * 64, (blk + 1) * 64)
                        Gc = GA if blk == 0 else GB
                        mm(R0, kT_, Sbf_h[:, :], [kTB[h], Sbf_h], [P_])
                        mm(R2, qT_, Sbf_h[:, :], [qTB[h], Sbf_h], [P_])
                        stt(Rb[rs, :], P_[rs, 0:128], nGb[rs, h:h + 1], Vba[rs, h, :], ALU.mult, ALU.add, [P_, nGb, VbB[h]], [Rb])
                        k.op("act", lambda e, rs=rs, h=h: e.activation(out=qsg[rs, :], in_=P_[rs, 256:384], func=AF.Copy, scale=gam[rs, h:h + 1]), [P_, gam], [qsg])
                        yield
                        mm(R1, TTb[:], Rb[:], [TTb, Rb], [P_])
                        cp(Vn[rs, :], P_[rs, 128:256], [P_], [Vn])
                        yield
                        mm(R3, PT[:], Vn[:], [PT, Vn], [P_])
                        mm(R0, kha[rs, h, :], Vn[rs, :], [khB[h], Vn], [P_])
                        tt(Osb[rs, :], qsg[rs, :], P_[rs, 384:512], ALU.add, [qsg, P_], [Osb])
                        stt(S_h[:, :], S_h[:, :], Gc[:, h:h + 1], R0, ALU.mult, ALU.add, [S_h, Gc, P_], [S_h])
                        yield
                        cp(Sbf_h[:, :], S_h[:, :], [S_h], [Sbf_h], en="act")
                        yield
                    act(qsg[:], Osb[:], AF.Square, [Osb], [qsg, ssq], accum=ssq[:])
                    yield
                    act(rs1[:], ssq[:], AF.Ln, [ssq, epsc], [rs1], scale=1.0 / 128.0, bias=epsc[:])
                    act(rs1[:], rs1[:], AF.Exp, [rs1], [rs1], scale=-0.5)
                    yield
                    On_ = Onr[h % 2]
                    ts(On_[:], Osb[:], rs1[:, 0:1], ALU.mult, [Osb, rs1], [On_])
                    r_ = (h % 4) * 256
                    tr(PSB[:, r_:r_ + 128], On_[:], identb[:], [On_, identb], [PSB])
                    stt(catT[:, h, :], PSB[:, r_:r_ + 128], dnw[:, 0:1], zs[:, h, :], ALU.mult, ALU.mult, [PSB, dnw, zs], [catT])
                    yield

                for hq in range(0, 8, 4):
                    active = [head_gen(hq + q_, HS[q_]) for q_ in range(4)]
                    while active:
                        for g_ in list(active):
                            try:
                                next(g_)
                            except StopIteration:
                                active.remove(g_)
                out_proj_add(8, xdst, xb, padded)

            for h_ in range(8):
                ms(Sh[h_][:, :], 0.0, [Sh[h_]]); ms(Sbfh[h_][:, :], 0.0, [Sbfh[h_]])
            ms(cbuf[:], 0.0, cbp)
            for j in range(nt):
                run_tile(O["yT"][:, :, j * 128:(j + 1) * 128], O["yT"][:, :, j * 128:(j + 1) * 128], XB[j], False, None)
            for h in range(8):
                st(O[f"odp{i}"][h], Sh[h][:, :], [Sh[h]])
            st(O[f"ocp{i}"], cbuf[:, :, 0:3], cbp)
            def odd_samples():
                k.barrier()
                W = slice(0, NS)
                k.dma(lambda e: e.dma_start(out=xt[:, :, W], in_=O["yT"][:, :, SEQ:SEQ + NS]), [XB[NT]], [xt])
                act(sq[:, :, W], xt[:, :, W], AF.Square, [xt], [sq])
                for kt in range(8):
                    mm(PS[0][:, W], onesb[:], sq[:, kt, W], [onesb, sq], [PS[0]], start=(kt == 0), stop=(kt == 7))
                act(lnv[:, W], PS[0][:, W], AF.Ln, [PS[0], epsc], [lnv], scale=1.0 / 1024.0, bias=epsc[:])
                act(rstd[:, W], lnv[:, W], AF.Exp, [lnv], [rstd], scale=-0.5)
                for kt in range(8):
                    stt(hn[:, kt, W], xt[:, kt, W], nw[:, L, kt:kt + 1], rstd[:, W], ALU.mult, ALU.mult, [xt, nw, rstd], [hn])
                sb_ = [PS[1], PS[3], PS[5], PS[6]]
                for ct in range(24):
                    P_ = sb_[ct % 4]
                    for kt in range(8):
                        mm(P_[:, W], WIN[:, kt, ct * 128:(ct + 1) * 128], hn[:, kt, W], [WIN, hn], [P_], start=(kt == 0), stop=(kt == 7))
                    cp(acc[:, ct, W], P_[:, W], [P_], [acc], en="act")
                for j in range(8):
                    P_ = sb_[j % 4]
                    t_ = thb[j % 4]
                    for kt in range(8):
                        mm(P_[:, W], WIN[:, kt, 3072 + j * 128:3072 + (j + 1) * 128], hn[:, kt, W], [WIN, hn], [P_], start=(kt == 0), stop=(kt == 7))
                    act(t_[:, W], P_[:, W], AF.Tanh, [P_], [t_], scale=0.5)
                    stt(zs[:, j, W], t_[:, W], 1.0, P_[:, W], ALU.add, ALU.mult, [t_, P_], [zs])
                for kt in range(8):
                    mm(PS[2][0:NS, 0:16], hn[:, kt, W], WIN[:, kt, 4096:4112], [hn, WIN], [PS[2]], start=(kt == 0), stop=(kt == 7))
                cp(ba[0:NS, :], PS[2][0:NS, 0:16], [PS[2]], [ba])
                prefetch_win(L + 1)
                k.dma(lambda e: e.dma_start(out=cbuf[:, :, 0:48], in_=D[f"scv{i}"].rearrange("p c i s -> p c (i s)")), (), [cbuf])
                wb = lambda i_: cvw[:, :, i_:i_ + 1].to_broadcast([128, 24, NS])
                tt(acth[:, :, W], cbuf[:, :, 0:16], wb(0), ALU.mult, [cbuf, cvw], [acth])
                for i_ in (1, 2):
                    tt(acth[:, :, 16:32], cbuf[:, :, 16 * i_:16 * i_ + 16], wb(i_), ALU.mult, [cbuf, cvw], [acth])
                    tt(acth[:, :, W], acth[:, :, W], acth[:, :, 16:32], ALU.add, [acth], [acth])
                tt(acth[:, :, 16:32], acc[:, :, W], wb(3), ALU.mult, [acc, cvw], [acth])
                tt(acth[:, :, W], acth[:, :, W], acth[:, :, 16:32], ALU.add, [acth], [acth])
                k.dma(lambda e: e.dma_start(out=O[f"ocs{i}"].rearrange("p c i s -> p c (i s)")[:, :, 0:32], in_=cbuf[:, :, 16:48]), [cbuf], (), out=True)
                k.dma(lambda e: e.dma_start(out=O[f"ocs{i}"][:, :, 2, :], in_=acc[:, :, W]), [acc], (), out=True)
                act(acth[:, :, 32:48], acth[:, :, W], AF.Tanh, [acth], [acth], scale=0.5)
                ts(acth[:, :, 32:48], acth[:, :, 32:48], 0.5, ALU.mult, [acth], [acth], s2=0.5, op1=ALU.add)
                tt(acth[:, :, W], acth[:, :, W], acth[:, :, 32:48], ALU.mult, [acth], [acth])
                v8 = lambda ap: ap.rearrange("p (a b) -> p a b", a=8)
                qn = v8(qsg[:, :]); kn = v8(Osb[:, :])
                for (c0, dstv, dstT, scl) in ((0, qn, qsg, 128.0 ** -0.5), (8, kn, Osb, 1.0)):
                    act(v8(sqk[:, :]), acth[:, c0:c0 + 8, W], AF.Square, [acth], [sqk])
                    mm(PS[3][:, 0:128], onesb[:], sqk[:, :], [onesb, sqk], [PS[3]])
                    act(l1[:, :], PS[3][:, 0:128], AF.Ln, [PS[3], epsc], [l1], bias=epsc[:])
                    act(rn[:, :], l1[:, :], AF.Exp, [l1], [rn], scale=-0.5)
                    stt(dstv, acth[:, c0:c0 + 8, W], scl, v8(rn[:, :]), ALU.mult, ALU.mult, [acth, rn], [dstT])
                P16 = slice(0, NS)
                act(beta[P16, :], ba[P16, 0:8], AF.Tanh, [ba], [beta], scale=0.5)
                ts(beta[P16, :], beta[P16, :], 0.5, ALU.mult, [beta], [beta], s2=0.5, op1=ALU.add)
                tt(g[P16, :], ba[P16, 8:16], dtb[P16, :], ALU.add, [ba, dtb], [g])
                act(g[P16, :], g[P16, :], AF.Exp, [g], [g])
                act(g[P16, :], g[P16, :], AF.Ln, [g, ones1], [g], bias=ones1[P16, :])
                tt(g[P16, :], g[P16, :], nea[P16, :], ALU.mult, [g, nea], [g])
                act(gam[P16, :], g[P16, :], AF.Exp, [g], [gam])
                eye3 = C("ident")[P16, 0:NS].unsqueeze(1).to_broadcast([NS, 8, NS])
                for (src, dstb) in ((gam, gbc), (beta, gbcn)):
                    tt(v8(Dst[P16, :]), src[P16, :].unsqueeze(2).to_broadcast([NS, 8, NS]), eye3, ALU.mult, [src, CT], [Dst])
                    mm(PS[2][:, 128:256], C("ones")[P16, :], Dst[P16, :], [CT, Dst], [PS[2]])
                    cp(dstb[:, :], PS[2][:, 128:256], [PS[2]], [dstb])
                ts(DTi[:, :], gbc[:, :], -1.0, ALU.mult, [gbc], [DTi])
                Ss = [V(S.ap[:, j, :], f"Sd{j}") for j in range(8)]
                sbanks = [PS[0], PS[1], PS[2], PS[3], PS[5], PS[6]]
                NSTR = 6
                tt(Dst[:, :], DTi[:, :], gbcn[:, :], ALU.mult, [DTi, gbcn], [Dst])
                tt(v8(TTm[:, :]), acth[:, 16:24, W], v8(gbcn[:, :]), ALU.mult, [acth, gbcn], [TTm])
                vbs = [HS[q_ % 4]["M0" if q_ < 4 else "M1"] for q_ in range(NSTR)]
                t2s = [HS[q_ % 4]["N0" if q_ < 4 else "N1"] for q_ in range(NSTR)]
                vcs = [HS[q_ % 4]["ssq" if q_ < 4 else "rs1"] for q_ in range(NSTR)]

                def dn_stream(q_):
                    P_ = sbanks[q_]; vb_ = vbs[q_]; t2_ = t2s[q_]; vc = vcs[q_]
                    Sj = Ss[q_]
                    for idx in range(q_, NS * 8, NSTR):
                        s, h = idx // 8, idx % 8
                        c_ = h * 16 + s
                        ld(Sj[:, :], D[f"sdel{i}"][s, h], [Sj])
                        mm(P_[:, 0:1], Sj[:, :], kn[:, h, s:s + 1], [Sj, Osb], [P_])
                        yield
                        stt(vc[:, 0:1], P_[:, 0:1], Dst[:, c_:c_ + 1], TTm[:, c_:c_ + 1], ALU.mult, ALU.add, [P_, Dst, TTm], [vc])
                        yield
                        ts(vb_[:, :], C("ones"), vc[:, 0:1], ALU.mult, [CT, vc], [vb_], en="pool")
                        mm(P_[:, 128:256], vb_[:, :], C("ident"), [vb_, CT], [P_])
                        yield
                        ts(t2_[:, :], P_[:, 128:256], kn[:, h, s:s + 1], ALU.mult, [P_, Osb], [t2_])
                        yield
                        stt(Sj[:, :], Sj[:, :], gbc[:, c_:c_ + 1], t2_[:, :], ALU.mult, ALU.add, [Sj, gbc, t2_], [Sj])
                        yield
                        mm(PS[4][:, c_:c_ + 1], Sj[:, :], qn[:, h, s:s + 1], [Sj, qsg], [PS[4]])
                        st(O[f"ods{i}"][s, h], Sj[:, :], [Sj])
                        yield
                active = [dn_stream(q_) for q_ in range(NSTR)]
                while active:
                    for g_ in list(active):
                        try:
                            next(g_)
                        except StopIteration:
                            active.remove(g_)
                act(sqk[:, :], PS[4][:, 0:128], AF.Square, [PS[4]], [sqk])
                mm(PS[3][:, 0:128], onesb[:], sqk[:, :], [onesb, sqk], [PS[3]])
                act(l1[:, :], PS[3][:, 0:128], AF.Ln, [PS[3], epsc], [l1], scale=1.0 / 128.0, bias=epsc[:])
                act(rn[:, :], l1[:, :], AF.Exp, [l1], [rn], scale=-0.5)
                tt(l1[:, :], PS[4][:, 0:128], rn[:, :], ALU.mult, [PS[4], rn], [l1])
                for h in range(8):
                    stt(catT[:, h, W], l1[:, h * 16:(h + 1) * 16], dnw[:, 0:1], zs[:, h, W], ALU.mult, ALU.mult, [l1, dnw, zs], [catT])
                out_proj_s(8)
            if ns > 0:
                odd_samples()


    yb = Buf("yT_dram")
    for L in range(nlayers):
        if L % 2 == 0:
            even_layer(L, L // 2)
        else:
            odd_layer(L, L // 2)
    if final:
        k.barrier()
        NBF = 4
        fx = [V(ARF.t[:, j_ * 1024:(j_ + 1) * 1024].rearrange("p (a b) -> p a b", a=8), f"fx{j_}") for j_ in range(NBF)]
        fsq = [V(ARB.t[:, j_ * 1024:(j_ + 1) * 1024].rearrange("p (a b) -> p a b", a=8), f"fsq{j_}") for j_ in range(NBF)]
        fln = [V(ARF.t[:, 4096 + j_ * 256:4096 + j_ * 256 + 128], f"fln{j_}") for j_ in range(NBF)]
        frs = [V(ARF.t[:, 4096 + j_ * 256 + 128:4096 + (j_ + 1) * 256], f"frs{j_}") for j_ in range(NBF)]
        tiles_f = [(O["yT"][:, :, j * 128:(j + 1) * 128], 128, XB[j]) for j in range(nt)]
        if ns > 0:
            tiles_f.append((O["yT"][:, :, SEQ:SEQ + NS], NS, XB[NT]))
        nf = len(tiles_f)

        def f_load(t_):
            ap, w_, xb_ = tiles_f[t_]
            X_ = fx[t_ % NBF]
            dst_ = X_[:, :, 0:w_]
            k.dma(lambda e, dst_=dst_, ap=ap: e.dma_start(out=dst_, in_=ap), [xb_], [X_])

        def f_comp(t_):
            ap, w_, xb_ = tiles_f[t_]
            q_ = t_ % NBF
            X_, Q_, L_, R_, P_ = fx[q_], fsq[q_], fln[q_], frs[q_], PS[q_]
            act(Q_[:, :, 0:w_], X_[:, :, 0:w_], AF.Square, [X_], [Q_])
            for kt in range(8):
                mm(P_[:, 0:w_], onesb[:], Q_[:, kt, 0:w_], [onesb, Q_], [P_], start=(kt == 0), stop=(kt == 7))
            act(L_[:, 0:w_], P_[:, 0:w_], AF.Ln, [P_, epsc], [L_], scale=1.0 / 1024.0, bias=epsc[:])
            act(R_[:, 0:w_], L_[:, 0:w_], AF.Exp, [L_], [R_], scale=-0.5)
            for kt in range(8):
                stt(X_[:, kt, 0:w_], X_[:, kt, 0:w_], fnw[:, kt:kt + 1], R_[:, 0:w_], ALU.mult, ALU.mult, [X_, fnw, R_], [X_])

        def f_store(t_):
            ap, w_, xb_ = tiles_f[t_]
            X_ = fx[t_ % NBF]
            src_ = X_[:, :, 0:w_]
            k.dma(lambda e, src_=src_, ap=ap: e.dma_start(out=ap, in_=src_), [X_], [xb_], out=True)

        for it in range(nf + 2):
            if it < nf:
                f_load(it)
            if 0 <= it - 1 < nf:
                f_comp(it - 1)
            if 0 <= it - 2 < nf:
                f_store(it - 2)
    k.finish()
    return nc


def _tile_rows(w, nk):
    return np.ascontiguousarray(w.reshape(nk, 128, -1).transpose(1, 0, 2))


def prepare_inputs(inp):
    f = lambda a: np.ascontiguousarray(np.asarray(a, dtype=np.float32))
    x_prompt = f(inp["x_prompt"]); x_sample = f(inp["x_sample"])
    consts = make_consts()
    nw = np.ascontiguousarray(f(inp["norm_w"]).reshape(4, 8, 128).transpose(2, 0, 1))
    fnw = np.ascontiguousarray(f(inp["final_norm_w"]).reshape(8, 128).T)
    smask = np.zeros((128, 1), np.float32); smask[0, 0] = 1.0
    shared = {"consts": consts, "nw": nw, "fnw": fnw, "smask": smask}
    for i in range(2):
        shared[f"wie{i}"] = _tile_rows(f(inp["w_in_even"][i]), 8)
        shared[f"woe{i}"] = _tile_rows(f(inp["w_out_even"][i]), 12)
        shared[f"wgl{i}"] = _tile_rows(f(inp["s5_w_glu"][i]), 4)
        shared[f"wgu{i}"] = f(inp["gla_w_gate_up"][i])
        shared[f"bgt{i}"] = f(inp["gla_b_gate"][i]).reshape(1, 512)
        shared[f"gnw{i}"] = np.ascontiguousarray(f(inp["gla_norm_w"][i]).reshape(2, 128).T)
        shared[f"bgf{i}"] = np.ascontiguousarray(f(inp["gla_b_gate"][i]).reshape(4, 128).T)
        chm = lambda a: np.ascontiguousarray(a.reshape(16, 2, 64).transpose(1, 2, 0).reshape(128, 16))
        shared[f"lre{i}"] = chm(f(inp["s5_lambda_re"][i]))
        shared[f"lim{i}"] = chm(f(inp["s5_lambda_im"][i]))
        shared[f"ldt{i}"] = chm(np.repeat(f(inp["s5_log_dt"][i])[:, None], 64, axis=1))
        for nm, src in (("bre", "s5_b_re"), ("bim", "s5_b_im")):
            b = f(inp[src][i])
            out = np.zeros((128, 16, 128), np.float32)
            for g in range(32):
                gp, g2 = g // 2, g % 2
                out[(g % 8) * 16:(g % 8) * 16 + 16, gp, g2 * 64:(g2 + 1) * 64] = b[g].T
            shared[f"{nm}{i}"] = out
        for nm, src in (("cre", "s5_c_re"), ("cim", "s5_c_im")):
            c_ = f(inp[src][i])
            out = np.zeros((128, 16, 128), np.float32)
            for g in range(32):
                gp, g2 = g // 2, g % 2
                out[g2 * 64:(g2 + 1) * 64, gp, (g % 8) * 16:(g % 8) * 16 + 16] = c_[g].T
            shared[f"{nm}{i}"] = out
        shared[f"s5d{i}"] = np.ascontiguousarray(f(inp["s5_d"][i]).reshape(4, 128).T)
        shared[f"bgl{i}"] = np.ascontiguousarray(f(inp["s5_b_glu"][i]).reshape(4, 128).T)
        shared[f"wio{i}"] = _tile_rows(f(inp["w_in_odd"][i]), 8)
        shared[f"woo{i}"] = _tile_rows(f(inp["w_out_odd"][i]), 8)
        shared[f"cvw{i}"] = np.ascontiguousarray(f(inp["dn_conv_w"][i]).reshape(4, 24, 128).transpose(2, 1, 0))
        shared[f"alg{i}"] = np.ascontiguousarray(np.broadcast_to(f(inp["dn_a_log"][i])[None, :], (128, 8)))
        shared[f"dtb{i}"] = np.ascontiguousarray(np.broadcast_to(f(inp["dn_dt_bias"][i])[None, :], (128, 8)))
        shared[f"dnw{i}"] = f(inp["dn_norm_w"][i]).reshape(128, 1)
    in_maps = []
    for c in range(NCORES):
        m = dict(shared)
        xs = np.concatenate([x_prompt[c], x_sample[c * NS:(c + 1) * NS, 0]], axis=0)
        m["xT"] = np.ascontiguousarray(xs.reshape(SEQ + NS, 8, 128).transpose(2, 1, 0))
        sl = slice(c * NS, (c + 1) * NS)
        for i in range(2):
            m[f"sgla{i}"] = f(inp["state_gla"][i][sl])
            for nm, src in (("s5r", "state_s5_re"), ("s5i", "state_s5_im")):
                a = f(inp[src][i][sl])
                m[f"{nm}{i}"] = np.ascontiguousarray(a.reshape(NS, 16, 2, 64).transpose(2, 3, 1, 0).reshape(128, 16, NS))
            m[f"sdel{i}"] = f(inp["state_delta"][i][sl])
            a = f(inp["state_conv"][i][sl])
            m[f"scv{i}"] = np.ascontiguousarray(a.reshape(NS, 3, 24, 128).transpose(3, 2, 1, 0))
        in_maps.append(m)
    return in_maps


def kernel(**inp):
    in_maps = prepare_inputs(inp)
    nc = build_program()
    res = run_bass_kernel_spmd(nc, in_maps, core_ids=list(range(NCORES)))
    return assemble(res.results)


def assemble(R):
    y_prompt = np.zeros((8, SEQ, 1024), np.float32); y_sample = np.zeros((128, 1, 1024), np.float32)
    gla_p = np.zeros((2, 8, 4, 128, 256), np.float32); gla_s = np.zeros((2, 128, 4, 128, 256), np.float32)
    s5r_p = np.zeros((2, 8, 32, 64), np.float32); s5i_p = np.zeros_like(s5r_p)
    s5r_s = np.zeros((2, 128, 32, 64), np.float32); s5i_s = np.zeros_like(s5r_s)
    dn_p = np.zeros((2, 8, 8, 128, 128), np.float32); dn_s = np.zeros((2, 128, 8, 128, 128), np.float32)
    cv_p = np.zeros((2, 8, 3, 3072), np.float32); cv_s = np.zeros((2, 128, 3, 3072), np.float32)
    unch = lambda a: a.reshape(2, 64, 16).transpose(2, 0, 1).reshape(32, 64)
    for c in range(NCORES):
        r = R[c]
        yt = r["yT"].transpose(2, 1, 0).reshape(SEQ + NS, 1024)
        y_prompt[c] = yt[:SEQ]; y_sample[c * NS:(c + 1) * NS, 0] = yt[SEQ:]
        sl = slice(c * NS, (c + 1) * NS)
        for i in range(2):
            gla_p[i, c] = r[f"ogp{i}"]; gla_s[i, sl] = r[f"ogs{i}"]
            s5r_p[i, c] = unch(r[f"o5rp{i}"]); s5i_p[i, c] = unch(r[f"o5ip{i}"])
            for s in range(NS):
                s5r_s[i, c * NS + s] = unch(r[f"o5rs{i}"][:, :, s]); s5i_s[i, c * NS + s] = unch(r[f"o5is{i}"][:, :, s])
            dn_p[i, c] = r[f"odp{i}"]; dn_s[i, sl] = r[f"ods{i}"]
            cv_p[i, c] = r[f"ocp{i}"].transpose(2, 1, 0).reshape(3, 3072)
            cv_s[i, sl] = r[f"ocs{i}"].transpose(3, 2, 1, 0).reshape(NS, 3, 3072)
    return (y_prompt, y_sample, gla_p, gla_s, s5r_p, s5i_p, s5r_s, s5i_s, dn_p, dn_s, cv_p, cv_s)
```

```python
import os
import numpy as np
from contextlib import ExitStack
import concourse.bass as bass
import concourse.mybir as mybir
from concourse.bass_utils import run_bass_kernel_spmd

F32 = mybir.dt.float32
BF16 = mybir.dt.bfloat16
I32 = mybir.dt.int32
AF = mybir.ActivationFunctionType
ALU = mybir.AluOpType

NCORES = 8
SEQ = 2048
NT = SEQ // 128
NS = 16
EPS = 1e-6
NEG = -1.0e5
STAGE = float(os.environ.get('KSTAGE', '99'))


class Buf:
    def __init__(self, name):
        self.name = name
        self.w = None
        self.r = []


class T:
    def __init__(self, k, name, shape, dtype, space="sbuf"):
        if space == "sbuf":
            self.t = k.es.enter_context(k.nc.sbuf_tensor("t_" + name, shape, dtype))
        else:
            self.t = k.es.enter_context(k.nc.psum_tensor("t_" + name, shape, dtype))
        self.b = Buf(name)
        self.b.psum = (space != "sbuf")

    def __getitem__(self, idx):
        return self.t[idx]


class V:
    def __init__(self, ap, name):
        self.ap = ap
        self.b = Buf(name)

    def __getitem__(self, idx):
        return self.ap[idx]


class K:
    def __init__(self):
        self.nc = bass.Bass("TRN2", target_bir_lowering=False)
        self.es = ExitStack()
        self.eng = {}
        for n in ["pe", "act", "dve", "pool", "sp"]:
            sem = self.es.enter_context(self.nc.semaphore("s_" + n))
            self.eng[n] = dict(sem=sem, cnt=0, waited={}, th=[])
        self.dma_sems = [self.es.enter_context(self.nc.semaphore(f"d{i}")) for i in range(32)]
        self.dma_use = [0] * 32
        self.ndma = 0
        self.ndma_q = {"sp": 0, "pool": 0}
        self.out_toks = []

    def _bufs(self, lst):
        out = []
        for x in lst:
            if isinstance(x, (T, V)):
                if getattr(x, "parts", None):
                    out.extend(x.parts)
                else:
                    out.append(x.b)
            else:
                out.append(x)
        return out

    def _deps(self, en, R, W):
        E = self.eng[en]
        deps = []
        for b in R:
            if b.w is not None:
                deps.append(b.w)
            if getattr(b, "psum", False):
                deps.extend(t_ for t_ in b.r if t_[2] != en)
        for b in W:
            if b.w is not None:
                deps.append(b.w)
            deps.extend(b.r)
        waits = {}
        for (sem, val, src) in deps:
            if src == en and en == "pe":
                continue
            key = id(sem)
            if E["waited"].get(key, 0) >= val:
                continue
            if key not in waits or waits[key][1] < val:
                waits[key] = (sem, val)
        for key, (sem, val) in waits.items():
            E["waited"][key] = val
            E["th"].append(lambda e, sem=sem, val=val: e.wait_ge(sem, val))

    def op(self, en, fn, R=(), W=()):
        R = self._bufs(R)
        W = self._bufs(W)
        E = self.eng[en]
        if E["cnt"] >= 30000:
            E.setdefault("past", []).append((E["sem"], E["cnt"]))
            self.nsem_extra = getattr(self, "nsem_extra", 0) + 1
            E["sem"] = self.es.enter_context(self.nc.semaphore(f"s_{en}_{self.nsem_extra}"))
            E["cnt"] = 0
        self._deps(en, R, W)
        E["cnt"] += 1
        sem = E["sem"]
        E["th"].append(lambda e, fn=fn, sem=sem: fn(e).then_inc(sem, 1))
        tok = (sem, E["cnt"], en)
        for b in W:
            b.w = tok
            b.r = []
        for b in R:
            b.r.append(tok)
            if len(b.r) > 12:
                b.r = b.r[-12:] if False else b.r

    def dma(self, fn, R=(), W=(), en="sp", out=False):
        R = self._bufs(R)
        W = self._bufs(W)
        E = self.eng[en]
        if en == "pool":
            i = 24 + self.ndma_q["pool"] % 8
        else:
            i = self.ndma_q["sp"] % 24
        self.ndma_q[en] += 1
        self.ndma += 1
        sem = self.dma_sems[i]
        prev = self.dma_use[i]
        self._deps(en, R, W)
        if prev > 0 and E["waited"].get(id(sem), 0) < prev:
            E["waited"][id(sem)] = prev
            E["th"].append(lambda e, sem=sem, prev=prev: e.wait_ge(sem, prev))
        val = prev + 16
        self.dma_use[i] = val
        E["th"].append(lambda e, fn=fn, sem=sem: fn(e).then_inc(sem, 16))
        tok = (sem, val, "dma")
        for b in W:
            b.w = tok
            b.r = []
        for b in R:
            b.r.append(tok)
        if out:
            self.out_toks.append(tok)
        return tok

    def barrier(self):
        toks = []
        for n, E in self.eng.items():
            if E["cnt"] > 0:
                toks.append((E["sem"], E["cnt"]))
            for (ps_, pc_) in E.get("past", []):
                toks.append((ps_, pc_))
        for i, sem in enumerate(self.dma_sems):
            if self.dma_use[i] > 0:
                toks.append((sem, self.dma_use[i]))
        for n, E in self.eng.items():
            for (sem, val) in toks:
                if sem is E["sem"] or any(sem is p_[0] for p_ in E.get("past", [])):
                    continue
                if E["waited"].get(id(sem), 0) >= val:
                    continue
                E["waited"][id(sem)] = val
                E["th"].append(lambda e, sem=sem, val=val: e.wait_ge(sem, val))

    def finish(self):
        E = self.eng["sp"]
        last = {}
        for (sem, val, _) in self.out_toks:
            if id(sem) not in last or last[id(sem)][1] < val:
                last[id(sem)] = (sem, val)
        for (sem, val) in last.values():
            E["th"].append(lambda e, sem=sem, val=val: e.wait_ge(sem, val))
        with self.nc.Block() as block:
            @block.tensor
            def _(e):
                for t in self.eng["pe"]["th"]:
                    t(e)

            @block.scalar
            def _(e):
                for t in self.eng["act"]["th"]:
                    t(e)

            @block.vector
            def _(e):
                for t in self.eng["dve"]["th"]:
                    t(e)

            @block.gpsimd
            def _(e):
                for t in self.eng["pool"]["th"]:
                    t(e)

            @block.sync
            def _(e):
                for t in self.eng["sp"]["th"]:
                    t(e)
        self.es.close()


CN = ["ident", "triuc", "c3", "masku", "trib", "c3b", "sela", "selb", "negs", "negi", "ones", "nones", "iota1"]


def make_consts():
    s = np.arange(128)[:, None]
    t = np.arange(128)[None, :]
    same = (s // 64) == (t // 64)
    c = {}
    c["ident"] = (s == t).astype(np.float32)
    c["triuc"] = (s <= t).astype(np.float32) * (-1.0 / 16.0)
    c["c3"] = (s > t).astype(np.float32) * (-1.0 / 16.0)
    c["masku"] = (s <= t).astype(np.float32)
    c["trib"] = ((s <= t) & same).astype(np.float32)
    c["c3b"] = ((s > t) & same).astype(np.float32)
    c["sela"] = np.broadcast_to((s < 64), (128, 128)).astype(np.float32)
    c["selb"] = np.broadcast_to((s >= 64), (128, 128)).astype(np.float32)
    c["negs"] = np.where((t < s) & same, 0.0, NEG).astype(np.float32)
    c["negi"] = np.where((s <= t) & same, 0.0, NEG).astype(np.float32)
    c["ones"] = np.ones((128, 128), np.float32)
    c["nones"] = -np.ones((128, 128), np.float32)
    c["iota1"] = np.broadcast_to(np.arange(1, 129, dtype=np.float32)[None, :], (128, 128)).copy()
    return np.stack([c[n] for n in CN], axis=1)


def build_program(nlayers=4, nt=NT, ns=NS, final=True):
    k = K()
    nc = k.nc

    def din(name, shape):
        return nc.dram_tensor(name, list(shape), F32, kind="ExternalInput").ap()

    def dout(name, shape):
        return nc.dram_tensor(name, list(shape), F32, kind="ExternalOutput").ap()

    D = {}
    D["xT"] = din("xT", [128, 8, SEQ + NS])
    D["consts"] = din("consts", [128, len(CN), 128])
    D["nw"] = din("nw", [128, 4, 8])
    D["fnw"] = din("fnw", [128, 8])
    for i in range(2):
        D[f"wie{i}"] = din(f"wie{i}", [128, 8, 4112])
        D[f"woe{i}"] = din(f"woe{i}", [128, 12, 1024])
        D[f"wgl{i}"] = din(f"wgl{i}", [128, 4, 512])
        D[f"wgu{i}"] = din(f"wgu{i}", [16, 512])
        D[f"bgt{i}"] = din(f"bgt{i}", [1, 512])
        D[f"gnw{i}"] = din(f"gnw{i}", [128, 2])
        D[f"bgf{i}"] = din(f"bgf{i}", [128, 4])
        for nm in ["lre", "lim", "ldt"]:
            D[f"{nm}{i}"] = din(f"{nm}{i}", [128, 16])
        for nm in ["bre", "bim", "cre", "cim"]:
            D[f"{nm}{i}"] = din(f"{nm}{i}", [128, 16, 128])
        D[f"s5d{i}"] = din(f"s5d{i}", [128, 4])
        D[f"bgl{i}"] = din(f"bgl{i}", [128, 4])
        D[f"sgla{i}"] = din(f"sgla{i}", [NS, 4, 128, 256])
        D[f"s5r{i}"] = din(f"s5r{i}", [128, 16, NS])
        D[f"s5i{i}"] = din(f"s5i{i}", [128, 16, NS])
        D[f"wio{i}"] = din(f"wio{i}", [128, 8, 4112])
        D[f"woo{i}"] = din(f"woo{i}", [128, 8, 1024])
        D[f"cvw{i}"] = din(f"cvw{i}", [128, 24, 4])
        D[f"alg{i}"] = din(f"alg{i}", [128, 8])
        D[f"dtb{i}"] = din(f"dtb{i}", [128, 8])
        D[f"dnw{i}"] = din(f"dnw{i}", [128, 1])
        D[f"sdel{i}"] = din(f"sdel{i}", [NS, 8, 128, 128])
        D[f"scv{i}"] = din(f"scv{i}", [128, 24, 3, NS])
    D["smask"] = din("smask", [128, 1])
    O = {}
    O["yT"] = dout("yT", [128, 8, SEQ + NS])
    for i in range(2):
        O[f"ogp{i}"] = dout(f"ogp{i}", [4, 128, 256])
        O[f"ogs{i}"] = dout(f"ogs{i}", [NS, 4, 128, 256])
        O[f"o5rp{i}"] = dout(f"o5rp{i}", [128, 16])
        O[f"o5ip{i}"] = dout(f"o5ip{i}", [128, 16])
        O[f"o5rs{i}"] = dout(f"o5rs{i}", [128, 16, NS])
        O[f"o5is{i}"] = dout(f"o5is{i}", [128, 16, NS])
        O[f"odp{i}"] = dout(f"odp{i}", [8, 128, 128])
        O[f"ods{i}"] = dout(f"ods{i}", [NS, 8, 128, 128])
        O[f"ocp{i}"] = dout(f"ocp{i}", [128, 24, 3])
        O[f"ocs{i}"] = dout(f"ocs{i}", [128, 24, 3, NS])

    def mm(out, lhsT, rhs, R, W, start=True, stop=True):
        k.op("pe", lambda e: e.matmul(out, lhsT=lhsT, rhs=rhs, start=start, stop=stop), R, W)

    def tr(out, in_, ident, R, W):
        k.op("pe", lambda e: e.transpose(out, in_, ident), R, W)

    def act(out, in_, func, R, W, scale=1.0, bias=None, accum=None):
        kw = dict(out=out, in_=in_, func=func, scale=scale)
        if bias is not None:
            kw["bias"] = bias
        if accum is not None:
            kw["accum_out"] = accum
        k.op("act", lambda e: e.activation(**kw), R, W)

    def tt(out, in0, in1, op, R, W, en="dve"):
        k.op(en, lambda e: e.tensor_tensor(out=out, in0=in0, in1=in1, op=op), R, W)

    def ts(out, in0, s1, op0, R, W, s2=None, op1=None, en="dve"):
        if op1 is None:
            k.op(en, lambda e: e.tensor_scalar(out=out, in0=in0, scalar1=s1, scalar2=None, op0=op0), R, W)
        else:
            k.op(en, lambda e: e.tensor_scalar(out=out, in0=in0, scalar1=s1, scalar2=s2, op0=op0, op1=op1), R, W)

    def stt(out, in0, scalar, in1, op0, op1, R, W):
        k.op("dve", lambda e: e.scalar_tensor_tensor(out=out, in0=in0, scalar=scalar, in1=in1, op0=op0, op1=op1), R, W)

    def cp(out, in_, R, W, en="dve"):
        if en == "act":
            k.op("act", lambda e: e.activation(out=out, in_=in_, func=AF.Copy), R, W)
        else:
            k.op(en, lambda e: e.tensor_copy(out=out, in_=in_), R, W)

    def ms(ap, val, W, en="dve"):
        k.op(en, lambda e: e.memset(ap, val), (), W)

    def ld(out, in_, W, en="sp"):
        return k.dma(lambda e: e.dma_start(out=out, in_=in_), (), W, en=en)

    def st(out, in_, R, en="sp"):
        return k.dma(lambda e: e.dma_start(out=out, in_=in_), R, (), en=en, out=True)

    CT = T(k, "CT", [128, len(CN), 128], F32)
    ld(CT[:], D["consts"], [CT])
    ci = {n: j for j, n in enumerate(CN)}

    def C(n):
        return CT[:, ci[n], :]

    onesb = T(k, "onesb", [128, 128], BF16)
    cp(onesb[:], C("ones"), [CT], [onesb])
    identb = T(k, "identb", [128, 128], BF16)
    cp(identb[:], C("ident"), [CT], [identb])
    nw = T(k, "nw", [128, 4, 8], F32)
    ld(nw[:], D["nw"], [nw])
    fnw = T(k, "fnw", [128, 8], F32)
    ld(fnw[:], D["fnw"], [fnw])
    smask = T(k, "smask", [128, 1], F32)
    ld(smask[:], D["smask"], [smask])
    epsc = T(k, "epsc", [128, 1], F32)
    ms(epsc[:], EPS, [epsc])
    onec = T(k, "onec", [128, 1], F32)
    ms(onec[:], 1.0, [onec])

    WIN = T(k, "WIN", [128, 8, 4112], BF16)
    WOUT = T(k, "WOUT", [128, 12, 1024], BF16)
    PS = [T(k, f"PS{j}", [128, 512], F32, "psum") for j in range(7)]
    PSB = T(k, "PSB", [128, 1024], BF16, "psum")
    xt = T(k, "xt", [128, 8, 128], F32)
    xt.parts = [Buf(f"xtp{j_}") for j_ in range(8)]
    sq = T(k, "sq", [128, 8, 128], BF16)
    lnv = T(k, "lnv", [128, 128], F32)
    rstd = T(k, "rstd", [128, 128], F32)
    hn = T(k, "hn", [128, 8, 128], BF16)
    catT = T(k, "catT", [128, 12, 128], BF16)

    def load_weights_bf16(dst, src, nk, ncols):
        if getattr(dst, "parts", None) is None:
            dst.parts = []
        j_ = 0
        for kt in range(nk):
            for c0 in range(0, ncols, 2048):
                c1 = min(ncols, c0 + 2048)
                if j_ >= len(dst.parts):
                    dst.parts.append(Buf(f"part{j_}"))
                k.dma(lambda e, kt=kt, c0=c0, c1=c1: e.dma_start(out=dst[:, kt, c0:c1], in_=src[:, kt, c0:c1]), (), [dst.parts[j_]], en="pool")
                j_ += 1

    XB = [Buf(f"xb{j}") for j in range(NT + NS + 1)]

    def norm_tile(xsrc_ap_fn, xb, L, padded):
        if not padded:
            k.dma(lambda e: e.dma_start(out=xt[:], in_=xsrc_ap_fn), [xb], [xt])
        else:
            ms(xt[:], 0.0, [xt])
            k.dma(lambda e: e.dma_start(out=xt[:, :, 0:1], in_=xsrc_ap_fn, allow_slow_non_contiguous=True), [xb], [xt])
        act(sq[:], xt[:], AF.Square, [xt], [sq])
        for kt in range(8):
            mm(PS[0][:, 0:128], onesb[:], sq[:, kt, :], [onesb, sq], [PS[0]], start=(kt == 0), stop=(kt == 7))
        act(lnv[:], PS[0][:, 0:128], AF.Ln, [PS[0], epsc], [lnv], scale=1.0 / 1024.0, bias=epsc[:])
        act(rstd[:], lnv[:], AF.Exp, [lnv], [rstd], scale=-0.5)
        for kt in range(8):
            stt(hn[:, kt, :], xt[:, kt, :], nw[:, L, kt:kt + 1], rstd[:], ALU.mult, ALU.mult, [xt, nw, rstd], [hn])

    def out_proj_add(nft, xdst_fn, xb, padded):
        for dm in range(8):
            P_ = PS[dm % 4]
            for ft in range(nft):
                mm(P_[:, 0:128], WOUT[:, ft, dm * 128:(dm + 1) * 128], catT[:, ft, :], [WOUT, catT], [P_],
                   start=(ft == 0), stop=(ft == nft - 1))
            tt(xt[:, dm, :], xt[:, dm, :], P_[:, 0:128], ALU.add, [xt.parts[dm], P_], [xt.parts[dm]])
        if padded:
            k.dma(lambda e: e.dma_start(out=xdst_fn, in_=xt[:, :, 0:1], allow_slow_non_contiguous=True), [xt], [xb], out=True)
        else:
            k.dma(lambda e: e.dma_start(out=xdst_fn, in_=xt[:]), [xt], [xb], out=True)

    def out_proj_s(nft):
        W = slice(0, NS)
        for dm in range(8):
            for ft in range(nft):
                mm(PS[0][:, W], WOUT[:, ft, dm * 128:(dm + 1) * 128], catT[:, ft, W], [WOUT, catT], [PS[0]],
                   start=(ft == 0), stop=(ft == nft - 1))
            tt(xt[:, dm, W], xt[:, dm, W], PS[0][:, W], ALU.add, [xt, PS[0]], [xt])
        k.dma(lambda e: e.dma_start(out=O["yT"][:, :, SEQ:SEQ + NS], in_=xt[:, :, W]), [xt], [XB[NT]], out=True)

    NFW = 16010
    NBW = 18300
    ARF = T(k, "ARF", [128, NFW], F32)
    ARB = T(k, "ARB", [128, NBW], BF16)
    angi = T(k, "angi", [128, 512], I32)

    def make_alloc():
        off = {"f": 0, "b": 0}

        def TT_(name, shape, dt=F32):
            n = int(np.prod(shape[1:]))
            if dt == BF16:
                ar, key, lim_ = ARB, "b", NBW
            else:
                ar, key, lim_ = ARF, "f", NFW
            o = off[key]
            off[key] += n
            assert off[key] <= lim_, (name, key, off[key])
            ap = ar.t[0:shape[0], o:o + n]
            if len(shape) == 3:
                ap = ap.rearrange("p (a b) -> p a b", a=shape[1])
            return V(ap, name)
        return TT_

    pref = {"L": -1}

    def prefetch_win(Lnext):
        if Lnext >= nlayers:
            return
        src = D[f"wie{Lnext // 2}"] if Lnext % 2 == 0 else D[f"wio{Lnext // 2}"]
        load_weights_bf16(WIN, src, 8, 4112)
        pref["L"] = Lnext

    def even_layer(L, i):
        k.barrier()
        if pref["L"] != L:
            load_weights_bf16(WIN, D[f"wie{i}"], 8, 4112)
        load_weights_bf16(WOUT, D[f"woe{i}"], 12, 1024)
        if True:
            TT_ = make_alloc()
            wgl = TT_("wgl", [128, 4, 512], BF16)
            load_weights_bf16(wgl, D[f"wgl{i}"], 4, 512)
            wgu = TT_("wgu", [16, 512], BF16)
            k.dma(lambda e: e.dma_start(out=wgu[:], in_=D[f"wgu{i}"]), (), [wgu], en="pool")
            bgt = TT_("bgt", [1, 512], BF16)
            k.dma(lambda e: e.dma_start(out=bgt[:], in_=D[f"bgt{i}"]), (), [bgt], en="pool")
            gnw = TT_("gnw", [128, 2])
            ld(gnw[:], D[f"gnw{i}"], [gnw])
            s5d = TT_("s5d", [128, 4])
            ld(s5d[:], D[f"s5d{i}"], [s5d])
            bgl = TT_("bgl", [128, 4])
            ld(bgl[:], D[f"bgl{i}"], [bgl])
            nbgf = TT_("nbgf", [128, 4])
            ld(nbgf[:], D[f"bgf{i}"], [nbgf])
            ts(nbgf[:], nbgf[:], -1.0, ALU.mult, [nbgf], [nbgf])
            bglh = TT_("bglh", [128, 4])
            ts(bglh[:], bgl[:], 0.5, ALU.mult, [bgl], [bglh])
            Bre = TT_("Bre", [128, 16, 128], BF16)
            Bim = TT_("Bim", [128, 16, 128], BF16)
            load_weights_bf16(Bre, D[f"bre{i}"], 16, 128)
            load_weights_bf16(Bim, D[f"bim{i}"], 16, 128)
            Cwr = TT_("Cwr", [128, 16, 128], BF16)
            Cwi = TT_("Cwi", [128, 16, 128], BF16)
            cst = TT_("cst", [128, 4, 128]); otmp_pre = TT_("otmp_pre", [128, 128])
            lre = TT_("lre", [128, 16]); lim = TT_("lim", [128, 16]); ldt = TT_("ldt", [128, 16])
            ld(lre[:], D[f"lre{i}"], [lre]); ld(lim[:], D[f"lim{i}"], [lim]); ld(ldt[:], D[f"ldt{i}"], [ldt])
            dt_ = TT_("dt_", [128, 16]); rho = TT_("rho", [128, 16]); th = TT_("th", [128, 16])
            act(dt_[:], ldt[:], AF.Exp, [ldt], [dt_])
            t1 = TT_("t1", [128, 16]); t2 = TT_("t2", [128, 16])
            tt(t1[:], lre[:], dt_[:], ALU.mult, [lre, dt_], [t1])
            act(rho[:], t1[:], AF.Exp, [t1], [rho])
            tt(th[:], lim[:], dt_[:], ALU.mult, [lim, dt_], [th])
            COS = TT_("COS", [128, 16, 128]); SIN = TT_("SIN", [128, 16, 128])
            ang = TT_("ang", [128, 128])
            zr = TT_("zr", [128, 256]); zi = TT_("zi", [128, 256]); ta = TT_("ta", [128, 256]); tb = TT_("tb", [128, 256])
            angf = V(ta.ap[:, 0:128], "angf"); angf.b = ta.b
            angd = V(tb.ap[:, 0:128], "angd"); angd.b = tb.b
            gr = TT_("gr", [128, 256]); gi = TT_("gi", [128, 256])

            def sin_chain(dst3, a2, tf, td, ti, shift):
                ops = []
                ops.append(lambda: ts(tf[:], a2[:], 1.0 / (2 * np.pi), ALU.mult, [a2], [tf], s2=shift / (2 * np.pi), op1=ALU.add))
                ops.append(lambda: cp(ti, tf[:], [tf], [angi]))
                ops.append(lambda: cp(td[:], ti, [angi], [td]))
                ops.append(lambda: tt(tf[:], tf[:], td[:], ALU.subtract, [tf, td], [tf]))
                ops.append(lambda: ts(td[:], tf[:], 0.5, ALU.is_gt, [tf], [td]))
                ops.append(lambda: tt(tf[:], tf[:], td[:], ALU.subtract, [tf, td], [tf]))
                ops.append(lambda: ts(td[:], tf[:], -0.5, ALU.is_lt, [tf], [td]))
                ops.append(lambda: tt(tf[:], tf[:], td[:], ALU.add, [tf, td], [tf]))
                ops.append(lambda: act(dst3, tf[:].rearrange("p (a b) -> p a b", a=2), AF.Sin, [tf], [COS, SIN], scale=2 * np.pi * 0.999999))
                return ops

            for g2 in range(8):
                for q_ in range(2):
                    gp = g2 * 2 + q_
                    ts(zr[:, q_ * 128:(q_ + 1) * 128], C("iota1"), th[:, gp:gp + 1], ALU.mult, [CT, th], [zr])
                ca = sin_chain(SIN[:, g2 * 2:(g2 + 1) * 2, :], zr, zi, ta, angi[:, 0:256], 0.0)
                cb = sin_chain(COS[:, g2 * 2:(g2 + 1) * 2, :], zr, tb, gr, angi[:, 256:512], np.pi / 2)
                for oa, ob in zip(ca, cb):
                    oa()
                    ob()
            ar = TT_("ar", [128, 16]); ai = TT_("ai", [128, 16])
            tt(ar[:], rho[:], COS[:, :, 0], ALU.mult, [rho, COS], [ar])
            tt(ai[:], rho[:], SIN[:, :, 0], ALU.mult, [rho, SIN], [ai])
            den = TT_("den", [128, 16]); wr = TT_("wr", [128, 16]); wi = TT_("wi", [128, 16]); am1 = TT_("am1", [128, 16])
            tt(t1[:], lre[:], lre[:], ALU.mult, [lre], [t1])
            tt(t2[:], lim[:], lim[:], ALU.mult, [lim], [t2])
            tt(den[:], t1[:], t2[:], ALU.add, [t1, t2], [den])
            k.op("dve", lambda e: e.reciprocal(out=den[:], in_=den[:]), [den], [den])
            ts(am1[:], ar[:], -1.0, ALU.add, [ar], [am1])
            tt(t1[:], am1[:], lre[:], ALU.mult, [am1, lre], [t1])
            tt(t2[:], ai[:], lim[:], ALU.mult, [ai, lim], [t2])
            tt(t1[:], t1[:], t2[:], ALU.add, [t1, t2], [t1])
            tt(wr[:], t1[:], den[:], ALU.mult, [t1, den], [wr])
            tt(t1[:], ai[:], lre[:], ALU.mult, [ai, lre], [t1])
            tt(t2[:], am1[:], lim[:], ALU.mult, [am1, lim], [t2])
            tt(t1[:], t1[:], t2[:], ALU.subtract, [t1, t2], [t1])
            tt(wi[:], t1[:], den[:], ALU.mult, [t1, den], [wi])
            iwr = TT_("iwr", [128, 16]); iwi = TT_("iwi", [128, 16])
            tt(t1[:], wr[:], wr[:], ALU.mult, [wr], [t1])
            tt(t2[:], wi[:], wi[:], ALU.mult, [wi], [t2])
            tt(t1[:], t1[:], t2[:], ALU.add, [t1, t2], [t1])
            k.op("dve", lambda e: e.reciprocal(out=t1[:], in_=t1[:]), [t1], [t1])
            tt(iwr[:], wr[:], t1[:], ALU.mult, [wr, t1], [iwr])
            tt(iwi[:], wi[:], t1[:], ALU.mult, [wi, t1], [iwi])
            ts(iwi[:], iwi[:], -1.0, ALU.mult, [iwi], [iwi])
            cstB = [Buf("cstA"), Buf("cstB")]
            for gp in range(16):
                q_ = gp % 2
                cb_ = cstB[q_]
                tA, tB = (ang, angf) if q_ == 0 else (angd, otmp_pre)
                ld(cst[:, 2 * q_, :], D[f"cre{i}"][:, gp, :], [cb_])
                ld(cst[:, 2 * q_ + 1, :], D[f"cim{i}"][:, gp, :], [cb_])
                ts(tA[:], cst[:, 2 * q_ + 1, :], wi[:, gp:gp + 1], ALU.mult, [cb_, wi], [tA])
                stt(Cwr[:, gp, :], cst[:, 2 * q_, :], wr[:, gp:gp + 1], tA[:], ALU.mult, ALU.subtract, [cb_, wr, tA], [Cwr])
                ts(tA[:], cst[:, 2 * q_ + 1, :], wr[:, gp:gp + 1], ALU.mult, [cb_, wr], [tA])
                stt(tB[:], cst[:, 2 * q_, :], wi[:, gp:gp + 1], tA[:], ALU.mult, ALU.add, [cb_, wi, tA], [tB])
                ts(Cwi[:, gp, :], tB[:], -1.0, ALU.mult, [tB], [Cwi])
            RHO = TT_("RHO", [128, 16, 128])
            for gp in range(16):
                ts(RHO[:, gp, :], C("ones"), rho[:, gp:gp + 1], ALU.mult, [CT, rho], [RHO])

            S = TT_("S", [128, 4, 256]); Sbf = TT_("Sbf", [128, 4, 256], BF16)
            hpr = TT_("hpr", [128, 16]); hpi = TT_("hpi", [128, 16])
            hs0 = TT_("hs0", [128, 16]); hs1 = TT_("hs1", [128, 16]); hs2 = TT_("hs2", [128, 16]); hs3 = TT_("hs3", [128, 16])
            gT = TT_("gT", [128, 8, 128]); uT = TT_("uT", [128, 4, 128]); uTb = TT_("uTb", [128, 4, 128], BF16)
            sgs = TT_("sgs", [128, 4, 128]); lrT = TT_("lrT", [16, 128], BF16)
            vtok = TT_("vtok", [128, 1024], BF16)
            sp = TT_("sp", [128, 512]); E1 = TT_("E1", [128, 4, 128]); E2 = TT_("E2", [128, 4, 128])
            E3 = TT_("E3", [128, 512]); qs = TT_("qs", [128, 4, 128], BF16); ks = TT_("ks", [128, 4, 128], BF16)
            khat = TT_("khat", [128, 512], BF16); ATm = TT_("ATm", [128, 128], BF16)
            sqo = TT_("sqo", [128, 2, 128], BF16); otmp = TT_("otmp", [128, 128])
            otr = [otmp] + [TT_(f"otmp{q_}", [128, 128]) for q_ in range(1, 4)]
            ATm2 = TT_("ATm2", [128, 128], BF16); sqo2 = TT_("sqo2", [128, 2, 128], BF16)
            Sh4 = [V(S.ap[:, h_, :], f"S4_{h_}") for h_ in range(4)]
            Sbfh4 = [V(Sbf.ap[:, h_, :], f"Sbf4_{h_}") for h_ in range(4)]
            otc = {"n": 0}

            def nxt_ot():
                otc["n"] += 1
                return otr[otc["n"] % 4]
            hrb4 = TT_("hrb4", [128, 4, 128], BF16); hib4 = TT_("hib4", [128, 4, 128], BF16)
            ta2 = TT_("ta2", [128, 256]); tb2 = TT_("tb2", [128, 256])
            hr2 = TT_("hr2", [128, 256]); hi2 = TT_("hi2", [128, 256])
            y5 = TT_("y5", [128, 4, 128]); y5b = TT_("y5b", [128, 4, 128], BF16)
            onesrow = TT_("onesrow", [1, 128], BF16)
            ms(onesrow[:], 1.0, [onesrow])
            ones1 = TT_("ones1", [128, 1]); ms(ones1[:], 1.0, [ones1])

            def run_tile(xsrc, xdst, xb, padded, sidx):
                norm_tile(xsrc, xb, L, padded)
                if padded:
                    for h in range(4):
                        ld(S[:, h, :], D[f"sgla{i}"][sidx, h], [S])
                    cp(Sbf[:], S[:], [S], [Sbf], en="act")
                    ld(hs0[:], D[f"s5r{i}"][sidx], [hs0])
                    ld(hs1[:], D[f"s5i{i}"][sidx], [hs1])
                    tt(hs2[:], hs0[:], iwr[:], ALU.mult, [hs0, iwr], [hs2])
                    tt(hs3[:], hs1[:], iwi[:], ALU.mult, [hs1, iwi], [hs3])
                    tt(hpr[:], hs2[:], hs3[:], ALU.subtract, [hs2, hs3], [hpr])
                    tt(hs2[:], hs0[:], iwi[:], ALU.mult, [hs0, iwi], [hs2])
                    tt(hs3[:], hs1[:], iwr[:], ALU.mult, [hs1, iwr], [hs3])
                    tt(hpi[:], hs2[:], hs3[:], ALU.add, [hs2, hs3], [hpi])

                def proj_f(col0, evac):
                    for kt in range(8):
                        mm(PS[1][:, 0:128], WIN[:, kt, col0:col0 + 128], hn[:, kt, :], [WIN, hn], [PS[1]], start=(kt == 0), stop=(kt == 7))
                    evac(PS[1][:, 0:128])

                if STAGE <= 1:
                    return
                rr = {"n": 0}
                banks = [PS[1], PS[3], PS[4], PS[5]]

                def proj_f(col0, evac):
                    P_ = banks[rr["n"] % 4]
                    rr["n"] += 1
                    for kt in range(8):
                        mm(P_[:, 0:128], WIN[:, kt, col0:col0 + 128], hn[:, kt, :], [WIN, hn], [P_], start=(kt == 0), stop=(kt == 7))
                    evac(P_[:, 0:128], P_)
                for j in range(4):
                    def ev_u(p, P_, j=j):
                        cp(uT[:, j, :], p, [P_], [uT], en="act")
                        cp(uTb[:, j, :], uT[:, j, :], [uT], [uTb], en="pool")
                    proj_f(3088 + j * 128, ev_u)

                def gen_proj_gla():
                    for kt in range(8):
                        mm(PS[1][0:16, 0:128], WIN[:, kt, 3072:3088], hn[:, kt, :], [WIN, hn], [PS[1]], start=(kt == 0), stop=(kt == 7))
                    cp(lrT[:], PS[1][0:16, 0:128], [PS[1]], [lrT])
                    yield
                    mm(PS[2][:, :], lrT[:], wgu[:], [lrT, wgu], [PS[2]], start=True, stop=False)
                    mm(PS[2][:, :], onesrow[:], bgt[:], [onesrow, bgt], [PS[2]], start=False, stop=True)
                    act(E3[:], PS[2][:, :], AF.Exp, [PS[2]], [E3], scale=-1.0)
                    yield
                    act(sp[:], E3[:], AF.Ln, [E3, ones1], [sp], bias=ones1[:])
                    yield
                    for h in range(4):
                        mm(PS[3][:, h * 128:(h + 1) * 128], sp[:, h * 128:(h + 1) * 128], C("triuc"), [sp, CT], [PS[3]])
                    mm(PS[2][:, :], C("c3"), sp[:], [CT, sp], [PS[2]])
                    act(E1[:], PS[3][:, :].rearrange("p (a b) -> p a b", a=4), AF.Exp, [PS[3]], [E1])
                    yield
                    act(E2[:], PS[3][:, :].rearrange("p (a b) -> p a b", a=4), AF.Exp, [PS[3]], [E2], scale=-1.0)
                    act(E3[:], PS[2][:, :], AF.Exp, [PS[2]], [E3])
                    yield
                    for h in range(4):
                        proj_f(h * 128, lambda p, P_, h=h: stt(qs[:, h, :], p, 128.0 ** -0.5, E1[:, h, :], ALU.mult, ALU.mult, [P_, E1], [qs]))
                        proj_f(512 + h * 128, lambda p, P_, h=h: tt(ks[:, h, :], p, E2[:, h, :], ALU.mult, [P_, E2], [ks]))
                        yield
                    for j in range(8):
                        proj_f(2048 + j * 128, lambda p, P_, j=j, : (lambda o_: (act(o_[:], p, AF.Tanh, [P_], [o_], scale=0.5), stt(gT[:, j, :], o_[:], 1.0, p, ALU.add, ALU.mult, [o_, P_], [gT])))(nxt_ot()))
                        ts(gT[:, j, :], gT[:, j, :], gnw[:, (j % 2):(j % 2) + 1], ALU.mult, [gT, gnw], [gT], s2=0.5, op1=ALU.mult, en="pool")
                        if j % 2 == 1:
                            yield
                    for j in range(4):
                        proj_f(3600 + j * 128, lambda p, P_, j=j: (lambda o_: (act(o_[:], p, AF.Tanh, [P_], [o_], scale=0.5), stt(sgs[:, j, :], o_[:], 1.0, p, ALU.add, ALU.mult, [o_, P_], [sgs])))(nxt_ot()))
                        if j % 2 == 1:
                            yield
                    for c in range(2):
                        for kt in range(8):
                            mm(PS[2][:, :], hn[:, kt, :], WIN[:, kt, 1024 + c * 512:1024 + (c + 1) * 512], [hn, WIN], [PS[2]], start=(kt == 0), stop=(kt == 7))
                        cp(vtok[:, c * 512:(c + 1) * 512], PS[2][:, :], [PS[2]], [vtok], en="act")
                        yield
                    for kt in range(8):
                        mm(PS[2][:, :], hn[:, kt, :], WIN[:, kt, 512:1024], [hn, WIN], [PS[2]], start=(kt == 0), stop=(kt == 7))
                    tt(khat[:], PS[2][:, :], E3[:], ALU.mult, [PS[2], E3], [khat])
                    yield
                    def gla_gen(par, PA, PB_, ATm_, sqo_, lnv_, rstd_):
                        for h in (par, par + 2):
                            S_h, Sbf_h = Sh4[h], Sbfh4[h]
                            mm(PA[:, 0:128], ks[:, h, :], qs[:, h, :], [ks, qs], [PA])
                            tt(ATm_[:], PA[:, 0:128], C("masku"), ALU.mult, [PA, CT], [ATm_])
                            yield
                            for half in range(2):
                                mm(PB_[:, half * 128:(half + 1) * 128], Sbf_h[:, half * 128:(half + 1) * 128], qs[:, h, :], [Sbf_h, qs], [PB_], start=True, stop=False)
                                mm(PB_[:, half * 128:(half + 1) * 128], vtok[:, h * 256 + half * 128:h * 256 + (half + 1) * 128], ATm_[:], [vtok, ATm_], [PB_], start=False, stop=True)
                            mm(PA[:, 256:512], khat[:, h * 128:(h + 1) * 128], vtok[:, h * 256:(h + 1) * 256], [khat, vtok], [PA])
                            stt(S_h[:, :], S_h[:, :], E1[:, h, 127:128], PA[:, 256:512], ALU.mult, ALU.add, [S_h, E1, PA], [S_h])
                            act(sqo_[:], PB_[:, 0:256].rearrange("p (a b) -> p a b", a=2), AF.Square, [PB_], [sqo_])
                            yield
                            cp(Sbf_h[:, :], S_h[:, :], [S_h], [Sbf_h], en="act")
                            mm(PA[:, 128:256], onesb[:], sqo_[:, 0, :], [onesb, sqo_], [PA], start=True, stop=False)
                            mm(PA[:, 128:256], onesb[:], sqo_[:, 1, :], [onesb, sqo_], [PA], start=False, stop=True)
                            act(lnv_[:], PA[:, 128:256], AF.Ln, [PA, epsc], [lnv_], scale=1.0 / 256.0, bias=epsc[:])
                            yield
                            act(rstd_[:], lnv_[:], AF.Exp, [lnv_], [rstd_], scale=-0.5)
                            yield
                            for half in range(2):
                                o_ = otr[par * 2 + half]
                                tt(o_[:], PB_[:, half * 128:(half + 1) * 128], rstd_[:], ALU.mult, [PB_, rstd_], [o_])
                                tt(catT[:, h * 2 + half, :], o_[:], gT[:, h * 2 + half, :], ALU.mult, [o_, gT], [catT], en="pool")
                            yield
                    subs = [gla_gen(0, PS[3], PS[4], ATm, sqo, lnv, rstd), gla_gen(1, PS[1], PS[2], ATm2, sqo2, otmp_pre, ang)]
                    while subs:
                        for g_ in list(subs):
                            try:
                                next(g_)
                            except StopIteration:
                                subs.remove(g_)
                        yield

                def gen_s5():
                    col = 127
                    v2 = lambda a_: a_[:, :].rearrange("p (a b) -> p a b", a=2)

                    def sA(c2):
                        ut = c2 // 2
                        for q in range(2):
                            gp = c2 * 2 + q
                            mm(PS[0][:, q * 128:(q + 1) * 128], Bre[:, gp, :], uTb[:, ut, :], [Bre, uTb], [PS[0]])
                            mm(PS[6][:, q * 128:(q + 1) * 128], Bim[:, gp, :], uTb[:, ut, :], [Bim, uTb], [PS[6]])
                        cs = COS[:, c2 * 2:(c2 + 1) * 2, :]
                        sn = SIN[:, c2 * 2:(c2 + 1) * 2, :]
                        p5 = PS[0][:, 0:256].rearrange("p (a b) -> p a b", a=2)
                        p6 = PS[6][:, 0:256].rearrange("p (a b) -> p a b", a=2)
                        tt(v2(ta), p5, cs, ALU.mult, [PS[0], COS], [ta])
                        tt(v2(tb), p6, sn, ALU.mult, [PS[6], SIN], [tb])
                        tt(zr[:], ta[:], tb[:], ALU.add, [ta, tb], [zr])
                        tt(v2(ta), p6, cs, ALU.mult, [PS[6], COS], [ta])
                        tt(v2(tb), p5, sn, ALU.mult, [PS[0], SIN], [tb])
                        tt(zi[:], ta[:], tb[:], ALU.subtract, [ta, tb], [zi])

                    def sB(c2):
                        for q in range(2):
                            gp = c2 * 2 + q
                            sl = slice(q * 128, (q + 1) * 128)
                            k.op("dve", lambda e, gp=gp, sl=sl: e.tensor_tensor_scan(out=gr[:, sl], data0=RHO[:, gp, :], data1=zr[:, sl], initial=hpr[:, gp:gp + 1], op0=ALU.mult, op1=ALU.add), [RHO, zr, hpr], [gr])
                            k.op("dve", lambda e, gp=gp, sl=sl: e.tensor_tensor_scan(out=gi[:, sl], data0=RHO[:, gp, :], data1=zi[:, sl], initial=hpi[:, gp:gp + 1], op0=ALU.mult, op1=ALU.add), [RHO, zi, hpi], [gi])

                    def sC(c2):
                        ut = c2 // 2
                        cs = COS[:, c2 * 2:(c2 + 1) * 2, :]
                        sn = SIN[:, c2 * 2:(c2 + 1) * 2, :]
                        tt(v2(ta2), v2(gr), cs, ALU.mult, [gr, COS], [ta2], en="pool")
                        tt(v2(tb2), v2(gi), sn, ALU.mult, [gi, SIN], [tb2], en="pool")
                        tt(hr2[:], ta2[:], tb2[:], ALU.subtract, [ta2, tb2], [hr2])
                        tt(v2(ta2), v2(gr), sn, ALU.mult, [gr, SIN], [ta2], en="pool")
                        tt(v2(tb2), v2(gi), cs, ALU.mult, [gi, COS], [tb2], en="pool")
                        tt(hi2[:], ta2[:], tb2[:], ALU.add, [ta2, tb2], [hi2])
                        hq = c2 % 2
                        cp(hrb4[:, hq * 2:(hq + 1) * 2, :], v2(hr2), [hr2], [hrb4], en="act")
                        cp(hib4[:, hq * 2:(hq + 1) * 2, :], v2(hi2), [hi2], [hib4], en="act")
                        cp(hpr[:, c2 * 2:(c2 + 1) * 2], v2(hr2)[:, :, col], [hr2], [hpr])
                        cp(hpi[:, c2 * 2:(c2 + 1) * 2], v2(hi2)[:, :, col], [hi2], [hpi])

                    def sD(c2):
                        ut = c2 // 2
                        for q in range(4):
                            gp = ut * 4 + q
                            mm(PS[0][:, 256:384], Cwr[:, gp, :], hrb4[:, q, :], [Cwr, hrb4], [PS[0]], start=(q == 0), stop=False)
                            mm(PS[0][:, 256:384], Cwi[:, gp, :], hib4[:, q, :], [Cwi, hib4], [PS[0]], start=False, stop=(q == 3))
                        stt(y5[:, ut, :], uT[:, ut, :], s5d[:, ut:ut + 1], PS[0][:, 256:384], ALU.mult, ALU.add, [uT, s5d, PS[0]], [y5])
                        yv = y5[:, ut, :]
                        tt(ta2[:, 0:128], yv, yv, ALU.mult, [y5], [ta2])
                        ts(ta2[:, 0:128], ta2[:, 0:128], 0.044715, ALU.mult, [ta2], [ta2], s2=1.0, op1=ALU.add)
                        tt(ta2[:, 0:128], ta2[:, 0:128], yv, ALU.mult, [ta2, y5], [ta2])
                        act(tb2[:, 0:128], ta2[:, 0:128], AF.Tanh, [ta2], [tb2], scale=0.7978845608028654)
                        ts(tb2[:, 0:128], tb2[:, 0:128], 1.0, ALU.add, [tb2], [tb2], s2=0.5, op1=ALU.mult)
                        tt(yv, yv, tb2[:, 0:128], ALU.mult, [y5, tb2], [y5])
                        cp(y5b[:, ut, :], y5[:, ut, :], [y5], [y5b], en="act")

                    for it in range(8 + 2):
                        if 0 <= it - 2 < 8:
                            sC(it - 2)
                            yield
                            if (it - 2) % 2 == 1:
                                sD(it - 2)
                                yield
                        if 0 <= it - 1 < 8:
                            sB(it - 1)
                            yield
                        if it < 8:
                            sA(it)
                            yield

                active = [gen_s5(), gen_proj_gla()]
                while active:
                    for g_ in list(active):
                        try:
                            next(g_)
                        except StopIteration:
                            active.remove(g_)
                gbk = [PS[5], PS[6], PS[3], PS[4]]
                for ot in range(4):
                    for kt in range(4):
                        mm(gbk[ot][:, 0:128], wgl[:, kt, ot * 128:(ot + 1) * 128], y5b[:, kt, :], [wgl, y5b], [gbk[ot]], start=(kt == 0), stop=(kt == 3))
                for ot in range(4):
                    act(otr[ot][:], gbk[ot][:, 0:128], AF.Tanh, [gbk[ot], bglh], [otr[ot]], scale=0.5, bias=bglh[:, ot:ot + 1])
                for ot in range(4):
                    stt(otr[ot][:], otr[ot][:], 1.0, y5[:, ot, :], ALU.add, ALU.mult, [otr[ot], y5], [otr[ot]])
                for ot in range(4):
                    stt(catT[:, 8 + ot, :], otr[ot][:], 0.25, sgs[:, ot, :], ALU.mult, ALU.mult, [otr[ot], sgs], [catT])
                out_proj_add(12, xdst, xb, padded)

            for h_ in range(4):
                ms(Sh4[h_][:, :], 0.0, [Sh4[h_]]); ms(Sbfh4[h_][:, :], 0.0, [Sbfh4[h_]])
            ms(hpr[:], 0.0, [hpr]); ms(hpi[:], 0.0, [hpi])
            xin = D["xT"] if L == 0 else O["yT"]
            for j in range(nt):
                run_tile(xin[:, :, j * 128:(j + 1) * 128], O["yT"][:, :, j * 128:(j + 1) * 128], XB[j], False, None)
            for h in range(4):
                st(O[f"ogp{i}"][h], Sh4[h][:, :], [Sh4[h]])

            def out_state(dre, dim):
                tt(hs0[:], hpr[:], wr[:], ALU.mult, [hpr, wr], [hs0])
                tt(hs1[:], hpi[:], wi[:], ALU.mult, [hpi, wi], [hs1])
                tt(hs2[:], hs0[:], hs1[:], ALU.subtract, [hs0, hs1], [hs2])
                tt(hs0[:], hpr[:], wi[:], ALU.mult, [hpr, wi], [hs0])
                tt(hs1[:], hpi[:], wr[:], ALU.mult, [hpi, wr], [hs1])
                tt(hs3[:], hs0[:], hs1[:], ALU.add, [hs0, hs1], [hs3])
                st(dre, hs2[:], [hs2])
                st(dim, hs3[:], [hs3])
            out_state(O[f"o5rp{i}"], O[f"o5ip{i}"])
            def even_samples():
                k.barrier()
                W = slice(0, NS)
                v16 = lambda ap: ap.rearrange("p (a b) -> p a b", a=16)
                bc = lambda t_: t_[:, :].unsqueeze(2).to_broadcast([128, 16, NS])
                k.dma(lambda e: e.dma_start(out=xt[:, :, W], in_=xin[:, :, SEQ:SEQ + NS]), [XB[NT]], [xt])
                act(sq[:, :, W], xt[:, :, W], AF.Square, [xt], [sq])
                for kt in range(8):
                    mm(PS[0][:, W], onesb[:], sq[:, kt, W], [onesb, sq], [PS[0]], start=(kt == 0), stop=(kt == 7))
                act(lnv[:, W], PS[0][:, W], AF.Ln, [PS[0], epsc], [lnv], scale=1.0 / 1024.0, bias=epsc[:])
                act(rstd[:, W], lnv[:, W], AF.Exp, [lnv], [rstd], scale=-0.5)
                for kt in range(8):
                    stt(hn[:, kt, W], xt[:, kt, W], nw[:, L, kt:kt + 1], rstd[:, W], ALU.mult, ALU.mult, [xt, nw, rstd], [hn])

                sb_ = [PS[1], PS[3], PS[5], PS[6]]
                sc_ = {"n": 0}

                def proj_s(col0, evac):
                    P_ = sb_[sc_["n"] % 4]
                    sc_["n"] += 1
                    for kt in range(8):
                        mm(P_[:, W], WIN[:, kt, col0:col0 + 128], hn[:, kt, W], [WIN, hn], [P_], start=(kt == 0), stop=(kt == 7))
                    evac(P_[:, W], P_)
                for kt in range(8):
                    mm(PS[1][0:16, W], WIN[:, kt, 3072:3088], hn[:, kt, W], [WIN, hn], [PS[1]], start=(kt == 0), stop=(kt == 7))
                cp(lrT[:, W], PS[1][0:16, W], [PS[1]], [lrT])
                for h in range(4):
                    mm(PS[2][:, h * 16:(h + 1) * 16], wgu[:, h * 128:(h + 1) * 128], lrT[:, W], [wgu, lrT], [PS[2]])
                    act(E1[:, h, W], PS[2][:, h * 16:(h + 1) * 16], AF.Exp, [PS[2], nbgf], [E1], scale=-1.0, bias=nbgf[:, h:h + 1])
                for h in range(4):
                    act(E1[:, h, W], E1[:, h, W], AF.Ln, [E1, ones1], [E1], bias=ones1[:])
                    act(E1[:, h, W], E1[:, h, W], AF.Exp, [E1], [E1], scale=-1.0 / 16.0)
                for h in range(4):
                    proj_s(h * 128, lambda p, P_, h=h: ts(E2[:, h, W], p, 128.0 ** -0.5, ALU.mult, [P_], [E2]))
                    proj_s(512 + h * 128, lambda p, P_, h=h: cp(E3[:, h * 16:(h + 1) * 16], p, [P_], [E3]))
                for j in range(8):
                    proj_s(2048 + j * 128, lambda p, P_, j=j: (lambda o_: (act(o_[:, W], p, AF.Tanh, [P_], [o_], scale=0.5), stt(gT[:, j, W], o_[:, W], 1.0, p, ALU.add, ALU.mult, [o_, P_], [gT])))(nxt_ot()))
                    ts(gT[:, j, W], gT[:, j, W], gnw[:, (j % 2):(j % 2) + 1], ALU.mult, [gT, gnw], [gT], s2=0.5, op1=ALU.mult)
                for j in range(4):
                    def ev_u(p, P_, j=j):
                        cp(uT[:, j, W], p, [P_], [uT], en="act")
                        cp(uTb[:, j, W], uT[:, j, W], [uT], [uTb])
                    proj_s(3088 + j * 128, ev_u)
                    proj_s(3600 + j * 128, lambda p, P_, j=j: (lambda o_: (act(o_[:, W], p, AF.Tanh, [P_], [o_], scale=0.5), stt(sgs[:, j, W], o_[:, W], 1.0, p, ALU.add, ALU.mult, [o_, P_], [sgs])))(nxt_ot()))
                vs2 = RHO[0:NS, 0:8, :].rearrange("p a b -> p (a b)")
                for c in range(2):
                    for kt in range(8):
                        mm(PS[2][0:NS, :], hn[:, kt, W], WIN[:, kt, 1024 + c * 512:1024 + (c + 1) * 512], [hn, WIN], [PS[2]], start=(kt == 0), stop=(kt == 7))
                    cp(vs2[:, c * 512:(c + 1) * 512], PS[2][0:NS, :], [PS[2]], [RHO])
                prefetch_win(L + 1)
                Ss = [V(S.ap[:, j, :], f"Ss{j}") for j in range(4)]
                sels = [V(RHO.ap[0:NS, 8 + q_, :], f"sel{q_}") for q_ in range(4)]
                gbanks = [PS[0], PS[1], PS[2], PS[6]]
                zzs = [zr, zi, ta, tb]

                def gla_stream(q_):
                    h = q_
                    P_ = gbanks[q_]; zz = zzs[q_]; selq = sels[q_]; Sj = Ss[q_]
                    for s in range(NS):
                        ts(selq[:, :], C("ones")[0:NS, :], C("ident")[0:NS, s:s + 1], ALU.mult, [CT], [selq], en="pool")
                        ld(Sj[:, :], D[f"sgla{i}"][s, h], [Sj])
                        mm(P_[:, 0:256], selq[:, :], vs2[:, h * 256:(h + 1) * 256], [selq, RHO], [P_])
                        yield
                        ts(zz[:], P_[:, 0:256], E3[:, h * 16 + s:h * 16 + s + 1], ALU.mult, [P_, E3], [zz])
                        yield
                        stt(Sj[:, :], Sj[:, :], E1[:, h, s:s + 1], zz[:], ALU.mult, ALU.add, [Sj, E1, zz], [Sj])
                        yield
                        for half in range(2):
                            c_ = (h * 2 + half) * 16 + s
                            mm(PS[4][:, c_:c_ + 1], Sj[:, half * 128:(half + 1) * 128], E2[:, h, s:s + 1], [Sj, E2], [PS[4]])
                        st(O[f"ogs{i}"][s, h], Sj[:, :], [Sj])
                        yield
                active = [gla_stream(q_) for q_ in range(4)]
                while active:
                    for g_ in list(active):
                        try:
                            next(g_)
                        except StopIteration:
                            active.remove(g_)
                act(sqo[:, 0, :], PS[4][:, 0:128], AF.Square, [PS[4]], [sqo])
                for h in range(4):
                    mm(PS[3][:, h * 16:(h + 1) * 16], onesb[:], sqo[:, 0, (h * 2) * 16:(h * 2 + 1) * 16], [onesb, sqo], [PS[3]], start=True, stop=False)
                    mm(PS[3][:, h * 16:(h + 1) * 16], onesb[:], sqo[:, 0, (h * 2 + 1) * 16:(h * 2 + 2) * 16], [onesb, sqo], [PS[3]], start=False, stop=True)
                act(lnv[:, 0:64], PS[3][:, 0:64], AF.Ln, [PS[3], epsc], [lnv], scale=1.0 / 256.0, bias=epsc[:])
                act(rstd[:, 0:64], lnv[:, 0:64], AF.Exp, [lnv], [rstd], scale=-0.5)
                for h in range(4):
                    for half in range(2):
                        j = h * 2 + half
                        tt(otmp[:, W], PS[4][:, j * 16:(j + 1) * 16], rstd[:, h * 16:(h + 1) * 16], ALU.mult, [PS[4], rstd], [otmp])
                        tt(catT[:, j, W], otmp[:, W], gT[:, j, W], ALU.mult, [otmp, gT], [catT])
                ld(gr[:, :], D[f"s5r{i}"].rearrange("p a b -> p (a b)"), [gr])
                ld(gi[:, :], D[f"s5i{i}"].rearrange("p a b -> p (a b)"), [gi])
                for gp in range(16):
                    mm(PS[5][:, gp * 16:(gp + 1) * 16], Bre[:, gp, :], uTb[:, gp // 4, W], [Bre, uTb], [PS[5]])
                    mm(PS[6][:, gp * 16:(gp + 1) * 16], Bim[:, gp, :], uTb[:, gp // 4, W], [Bim, uTb], [PS[6]])

                def cmul(dr, di, xr, xi, cr, ci):
                    tt(v16(ta[:, :]), v16(xr[:, :]), bc(cr), ALU.mult, [xr, cr], [ta])
                    tt(v16(tb[:, :]), v16(xi[:, :]), bc(ci), ALU.mult, [xi, ci], [tb])
                    tt(dr[:, :], ta[:, :], tb[:, :], ALU.subtract, [ta, tb], [dr])
                    tt(v16(ta[:, :]), v16(xr[:, :]), bc(ci), ALU.mult, [xr, ci], [ta])
                    tt(v16(tb[:, :]), v16(xi[:, :]), bc(cr), ALU.mult, [xi, cr], [tb])
                    tt(di[:, :], ta[:, :], tb[:, :], ALU.add, [ta, tb], [di])
                cmul(zr, zi, gr, gi, iwr, iwi)
                cmul(gr, gi, zr, zi, ar, ai)
                tt(gr[:, :], gr[:, :], PS[5][:, 0:256], ALU.add, [gr, PS[5]], [gr])
                tt(gi[:, :], gi[:, :], PS[6][:, 0:256], ALU.add, [gi, PS[6]], [gi])
                hrv = hrb4[:, 0:2, :].rearrange("p a b -> p (a b)")
                hiv = hib4[:, 0:2, :].rearrange("p a b -> p (a b)")
                cp(hrv, gr[:, :], [gr], [hrb4], en="act")
                cp(hiv, gi[:, :], [gi], [hib4], en="act")
                cmul(zr, zi, gr, gi, wr, wi)
                st(O[f"o5rs{i}"].rearrange("p a b -> p (a b)"), zr[:, :], [zr])
                st(O[f"o5is{i}"].rearrange("p a b -> p (a b)"), zi[:, :], [zi])
                for ut in range(4):
                    for q in range(4):
                        gp = ut * 4 + q
                        mm(PS[3][:, 256 + ut * 16:256 + (ut + 1) * 16], Cwr[:, gp, :], hrv[:, gp * 16:(gp + 1) * 16], [Cwr, hrb4], [PS[3]], start=(q == 0), stop=False)
                        mm(PS[3][:, 256 + ut * 16:256 + (ut + 1) * 16], Cwi[:, gp, :], hiv[:, gp * 16:(gp + 1) * 16], [Cwi, hib4], [PS[3]], start=False, stop=(q == 3))
                    stt(y5[:, ut, W], uT[:, ut, W], s5d[:, ut:ut + 1], PS[3][:, 256 + ut * 16:256 + (ut + 1) * 16], ALU.mult, ALU.add, [uT, s5d, PS[3]], [y5])
                for ut in range(4):
                    yv = y5[:, ut, W]
                    tt(ta[:, W], yv, yv, ALU.mult, [y5], [ta])
                    ts(ta[:, W], ta[:, W], 0.044715, ALU.mult, [ta], [ta], s2=1.0, op1=ALU.add)
                    tt(ta[:, W], ta[:, W], yv, ALU.mult, [ta, y5], [ta])
                    act(tb[:, W], ta[:, W], AF.Tanh, [ta], [tb], scale=0.7978845608028654)
                    ts(tb[:, W], tb[:, W], 1.0, ALU.add, [tb], [tb], s2=0.5, op1=ALU.mult)
                    tt(yv, yv, tb[:, W], ALU.mult, [y5, tb], [y5])
                    cp(y5b[:, ut, W], y5[:, ut, W], [y5], [y5b], en="act")
                gbk = [PS[5], PS[6], PS[1], PS[2]]
                for ot in range(4):
                    for kt in range(4):
                        mm(gbk[ot][:, W], wgl[:, kt, ot * 128:(ot + 1) * 128], y5b[:, kt, W], [wgl, y5b], [gbk[ot]], start=(kt == 0), stop=(kt == 3))
                for ot in range(4):
                    act(otr[ot][:, W], gbk[ot][:, W], AF.Tanh, [gbk[ot], bglh], [otr[ot]], scale=0.5, bias=bglh[:, ot:ot + 1])
                for ot in range(4):
                    stt(otr[ot][:, W], otr[ot][:, W], 1.0, y5[:, ot, W], ALU.add, ALU.mult, [otr[ot], y5], [otr[ot]])
                for ot in range(4):
                    stt(catT[:, 8 + ot, W], otr[ot][:, W], 0.25, sgs[:, ot, W], ALU.mult, ALU.mult, [otr[ot], sgs], [catT])
                out_proj_s(12)
            if ns > 0:
                even_samples()


    def odd_layer(L, i):
        k.barrier()
        if pref["L"] != L:
            load_weights_bf16(WIN, D[f"wio{i}"], 8, 4112)
        load_weights_bf16(WOUT, D[f"woo{i}"], 8, 1024)
        if True:
            TT_ = make_alloc()
            cvw = TT_("cvw", [128, 24, 4]); ld(cvw[:], D[f"cvw{i}"], [cvw])
            alg = TT_("alg", [128, 8]); ld(alg[:], D[f"alg{i}"], [alg])
            dtb = TT_("dtb", [128, 8]); ld(dtb[:], D[f"dtb{i}"], [dtb])
            dnw = TT_("dnw", [128, 1]); ld(dnw[:], D[f"dnw{i}"], [dnw])
            ts(dnw[:], dnw[:], 0.5, ALU.mult, [dnw], [dnw])
            nea = TT_("nea", [128, 8])
            act(nea[:], alg[:], AF.Exp, [alg], [nea])
            ts(nea[:], nea[:], -1.0, ALU.mult, [nea], [nea])
            ones1 = TT_("ones1", [128, 1]); ms(ones1[:], 1.0, [ones1])
            S = TT_("S", [128, 8, 128]); Sbf = TT_("Sbf", [128, 8, 128], BF16)
            cbuf = TT_("cbuf", [128, 24, 131])
            acc = TT_("acc", [128, 24, 128]); acthr = TT_("acthr", [128, 1152])
            acth = V(acthr.ap[:, 0:1152].rearrange("p (a b) -> p a b", a=24), "acth"); acth.b = acthr.b
            acth8 = V(acthr.ap[:, 0:1024].rearrange("p (a b) -> p a b", a=8), "acth8"); acth8.b = acthr.b
            zs = TT_("zs", [128, 8, 128])
            ba = TT_("ba", [128, 16])
            beta = TT_("beta", [128, 8]); g = TT_("g", [128, 8]); gam = TT_("gam", [128, 8]); ghat = TT_("ghat", [128, 8])
            GA = TT_("GA", [128, 8]); GB = TT_("GB", [128, 8]); nGb = TT_("nGb", [128, 8]); nbeta = TT_("nbeta", [128, 8])
            sqk = TT_("sqk", [128, 128], BF16); rn = TT_("rn", [128, 128]); l1 = TT_("l1", [128, 128])
            qTb = TT_("qTb", [128, 128], BF16); kTb = TT_("kTb", [128, 128], BF16); vTb = TT_("vTb", [128, 128], BF16)
            ktk = TT_("ktk", [128, 128]); khat = TT_("khat", [128, 128], BF16); vtk = TT_("vtk", [128, 128]); Vb = TT_("Vb", [128, 128])
            gbc = TT_("gbc", [128, 128]); gbcn = TT_("gbcn", [128, 128])
            Dst = TT_("Dst", [128, 128]); DTi = TT_("DTi", [128, 128])
            M0 = TT_("M0", [128, 128]); N0 = TT_("N0", [128, 128]); M1 = TT_("M1", [128, 128]); N1 = TT_("N1", [128, 128])
            TTm = TT_("TTm", [128, 128]); TTb = TT_("TTb", [128, 128], BF16); PT = TT_("PT", [128, 128], BF16)
            Rb = TT_("Rb", [128, 128], BF16); Vn = TT_("Vn", [128, 128], BF16)
            qsg = TT_("qsg", [128, 128]); Osb = TT_("Osb", [128, 128]); Onb = TT_("Onb", [128, 128], BF16)
            ssq = TT_("ssq", [128, 1]); rs1 = TT_("rs1", [128, 1])
            sqr = [sqk] + [TT_(f"sqk{q_}", [128, 128], BF16) for q_ in range(1, 4)]
            l1r = [l1, rn, ktk, vtk]
            vTr = [vTb, TT_("vTb1", [128, 128], BF16)]
            Onr = [Onb, TT_("Onb1", [128, 128], BF16)]
            gbr = [(gbc, gbcn), (Dst, DTi)]
            qTa = TT_("qTa", [128, 8, 128], BF16); kTa = TT_("kTa", [128, 8, 128], BF16)
            kha = TT_("kha", [128, 8, 128], BF16); Vba = TT_("Vba", [128, 8, 128], BF16)
            Dsa = TT_("Dsa", [128, 8, 128], BF16); DTa = TT_("DTa", [128, 8, 128], BF16)
            qTB = [Buf(f"qTB{h_}") for h_ in range(8)]; kTB = [Buf(f"kTB{h_}") for h_ in range(8)]
            khB = [Buf(f"khB{h_}") for h_ in range(8)]; VbB = [Buf(f"VbB{h_}") for h_ in range(8)]
            DsB = [Buf(f"DsB{h_}") for h_ in range(8)]; DTB = [Buf(f"DTB{h_}") for h_ in range(8)]
            HS = [dict(M0=M0, N0=N0, M1=M1, N1=N1, TTm=TTm, qsg=qsg, Osb=Osb, TTb=TTb, PT=PT, Rb=Rb, Vn=Vn, ssq=ssq, rs1=rs1, P=PS[1])]
            for q_ in range(1, 4):
                d_ = {n_: TT_(f"{n_}_{q_}", [128, 128]) for n_ in ["M0", "N0", "M1", "N1", "TTm", "qsg", "Osb"]}
                d_.update({n_: TT_(f"{n_}_{q_}", [128, 128], BF16) for n_ in ["TTb", "PT", "Rb", "Vn"]})
                d_.update(ssq=TT_(f"ssq_{q_}", [128, 1]), rs1=TT_(f"rs1_{q_}", [128, 1]), P=PS[1 + q_])
                HS.append(d_)
            cbp = [Buf(f"cbp{c_}") for c_ in range(24)]
            acp = [Buf(f"acp{c_}") for c_ in range(24)]
            ctmp = TT_("ctmp", [128, 128])
            thb = [TT_(f"thb{q_}", [128, 128]) for q_ in range(4)]
            Sh = [V(S.ap[:, h_, :], f"Sh{h_}") for h_ in range(8)]
            Sbfh = [V(Sbf.ap[:, h_, :], f"Sbfh{h_}") for h_ in range(8)]

            def run_tile(xsrc, xdst, xb, padded, sidx):
                norm_tile(xsrc, xb, L, padded)
                if padded:
                    for h in range(8):
                        ld(S[:, h, :], D[f"sdel{i}"][sidx, h], [S])
                    cp(Sbf[:], S[:], [S], [Sbf], en="act")
                    ld(cbuf[:, :, 0:3], D[f"scv{i}"][sidx], [cbuf])
                def chain_gen():
                    for kt in range(8):
                        mm(PS[2][:, 0:16], hn[:, kt, :], WIN[:, kt, 4096:4112], [hn, WIN], [PS[2]], start=(kt == 0), stop=(kt == 7))
                    cp(ba[:], PS[2][:, 0:16], [PS[2]], [ba])
                    yield
                    act(beta[:], ba[:, 0:8], AF.Tanh, [ba], [beta], scale=0.5)
                    tt(g[:], ba[:, 8:16], dtb[:], ALU.add, [ba, dtb], [g])
                    yield
                    ts(beta[:], beta[:], 0.5, ALU.mult, [beta], [beta], s2=0.5, op1=ALU.add)
                    act(g[:], g[:], AF.Exp, [g], [g])
                    yield
                    act(g[:], g[:], AF.Ln, [g, ones1], [g], bias=ones1[:])
                    ts(nbeta[:], beta[:], -1.0, ALU.mult, [beta], [nbeta])
                    yield
                    tt(g[:], g[:], nea[:], ALU.mult, [g, nea], [g])
                    yield
                    mm(PS[2][:, 0:8], C("trib"), g[:], [CT, g], [PS[2]])
                    mm(PS[2][:, 8:16], C("c3b"), g[:], [CT, g], [PS[2]])
                    mm(PS[2][:, 16:24], C("sela"), g[:], [CT, g], [PS[2]])
                    mm(PS[2][:, 24:32], C("selb"), g[:], [CT, g], [PS[2]])
                    yield
                    act(gam[:], PS[2][:, 0:8], AF.Exp, [PS[2]], [gam])
                    act(ghat[:], PS[2][:, 8:16], AF.Exp, [PS[2]], [ghat])
                    act(GA[:], PS[2][:, 16:24], AF.Exp, [PS[2]], [GA])
                    act(GB[:], PS[2][:, 24:32], AF.Exp, [PS[2]], [GB])
                    yield
                    tt(nGb[:], nbeta[:], gam[:], ALU.mult, [nbeta, gam], [nGb])
                    yield
                chain = chain_gen()
                pb = [PS[1], PS[4], PS[5], PS[6]]

                def stA(ct):
                    P_ = pb[ct % 4]
                    for kt in range(8):
                        mm(P_[:, 0:128], WIN[:, kt, ct * 128:(ct + 1) * 128], hn[:, kt, :], [WIN, hn], [P_], start=(kt == 0), stop=(kt == 7))
                    cp(cbuf[:, ct, 3:131], P_[:, 0:128], [P_], [cbp[ct]], en="act")

                def stBg(gq):
                    cts = [gq * 4 + q_ for q_ in range(4)]
                    for ct in cts:
                        ts(acc[:, ct, :], cbuf[:, ct, 0:128], cvw[:, ct, 0:1], ALU.mult, [cbp[ct], cvw], [acp[ct]])
                    for w_ in range(1, 4):
                        for ct in cts:
                            stt(acc[:, ct, :], cbuf[:, ct, w_:w_ + 128], cvw[:, ct, w_:w_ + 1], acc[:, ct, :], ALU.mult, ALU.add, [cbp[ct], cvw, acp[ct]], [acp[ct]])

                def stCg(gq):
                    for q_ in range(4):
                        ct = gq * 4 + q_
                        act(thb[q_][:], acc[:, ct, :], AF.Tanh, [acp[ct]], [thb[q_]], scale=0.5)

                def stDg(gq):
                    for q_ in range(4):
                        e_ = "pool" if q_ % 2 == 1 else "dve"
                        ts(thb[q_][:], thb[q_][:], 1.0, ALU.add, [thb[q_]], [thb[q_]], s2=0.5, op1=ALU.mult, en=e_)
                    for q_ in range(4):
                        ct = gq * 4 + q_
                        e_ = "pool" if q_ % 2 == 1 else "dve"
                        tt(acc[:, ct, :], acc[:, ct, :], thb[q_][:], ALU.mult, [acp[ct], thb[q_]], [acp[ct]], en=e_)

                for gq in range(6 + 3):
                    next(chain, None)
                    if gq < 6:
                        for q_ in range(4):
                            stA(gq * 4 + q_)
                    if 0 <= gq - 1 < 6:
                        stBg(gq - 1)
                    if 0 <= gq - 3 < 6:
                        stDg(gq - 3)
                    if 0 <= gq - 2 < 6:
                        stCg(gq - 2)
                for j in range(8):
                    P_ = pb[j % 4]
                    for kt in range(8):
                        mm(P_[:, 0:128], WIN[:, kt, 3072 + j * 128:3072 + (j + 1) * 128], hn[:, kt, :], [WIN, hn], [P_], start=(kt == 0), stop=(kt == 7))
                    th_ = thb[j % 2]
                    act(th_[:], P_[:, 0:128], AF.Tanh, [P_], [th_], scale=0.5)
                    stt(zs[:, j, :], th_[:], 1.0, P_[:, 0:128], ALU.add, ALU.mult, [th_, P_], [zs])
                for _ in chain:
                    pass
                if padded:
                    st(O[f"ocs{i}"][sidx], cbuf[:, :, 1:4], [cbuf])
                else:
                    cp(cbuf[:, :, 0:3], cbuf[:, :, 128:131], cbp, cbp)
                l2tiles = [(h_, 0) for h_ in range(8)] + [(h_, 1) for h_ in range(8)]
                l2banks = [PS[1], PS[2]]

                def l2A(t_):
                    h_, isk = l2tiles[t_]
                    ct = 8 * isk + h_
                    sq_ = sqr[t_ % 4]
                    act(sq_[:], acc[:, ct, :], AF.Square, [acp[ct]], [sq_])
                    P_ = l2banks[t_ % 2]
                    r_ = ((t_ // 2) % 4) * 128
                    mm(P_[:, r_:r_ + 128], onesb[:], sq_[:], [onesb, sq_], [P_])

                def l2B(t_):
                    P_ = l2banks[t_ % 2]
                    r_ = ((t_ // 2) % 4) * 128
                    l_ = l1r[t_ % 4]
                    act(l_[:], P_[:, r_:r_ + 128], AF.Ln, [P_, epsc], [l_], bias=epsc[:])

                def l2C(t_):
                    l_ = l1r[t_ % 4]
                    act(l_[:], l_[:], AF.Exp, [l_], [l_], scale=-0.5)

                def l2D(t_):
                    h_, isk = l2tiles[t_]
                    ct = 8 * isk + h_
                    l_ = l1r[t_ % 4]
                    if isk:
                        stt(kTa[:, h_, :], acc[:, ct, :], 1.0, l_[:], ALU.mult, ALU.mult, [acp[ct], l_], [kTB[h_]])
                    else:
                        stt(qTa[:, h_, :], acc[:, ct, :], 128.0 ** -0.5, l_[:], ALU.mult, ALU.mult, [acp[ct], l_], [qTB[h_]])

                for it in range(16 + 3):
                    if it < 16:
                        l2A(it)
                    if 0 <= it - 3 < 16:
                        l2D(it - 3)
                    if 0 <= it - 2 < 16:
                        l2C(it - 2)
                    if 0 <= it - 1 < 16:
                        l2B(it - 1)
                dbanks = [PS[3], PS[4], PS[5], PS[6]]

                def trA(h_):
                    v_ = vTr[h_ % 2]
                    cp(v_[:], acc[:, 16 + h_, :], [acp[16 + h_]], [v_], en="pool")
                    r_ = (h_ % 4) * 256
                    tr(PSB[:, r_:r_ + 128], kTa[:, h_, :], identb[:], [kTB[h_], identb], [PSB])
                    tr(PSB[:, r_ + 128:r_ + 256], v_[:], identb[:], [v_, identb], [PSB])

                def trB(h_):
                    r_ = (h_ % 4) * 256
                    ts(kha[:, h_, :], PSB[:, r_:r_ + 128], ghat[:, h_:h_ + 1], ALU.mult, [PSB, ghat], [khB[h_]])
                    ts(Vba[:, h_, :], PSB[:, r_ + 128:r_ + 256], beta[:, h_:h_ + 1], ALU.mult, [PSB, beta], [VbB[h_]])

                def dcA(h_):
                    gb_, gn_ = gbr[h_ % 2]
                    ts(gb_[:], C("trib"), g[:, h_:h_ + 1], ALU.mult, [CT, g], [gb_], en="pool")
                    ts(gn_[:], gb_[:], -1.0, ALU.mult, [gb_], [gn_], en="pool")

                def dcB(h_):
                    gb_, gn_ = gbr[h_ % 2]
                    P_ = dbanks[h_ % 4]
                    mm(P_[:, 0:128], gb_[:], C("ones"), [gb_, CT], [P_], start=True, stop=False)
                    mm(P_[:, 0:128], C("nones"), gb_[:], [CT, gb_], [P_], start=False, stop=False)
                    mm(P_[:, 0:128], C("ident"), C("negs"), [CT], [P_], start=False, stop=True)
                    mm(P_[:, 128:256], C("ones"), gb_[:], [gb_, CT], [P_], start=True, stop=False)
                    mm(P_[:, 128:256], gn_[:], C("ones"), [CT, gn_], [P_], start=False, stop=False)
                    mm(P_[:, 128:256], C("ident"), C("negi"), [CT], [P_], start=False, stop=True)

                def dcC(h_):
                    P_ = dbanks[h_ % 4]
                    act(Dsa[:, h_, :], P_[:, 0:128], AF.Exp, [P_], [DsB[h_]])
                    act(DTa[:, h_, :], P_[:, 128:256], AF.Exp, [P_], [DTB[h_]])

                for it in range(8 + 2):
                    if it < 8:
                        trA(it)
                        dcA(it)
                    if 0 <= it - 1 < 8:
                        dcB(it - 1)
                        trB(it - 1)
                    if 0 <= it - 2 < 8:
                        dcC(it - 2)

                def head_gen(h, B):
                    M0, N0, M1, N1, TTm, qsg, Osb = (B[n_] for n_ in ["M0", "N0", "M1", "N1", "TTm", "qsg", "Osb"])
                    TTb, PT, Rb, Vn = (B[n_] for n_ in ["TTb", "PT", "Rb", "Vn"])
                    ssq, rs1, P_ = B["ssq"], B["rs1"], B["P"]
                    S_h, Sbf_h = Sh[h], Sbfh[h]
                    kT_, qT_ = kTa[:, h, :], qTa[:, h, :]
                    R0, R1, R2, R3 = (P_[:, 0:128], P_[:, 128:256], P_[:, 256:384], P_[:, 384:512])
                    mm(R2, kT_, kT_, [kTB[h]], [P_])
                    mm(R1, kT_, qT_, [kTB[h], qTB[h]], [P_])
                    stt(M0[:], R2, nbeta[:, h:h + 1], Dsa[:, h, :], ALU.mult, ALU.mult, [P_, nbeta, DsB[h]], [M0])
                    tt(PT[:], R1, DTa[:, h, :], ALU.mult, [P_, DTB[h]], [PT])
                    ms(Rb[:], 0.0, [Rb], en="pool"); ms(Vn[:], 0.0, [Vn], en="pool")
                    yield
                    tr(R3, M0[:], C("ident"), [M0, CT], [P_])
                    cp(N0[:], R3, [P_], [N0], en="act")
                    yield
                    tt(TTm[:], N0[:], C("ident"), ALU.add, [N0, CT], [TTm])
                    bufs_ = [(M0, N0), (M1, N1)]
                    mm(R0, N0[:], M0[:], [N0, M0], [P_])
                    mm(R1, M0[:], N0[:], [M0, N0], [P_])
                    cp(M1[:], R0, [P_], [M1], en="act")
                    cp(N1[:], R1, [P_], [N1], en="act")
                    yield
                    for lvl in range(5):
                        Mn, Nn = bufs_[(lvl + 1) % 2]
                        Mo, No = bufs_[lvl % 2]
                        if lvl < 4:
                            mm(R0, Nn[:], Mn[:], [Nn, Mn], [P_])
                            mm(R1, Mn[:], Nn[:], [Mn, Nn], [P_])
                        mm(R2, Mn[:], TTm[:], [Mn, TTm], [P_])
                        if lvl < 4:
                            cp(Mo[:], R0, [P_], [Mo], en="act")
                            cp(No[:], R1, [P_], [No], en="act")
                        tt(TTm[:], TTm[:], R2, ALU.add, [TTm, P_], [TTm])
                        yield
                    cp(TTb[:], TTm[:], [TTm], [TTb], en="act")
                    yield
                    for blk in range(2):
                        rs = slice(blk * 64, (blk + 1) * 64)
                        Gc = GA if blk == 0 else GB
                        mm(R0, kT_, Sbf_h[:, :], [kTB[h], Sbf_h], [P_])
                        mm(R2, qT_, Sbf_h[:, :], [qTB[h], Sbf_h], [P_])
                        stt(Rb[rs, :], P_[rs, 0:128], nGb[rs, h:h + 1], Vba[rs, h, :], ALU.mult, ALU.add, [P_, nGb, VbB[h]], [Rb])
                        k.op("act", lambda e, rs=rs, h=h: e.activation(out=qsg[rs, :], in_=P_[rs, 256:384], func=AF.Copy, scale=gam[rs, h:h + 1]), [P_, gam], [qsg])
                        yield
                        mm(R1, TTb[:], Rb[:], [TTb, Rb], [P_])
                        cp(Vn[rs, :], P_[rs, 128:256], [P_], [Vn])
                        yield
                        mm(R3, PT[:], Vn[:], [PT, Vn], [P_])
                        mm(R0, kha[rs, h, :], Vn[rs, :], [khB[h], Vn], [P_])
                        tt(Osb[rs, :], qsg[rs, :], P_[rs, 384:512], ALU.add, [qsg, P_], [Osb])
                        stt(S_h[:, :], S_h[:, :], Gc[:, h:h + 1], R0, ALU.mult, ALU.add, [S_h, Gc, P_], [S_h])
                        yield
                        cp(Sbf_h[:, :], S_h[:, :], [S_h], [Sbf_h], en="act")
                        yield
                    act(qsg[:], Osb[:], AF.Square, [Osb], [qsg, ssq], accum=ssq[:])
                    yield
                    act(rs1[:], ssq[:], AF.Ln, [ssq, epsc], [rs1], scale=1.0 / 128.0, bias=epsc[:])
                    act(rs1[:], rs1[:], AF.Exp, [rs1], [rs1], scale=-0.5)
                    yield
                    On_ = Onr[h % 2]
                    ts(On_[:], Osb[:], rs1[:, 0:1], ALU.mult, [Osb, rs1], [On_])
                    r_ = (h % 4) * 256
                    tr(PSB[:, r_:r_ + 128], On_[:], identb[:], [On_, identb], [PSB])
                    stt(catT[:, h, :], PSB[:, r_:r_ + 128], dnw[:, 0:1], zs[:, h, :], ALU.mult, ALU.mult, [PSB, dnw, zs], [catT])
                    yield

                for hq in range(0, 8, 4):
                    active = [head_gen(hq + q_, HS[q_]) for q_ in range(4)]
                    while active:
                        for g_ in list(active):
                            try:
                                next(g_)
                            except StopIteration:
                                active.remove(g_)
                out_proj_add(8, xdst, xb, padded)

            for h_ in range(8):
                ms(Sh[h_][:, :], 0.0, [Sh[h_]]); ms(Sbfh[h_][:, :], 0.0, [Sbfh[h_]])
            ms(cbuf[:], 0.0, cbp)
            for j in range(nt):
                run_tile(O["yT"][:, :, j * 128:(j + 1) * 128], O["yT"][:, :, j * 128:(j + 1) * 128], XB[j], False, None)
            for h in range(8):
                st(O[f"odp{i}"][h], Sh[h][:, :], [Sh[h]])
            st(O[f"ocp{i}"], cbuf[:, :, 0:3], cbp)
            def odd_samples():
                k.barrier()
                W = slice(0, NS)
                k.dma(lambda e: e.dma_start(out=xt[:, :, W], in_=O["yT"][:, :, SEQ:SEQ + NS]), [XB[NT]], [xt])
                act(sq[:, :, W], xt[:, :, W], AF.Square, [xt], [sq])
                for kt in range(8):
                    mm(PS[0][:, W], onesb[:], sq[:, kt, W], [onesb, sq], [PS[0]], start=(kt == 0), stop=(kt == 7))
                act(lnv[:, W], PS[0][:, W], AF.Ln, [PS[0], epsc], [lnv], scale=1.0 / 1024.0, bias=epsc[:])
                act(rstd[:, W], lnv[:, W], AF.Exp, [lnv], [rstd], scale=-0.5)
                for kt in range(8):
                    stt(hn[:, kt, W], xt[:, kt, W], nw[:, L, kt:kt + 1], rstd[:, W], ALU.mult, ALU.mult, [xt, nw, rstd], [hn])
                sb_ = [PS[1], PS[3], PS[5], PS[6]]
                for ct in range(24):
                    P_ = sb_[ct % 4]
                    for kt in range(8):
                        mm(P_[:, W], WIN[:, kt, ct * 128:(ct + 1) * 128], hn[:, kt, W], [WIN, hn], [P_], start=(kt == 0), stop=(kt == 7))
                    cp(acc[:, ct, W], P_[:, W], [P_], [acc], en="act")
                for j in range(8):
                    P_ = sb_[j % 4]
                    t_ = thb[j % 4]
                    for kt in range(8):
                        mm(P_[:, W], WIN[:, kt, 3072 + j * 128:3072 + (j + 1) * 128], hn[:, kt, W], [WIN, hn], [P_], start=(kt == 0), stop=(kt == 7))
                    act(t_[:, W], P_[:, W], AF.Tanh, [P_], [t_], scale=0.5)
                    stt(zs[:, j, W], t_[:, W], 1.0, P_[:, W], ALU.add, ALU.mult, [t_, P_], [zs])
                for kt in range(8):
                    mm(PS[2][0:NS, 0:16], hn[:, kt, W], WIN[:, kt, 4096:4112], [hn, WIN], [PS[2]], start=(kt == 0), stop=(kt == 7))
                cp(ba[0:NS, :], PS[2][0:NS, 0:16], [PS[2]], [ba])
                prefetch_win(L + 1)
                k.dma(lambda e: e.dma_start(out=cbuf[:, :, 0:48], in_=D[f"scv{i}"].rearrange("p c i s -> p c (i s)")), (), [cbuf])
                wb = lambda i_: cvw[:, :, i_:i_ + 1].to_broadcast([128, 24, NS])
                tt(acth[:, :, W], cbuf[:, :, 0:16], wb(0), ALU.mult, [cbuf, cvw], [acth])
                for i_ in (1, 2):
                    tt(acth[:, :, 16:32], cbuf[:, :, 16 * i_:16 * i_ + 16], wb(i_), ALU.mult, [cbuf, cvw], [acth])
                    tt(acth[:, :, W], acth[:, :, W], acth[:, :, 16:32], ALU.add, [acth], [acth])
                tt(acth[:, :, 16:32], acc[:, :, W], wb(3), ALU.mult, [acc, cvw], [acth])
                tt(acth[:, :, W], acth[:, :, W], acth[:, :, 16:32], ALU.add, [acth], [acth])
                k.dma(lambda e: e.dma_start(out=O[f"ocs{i}"].rearrange("p c i s -> p c (i s)")[:, :, 0:32], in_=cbuf[:, :, 16:48]), [cbuf], (), out=True)
                k.dma(lambda e: e.dma_start(out=O[f"ocs{i}"][:, :, 2, :], in_=acc[:, :, W]), [acc], (), out=True)
                act(acth[:, :, 32:48], acth[:, :, W], AF.Tanh, [acth], [acth], scale=0.5)
                ts(acth[:, :, 32:48], acth[:, :, 32:48], 0.5, ALU.mult, [acth], [acth], s2=0.5, op1=ALU.add)
                tt(acth[:, :, W], acth[:, :, W], acth[:, :, 32:48], ALU.mult, [acth], [acth])
                v8 = lambda ap: ap.rearrange("p (a b) -> p a b", a=8)
                qn = v8(qsg[:, :]); kn = v8(Osb[:, :])
                for (c0, dstv, dstT, scl) in ((0, qn, qsg, 128.0 ** -0.5), (8, kn, Osb, 1.0)):
                    act(v8(sqk[:, :]), acth[:, c0:c0 + 8, W], AF.Square, [acth], [sqk])
                    mm(PS[3][:, 0:128], onesb[:], sqk[:, :], [onesb, sqk], [PS[3]])
                    act(l1[:, :], PS[3][:, 0:128], AF.Ln, [PS[3], epsc], [l1], bias=epsc[:])
                    act(rn[:, :], l1[:, :], AF.Exp, [l1], [rn], scale=-0.5)
                    stt(dstv, acth[:, c0:c0 + 8, W], scl, v8(rn[:, :]), ALU.mult, ALU.mult, [acth, rn], [dstT])
                P16 = slice(0, NS)
                act(beta[P16, :], ba[P16, 0:8], AF.Tanh, [ba], [beta], scale=0.5)
                ts(beta[P16, :], beta[P16, :], 0.5, ALU.mult, [beta], [beta], s2=0.5, op1=ALU.add)
                tt(g[P16, :], ba[P16, 8:16], dtb[P16, :], ALU.add, [ba, dtb], [g])
                act(g[P16, :], g[P16, :], AF.Exp, [g], [g])
                act(g[P16, :], g[P16, :], AF.Ln, [g, ones1], [g], bias=ones1[P16, :])
                tt(g[P16, :], g[P16, :], nea[P16, :], ALU.mult, [g, nea], [g])
                act(gam[P16, :], g[P16, :], AF.Exp, [g], [gam])
                eye3 = C("ident")[P16, 0:NS].unsqueeze(1).to_broadcast([NS, 8, NS])
                for (src, dstb) in ((gam, gbc), (beta, gbcn)):
                    tt(v8(Dst[P16, :]), src[P16, :].unsqueeze(2).to_broadcast([NS, 8, NS]), eye3, ALU.mult, [src, CT], [Dst])
                    mm(PS[2][:, 128:256], C("ones")[P16, :], Dst[P16, :], [CT, Dst], [PS[2]])
                    cp(dstb[:, :], PS[2][:, 128:256], [PS[2]], [dstb])
                ts(DTi[:, :], gbc[:, :], -1.0, ALU.mult, [gbc], [DTi])
                Ss = [V(S.ap[:, j, :], f"Sd{j}") for j in range(8)]
                sbanks = [PS[0], PS[1], PS[2], PS[3], PS[5], PS[6]]
                NSTR = 6
                tt(Dst[:, :], DTi[:, :], gbcn[:, :], ALU.mult, [DTi, gbcn], [Dst])
                tt(v8(TTm[:, :]), acth[:, 16:24, W], v8(gbcn[:, :]), ALU.mult, [acth, gbcn], [TTm])
                vbs = [HS[q_ % 4]["M0" if q_ < 4 else "M1"] for q_ in range(NSTR)]
                t2s = [HS[q_ % 4]["N0" if q_ < 4 else "N1"] for q_ in range(NSTR)]
                vcs = [HS[q_ % 4]["ssq" if q_ < 4 else "rs1"] for q_ in range(NSTR)]

                def dn_stream(q_):
                    P_ = sbanks[q_]; vb_ = vbs[q_]; t2_ = t2s[q_]; vc = vcs[q_]
                    Sj = Ss[q_]
                    for idx in range(q_, NS * 8, NSTR):
                        s, h = idx // 8, idx % 8
                        c_ = h * 16 + s
                        ld(Sj[:, :], D[f"sdel{i}"][s, h], [Sj])
                        mm(P_[:, 0:1], Sj[:, :], kn[:, h, s:s + 1], [Sj, Osb], [P_])
                        yield
                        stt(vc[:, 0:1], P_[:, 0:1], Dst[:, c_:c_ + 1], TTm[:, c_:c_ + 1], ALU.mult, ALU.add, [P_, Dst, TTm], [vc])
                        yield
                        ts(vb_[:, :], C("ones"), vc[:, 0:1], ALU.mult, [CT, vc], [vb_], en="pool")
                        mm(P_[:, 128:256], vb_[:, :], C("ident"), [vb_, CT], [P_])
                        yield
                        ts(t2_[:, :], P_[:, 128:256], kn[:, h, s:s + 1], ALU.mult, [P_, Osb], [t2_])
                        yield
                        stt(Sj[:, :], Sj[:, :], gbc[:, c_:c_ + 1], t2_[:, :], ALU.mult, ALU.add, [Sj, gbc, t2_], [Sj])
                        yield
                        mm(PS[4][:, c_:c_ + 1], Sj[:, :], qn[:, h, s:s + 1], [Sj, qsg], [PS[4]])
                        st(O[f"ods{i}"][s, h], Sj[:, :], [Sj])
                        yield
                active = [dn_stream(q_) for q_ in range(NSTR)]
                while active:
                    for g_ in list(active):
                        try:
                            next(g_)
                        except StopIteration:
                            active.remove(g_)
                act(sqk[:, :], PS[4][:, 0:128], AF.Square, [PS[4]], [sqk])
                mm(PS[3][:, 0:128], onesb[:], sqk[:, :], [onesb, sqk], [PS[3]])
                act(l1[:, :], PS[3][:, 0:128], AF.Ln, [PS[3], epsc], [l1], scale=1.0 / 128.0, bias=epsc[:])
                act(rn[:, :], l1[:, :], AF.Exp, [l1], [rn], scale=-0.5)
                tt(l1[:, :], PS[4][:, 0:128], rn[:, :], ALU.mult, [PS[4], rn], [l1])
                for h in range(8):
                    stt(catT[:, h, W], l1[:, h * 16:(h + 1) * 16], dnw[:, 0:1], zs[:, h, W], ALU.mult, ALU.mult, [l1, dnw, zs], [catT])
                out_proj_s(8)
            if ns > 0:
                odd_samples()


    yb = Buf("yT_dram")
    for L in range(nlayers):
        if L % 2 == 0:
            even_layer(L, L // 2)
        else:
            odd_layer(L, L // 2)
    if final:
        k.barrier()
        NBF = 4
        fx = [V(ARF.t[:, j_ * 1024:(j_ + 1) * 1024].rearrange("p (a b) -> p a b", a=8), f"fx{j_}") for j_ in range(NBF)]
        fsq = [V(ARB.t[:, j_ * 1024:(j_ + 1) * 1024].rearrange("p (a b) -> p a b", a=8), f"fsq{j_}") for j_ in range(NBF)]
        fln = [V(ARF.t[:, 4096 + j_ * 256:4096 + j_ * 256 + 128], f"fln{j_}") for j_ in range(NBF)]
        frs = [V(ARF.t[:, 4096 + j_ * 256 + 128:4096 + (j_ + 1) * 256], f"frs{j_}") for j_ in range(NBF)]
        tiles_f = [(O["yT"][:, :, j * 128:(j + 1) * 128], 128, XB[j]) for j in range(nt)]
        if ns > 0:
            tiles_f.append((O["yT"][:, :, SEQ:SEQ + NS], NS, XB[NT]))
        nf = len(tiles_f)

        def f_load(t_):
            ap, w_, xb_ = tiles_f[t_]
            X_ = fx[t_ % NBF]
            dst_ = X_[:, :, 0:w_]
            k.dma(lambda e, dst_=dst_, ap=ap: e.dma_start(out=dst_, in_=ap), [xb_], [X_])

        def f_comp(t_):
            ap, w_, xb_ = tiles_f[t_]
            q_ = t_ % NBF
            X_, Q_, L_, R_, P_ = fx[q_], fsq[q_], fln[q_], frs[q_], PS[q_]
            act(Q_[:, :, 0:w_], X_[:, :, 0:w_], AF.Square, [X_], [Q_])
            for kt in range(8):
                mm(P_[:, 0:w_], onesb[:], Q_[:, kt, 0:w_], [onesb, Q_], [P_], start=(kt == 0), stop=(kt == 7))
            act(L_[:, 0:w_], P_[:, 0:w_], AF.Ln, [P_, epsc], [L_], scale=1.0 / 1024.0, bias=epsc[:])
            act(R_[:, 0:w_], L_[:, 0:w_], AF.Exp, [L_], [R_], scale=-0.5)
            for kt in range(8):
                stt(X_[:, kt, 0:w_], X_[:, kt, 0:w_], fnw[:, kt:kt + 1], R_[:, 0:w_], ALU.mult, ALU.mult, [X_, fnw, R_], [X_])

        def f_store(t_):
            ap, w_, xb_ = tiles_f[t_]
            X_ = fx[t_ % NBF]
            src_ = X_[:, :, 0:w_]
            k.dma(lambda e, src_=src_, ap=ap: e.dma_start(out=ap, in_=src_), [X_], [xb_], out=True)

        for it in range(nf + 2):
            if it < nf:
                f_load(it)
            if 0 <= it - 1 < nf:
                f_comp(it - 1)
            if 0 <= it - 2 < nf:
                f_store(it - 2)
    k.finish()
    return nc


def _tile_rows(w, nk):
    return np.ascontiguousarray(w.reshape(nk, 128, -1).transpose(1, 0, 2))


def prepare_inputs(inp):
    f = lambda a: np.ascontiguousarray(np.asarray(a, dtype=np.float32))
    x_prompt = f(inp["x_prompt"]); x_sample = f(inp["x_sample"])
    consts = make_consts()
    nw = np.ascontiguousarray(f(inp["norm_w"]).reshape(4, 8, 128).transpose(2, 0, 1))
    fnw = np.ascontiguousarray(f(inp["final_norm_w"]).reshape(8, 128).T)
    smask = np.zeros((128, 1), np.float32); smask[0, 0] = 1.0
    shared = {"consts": consts, "nw": nw, "fnw": fnw, "smask": smask}
    for i in range(2):
        shared[f"wie{i}"] = _tile_rows(f(inp["w_in_even"][i]), 8)
        shared[f"woe{i}"] = _tile_rows(f(inp["w_out_even"][i]), 12)
        shared[f"wgl{i}"] = _tile_rows(f(inp["s5_w_glu"][i]), 4)
        shared[f"wgu{i}"] = f(inp["gla_w_gate_up"][i])
        shared[f"bgt{i}"] = f(inp["gla_b_gate"][i]).reshape(1, 512)
        shared[f"gnw{i}"] = np.ascontiguousarray(f(inp["gla_norm_w"][i]).reshape(2, 128).T)
        shared[f"bgf{i}"] = np.ascontiguousarray(f(inp["gla_b_gate"][i]).reshape(4, 128).T)
        chm = lambda a: np.ascontiguousarray(a.reshape(16, 2, 64).transpose(1, 2, 0).reshape(128, 16))
        shared[f"lre{i}"] = chm(f(inp["s5_lambda_re"][i]))
        shared[f"lim{i}"] = chm(f(inp["s5_lambda_im"][i]))
        shared[f"ldt{i}"] = chm(np.repeat(f(inp["s5_log_dt"][i])[:, None], 64, axis=1))
        for nm, src in (("bre", "s5_b_re"), ("bim", "s5_b_im")):
            b = f(inp[src][i])
            out = np.zeros((128, 16, 128), np.float32)
            for g in range(32):
                gp, g2 = g // 2, g % 2
                out[(g % 8) * 16:(g % 8) * 16 + 16, gp, g2 * 64:(g2 + 1) * 64] = b[g].T
            shared[f"{nm}{i}"] = out
        for nm, src in (("cre", "s5_c_re"), ("cim", "s5_c_im")):
            c_ = f(inp[src][i])
            out = np.zeros((128, 16, 128), np.float32)
            for g in range(32):
                gp, g2 = g // 2, g % 2
                out[g2 * 64:(g2 + 1) * 64, gp, (g % 8) * 16:(g % 8) * 16 + 16] = c_[g].T
            shared[f"{nm}{i}"] = out
        shared[f"s5d{i}"] = np.ascontiguousarray(f(inp["s5_d"][i]).reshape(4, 128).T)
        shared[f"bgl{i}"] = np.ascontiguousarray(f(inp["s5_b_glu"][i]).reshape(4, 128).T)
        shared[f"wio{i}"] = _tile_rows(f(inp["w_in_odd"][i]), 8)
        shared[f"woo{i}"] = _tile_rows(f(inp["w_out_odd"][i]), 8)
        shared[f"cvw{i}"] = np.ascontiguousarray(f(inp["dn_conv_w"][i]).reshape(4, 24, 128).transpose(2, 1, 0))
        shared[f"alg{i}"] = np.ascontiguousarray(np.broadcast_to(f(inp["dn_a_log"][i])[None, :], (128, 8)))
        shared[f"dtb{i}"] = np.ascontiguousarray(np.broadcast_to(f(inp["dn_dt_bias"][i])[None, :], (128, 8)))
        shared[f"dnw{i}"] = f(inp["dn_norm_w"][i]).reshape(128, 1)
    in_maps = []
    for c in range(NCORES):
        m = dict(shared)
        xs = np.concatenate([x_prompt[c], x_sample[c * NS:(c + 1) * NS, 0]], axis=0)
        m["xT"] = np.ascontiguousarray(xs.reshape(SEQ + NS, 8, 128).transpose(2, 1, 0))
        sl = slice(c * NS, (c + 1) * NS)
        for i in range(2):
            m[f"sgla{i}"] = f(inp["state_gla"][i][sl])
            for nm, src in (("s5r", "state_s5_re"), ("s5i", "state_s5_im")):
                a = f(inp[src][i][sl])
                m[f"{nm}{i}"] = np.ascontiguousarray(a.reshape(NS, 16, 2, 64).transpose(2, 3, 1, 0).reshape(128, 16, NS))
            m[f"sdel{i}"] = f(inp["state_delta"][i][sl])
            a = f(inp["state_conv"][i][sl])
            m[f"scv{i}"] = np.ascontiguousarray(a.reshape(NS, 3, 24, 128).transpose(3, 2, 1, 0))
        in_maps.append(m)
    return in_maps


def kernel(**inp):
    in_maps = prepare_inputs(inp)
    nc = build_program()
    res = run_bass_kernel_spmd(nc, in_maps, core_ids=list(range(NCORES)))
    return assemble(res.results)


def assemble(R):
    y_prompt = np.zeros((8, SEQ, 1024), np.float32); y_sample = np.zeros((128, 1, 1024), np.float32)
    gla_p = np.zeros((2, 8, 4, 128, 256), np.float32); gla_s = np.zeros((2, 128, 4, 128, 256), np.float32)
    s5r_p = np.zeros((2, 8, 32, 64), np.float32); s5i_p = np.zeros_like(s5r_p)
    s5r_s = np.zeros((2, 128, 32, 64), np.float32); s5i_s = np.zeros_like(s5r_s)
    dn_p = np.zeros((2, 8, 8, 128, 128), np.float32); dn_s = np.zeros((2, 128, 8, 128, 128), np.float32)
    cv_p = np.zeros((2, 8, 3, 3072), np.float32); cv_s = np.zeros((2, 128, 3, 3072), np.float32)
    unch = lambda a: a.reshape(2, 64, 16).transpose(2, 0, 1).reshape(32, 64)
    for c in range(NCORES):
        r = R[c]
        yt = r["yT"].transpose(2, 1, 0).reshape(SEQ + NS, 1024)
        y_prompt[c] = yt[:SEQ]; y_sample[c * NS:(c + 1) * NS, 0] = yt[SEQ:]
        sl = slice(c * NS, (c + 1) * NS)
        for i in range(2):
            gla_p[i, c] = r[f"ogp{i}"]; gla_s[i, sl] = r[f"ogs{i}"]
            s5r_p[i, c] = unch(r[f"o5rp{i}"]); s5i_p[i, c] = unch(r[f"o5ip{i}"])
            for s in range(NS):
                s5r_s[i, c * NS + s] = unch(r[f"o5rs{i}"][:, :, s]); s5i_s[i, c * NS + s] = unch(r[f"o5is{i}"][:, :, s])
            dn_p[i, c] = r[f"odp{i}"]; dn_s[i, sl] = r[f"ods{i}"]
            cv_p[i, c] = r[f"ocp{i}"].transpose(2, 1, 0).reshape(3, 3072)
            cv_s[i, sl] = r[f"ocs{i}"].transpose(3, 2, 1, 0).reshape(NS, 3, 3072)
    return (y_prompt, y_sample, gla_p, gla_s, s5r_p, s5i_p, s5r_s, s5i_s, dn_p, dn_s, cv_p, cv_s)
```
